# Optimizing a Trainium2 kernel written in Bass

```python
import math
import jax, jax.numpy as jnp
from jax import lax
import numpy as np


D_MODEL = 1024
BATCH = 8
SEQ = 2048
DEPTH = 2

N_A_LAYERS = DEPTH // 2
N_B_LAYERS = DEPTH - N_A_LAYERS
RET_HEADS = 4
RET_QK_DIM = D_MODEL // RET_HEADS
RET_V_DIM = 2 * RET_QK_DIM
RET_CHUNK = 128
DIFF_HEADS = 4
DIFF_HEAD_DIM = D_MODEL // (2 * DIFF_HEADS)
DIFF_V_DIM = 2 * DIFF_HEAD_DIM
Q_BLOCK = 128
D_FF = 2816
CONV_WIDTH = 3
ROPE_THETA = 10000.0
NORM_EPS = 1e-6

kernel_name = 'yoco_retention_diffattn_convffn_adaln'


def rms_norm(x, gain=None, eps=NORM_EPS):
    xf = x.astype(jnp.float32)
    y = xf * lax.rsqrt(jnp.mean(xf * xf, axis=-1, keepdims=True) + eps)
    if gain is not None:
        y = y * gain.astype(jnp.float32)
    return y.astype(x.dtype)


def rope(x, freqs):
    s, d = x.shape[1], x.shape[-1]
    ang = jnp.arange(s, dtype=jnp.float32)[:, None] * freqs[None, :]
    shape = (s,) + (1,) * (x.ndim - 3) + (d // 2,)
    cos = jnp.cos(ang).reshape(shape)
    sin = jnp.sin(ang).reshape(shape)
    xf = x.astype(jnp.float32)
    x1, x2 = xf[..., : d // 2], xf[..., d // 2:]
    return jnp.concatenate([x1 * cos - x2 * sin, x2 * cos + x1 * sin], axis=-1).astype(x.dtype)


def adaln(c, w, b):
    return jax.nn.silu(c) @ w + b


def retention(q, k, v):
    b, s, h, dk = q.shape
    dv = v.shape[-1]
    c = RET_CHUNK
    n = s // c
    log_gamma = jnp.log(1.0 - 2.0 ** (-5.0 - jnp.arange(h, dtype=jnp.float32)))
    i = jnp.arange(c, dtype=jnp.float32)
    diff = i[:, None] - i[None, :]
    dmask = jnp.where(diff >= 0, jnp.exp(log_gamma[:, None, None] * jnp.maximum(diff, 0.0)), 0.0)
    q_dec = jnp.exp(log_gamma[:, None] * (i + 1.0))
    k_dec = jnp.exp(log_gamma[:, None] * (c - 1.0 - i))
    c_dec = jnp.exp(log_gamma * c)

    def to_chunks(t):
        return t.astype(jnp.float32).reshape(b, n, c, h, t.shape[-1]).transpose(1, 0, 3, 2, 4)

    def step(state, xs):
        qc, kc, vc = xs
        inner = jnp.einsum('bhqd,bhkd->bhqk', qc, kc) * dmask
        out = (jnp.einsum('bhqk,bhkv->bhqv', inner, vc)
               + jnp.einsum('bhqd,bhdv->bhqv', qc * q_dec[..., None], state))
        state = state * c_dec[:, None, None] + jnp.einsum('bhkd,bhkv->bhdv', kc * k_dec[..., None], vc)
        return state, out

    state0 = jnp.zeros((b, h, dk, dv), jnp.float32)
    _, out = lax.scan(step, state0, (to_chunks(q), to_chunks(k), to_chunks(v)))
    return out.transpose(1, 0, 3, 2, 4).reshape(b, s, h, dv).astype(v.dtype)


def diff_attention(q, k, v, lam):
    b, s, h, _, d = q.shape
    dv = v.shape[-1]
    nb = s // Q_BLOCK
    qb = q.reshape(b, nb, Q_BLOCK, h, 2, d).transpose(1, 0, 3, 4, 2, 5)
    kt = k.transpose(0, 2, 3, 1, 4)
    vt = v.transpose(0, 2, 1, 3)
    kpos = jnp.arange(s, dtype=jnp.int32)
    scale = d ** -0.5

    def block(args):
        qblk, start = args
        qpos = start + jnp.arange(Q_BLOCK, dtype=jnp.int32)
        sc = jnp.einsum('bhpqd,bhpkd->bhpqk', qblk, kt).astype(jnp.float32) * scale
        sc = jnp.where(kpos[None, :] <= qpos[:, None], sc, -jnp.inf)
        p = jax.nn.softmax(sc, axis=-1)
        a = p[:, :, 0] - lam * p[:, :, 1]
        return jnp.einsum('bhqk,bhkv->bhqv', a.astype(vt.dtype), vt)

    starts = jnp.arange(nb, dtype=jnp.int32) * Q_BLOCK
    out = lax.map(block, (qb, starts))
    return out.transpose(1, 0, 3, 2, 4).reshape(b, s, h, dv)


def conv_ffn(h, w_in, w_conv, b_conv, w_down):
    a, g = jnp.split(h @ w_in, 2, axis=-1)
    a = lax.conv_general_dilated(a, w_conv[:, None, :], window_strides=(1,),
                                 padding=[(CONV_WIDTH - 1, 0)],
                                 dimension_numbers=('NWC', 'WIO', 'NWC'),
                                 feature_group_count=D_FF) + b_conv
    return (jax.nn.gelu(a, approximate=False) * g) @ w_down


def setup_inputs(seed: int = 0) -> dict:
    key = jax.random.key(seed)
    ks = jax.random.split(key, 20)
    f32 = jnp.float32
    D = D_MODEL

    def nrm(k, shape, scale):
        return jax.random.normal(k, shape, f32) * scale

    ret_in = 2 * RET_HEADS * RET_QK_DIM + 2 * RET_HEADS * RET_V_DIM
    ret_out = RET_HEADS * RET_V_DIM
    diff_q = DIFF_HEADS * 2 * DIFF_HEAD_DIM
    kv = diff_q + DIFF_HEADS * DIFF_V_DIM
    diff_o = DIFF_HEADS * DIFF_V_DIM
    return {
        'x': nrm(ks[0], (BATCH, SEQ, D), 1.0),
        'c': nrm(ks[1], (BATCH, D), 1.0),
        'norm_gain': 1.0 + nrm(ks[2], (DEPTH, 2, D), 0.05),
        'w_ada': nrm(ks[3], (DEPTH, D, 6 * D), D ** -0.5),
        'b_ada': nrm(ks[4], (DEPTH, 6 * D), 0.02),
        'ret_w_in': nrm(ks[5], (N_A_LAYERS, D, ret_in), D ** -0.5),
        'ret_w_out': nrm(ks[6], (N_A_LAYERS, ret_out, D), ret_out ** -0.5),
        'ffn_w_in': nrm(ks[7], (DEPTH, D, 2 * D_FF), D ** -0.5),
        'ffn_w_conv': nrm(ks[8], (DEPTH, CONV_WIDTH, D_FF), CONV_WIDTH ** -0.5),
        'ffn_b_conv': nrm(ks[9], (DEPTH, D_FF), 0.02),
        'ffn_w_down': nrm(ks[10], (DEPTH, D_FF, D), D_FF ** -0.5),
        'kv_norm_gain': 1.0 + nrm(ks[11], (D,), 0.05),
        'kv_w_ada': nrm(ks[12], (D, 2 * D), D ** -0.5),
        'kv_b_ada': nrm(ks[13], (2 * D,), 0.02),
        'w_kv': nrm(ks[14], (D, kv), D ** -0.5),
        'diff_w_q': nrm(ks[15], (N_B_LAYERS, D, diff_q), D ** -0.5),
        'diff_lambda': nrm(ks[16], (N_B_LAYERS, 4, DIFF_HEAD_DIM), 0.1),
        'diff_subln_gain': 1.0 + nrm(ks[17], (N_B_LAYERS, DIFF_V_DIM), 0.05),
        'diff_w_out': nrm(ks[18], (N_B_LAYERS, diff_o, D), diff_o ** -0.5),
        'final_norm_gain': 1.0 + nrm(ks[19], (D,), 0.05),
    }


def reference(x, c, norm_gain, w_ada, b_ada, ret_w_in, ret_w_out, ffn_w_in, ffn_w_conv,
              ffn_b_conv, ffn_w_down, kv_norm_gain, kv_w_ada, kv_b_ada, w_kv, diff_w_q,
              diff_lambda, diff_subln_gain, diff_w_out, final_norm_gain):
    b, s, _ = x.shape
    ret_freqs = 1.0 / (ROPE_THETA ** jnp.linspace(0.0, 1.0, RET_QK_DIM // 2))
    diff_freqs = 1.0 / (ROPE_THETA ** (jnp.arange(0, DIFF_HEAD_DIM, 2, dtype=jnp.float32) / DIFF_HEAD_DIM))
    ret_split = [RET_HEADS * RET_QK_DIM, 2 * RET_HEADS * RET_QK_DIM,
                 2 * RET_HEADS * RET_QK_DIM + RET_HEADS * RET_V_DIM]
    k_dim = DIFF_HEADS * 2 * DIFF_HEAD_DIM
    k_sh = None
    v_sh = None
    for layer in range(DEPTH):
        mod = adaln(c, w_ada[layer], b_ada[layer])[:, None, :]
        sh1, sc1, g1, sh2, sc2, g2 = jnp.split(mod, 6, axis=-1)
        h = rms_norm(x, norm_gain[layer, 0]) * (1.0 + sc1) + sh1
        if layer < N_A_LAYERS:
            q, k, v, gt = jnp.split(h @ ret_w_in[layer], ret_split, axis=-1)
            q = rope(q.reshape(b, s, RET_HEADS, RET_QK_DIM), ret_freqs)
            k = rope(k.reshape(b, s, RET_HEADS, RET_QK_DIM), ret_freqs) * (RET_QK_DIM ** -0.5)
            o = retention(q, k, v.reshape(b, s, RET_HEADS, RET_V_DIM))
            o = rms_norm(o).reshape(b, s, RET_HEADS * RET_V_DIM)
            mix = (jax.nn.silu(gt) * o) @ ret_w_out[layer]
        else:
            j = layer - N_A_LAYERS
            q = rope((h @ diff_w_q[j]).reshape(b, s, DIFF_HEADS, 2, DIFF_HEAD_DIM), diff_freqs)
            lv = diff_lambda[j].astype(jnp.float32)
            lam_init = 0.8 - 0.6 * math.exp(-0.3 * layer)
            lam = jnp.exp(jnp.sum(lv[0] * lv[1])) - jnp.exp(jnp.sum(lv[2] * lv[3])) + lam_init
            o = diff_attention(q, k_sh, v_sh, lam)
            o = rms_norm(o, diff_subln_gain[j]) * (1.0 - lam_init)
            mix = o.reshape(b, s, DIFF_HEADS * DIFF_V_DIM) @ diff_w_out[j]
        x = x + g1 * mix
        h = rms_norm(x, norm_gain[layer, 1]) * (1.0 + sc2) + sh2
        x = x + g2 * conv_ffn(h, ffn_w_in[layer], ffn_w_conv[layer], ffn_b_conv[layer], ffn_w_down[layer])
        if layer == N_A_LAYERS - 1:
            kv_sh, kv_sc = jnp.split(adaln(c, kv_w_ada, kv_b_ada)[:, None, :], 2, axis=-1)
            hk = rms_norm(x, kv_norm_gain) * (1.0 + kv_sc) + kv_sh
            kv = hk @ w_kv
            k_sh = rope(kv[..., :k_dim].reshape(b, s, DIFF_HEADS, 2, DIFF_HEAD_DIM), diff_freqs)
            v_sh = kv[..., k_dim:].reshape(b, s, DIFF_HEADS, DIFF_V_DIM)
    return rms_norm(x, final_norm_gain)
```

```python
import math
from contextlib import ExitStack

import numpy as np
import concourse.bass as bass
import concourse.mybir as mybir
from concourse.bass_utils import run_bass_kernel_spmd

F32 = mybir.dt.float32
BF16 = mybir.dt.bfloat16
AF = mybir.ActivationFunctionType
ALU = mybir.AluOpType

D = 1024
S = 2048
NT = 16
NC8 = 8
DFF = 2816
NFC = 22
EPS = 1e-6
SQRT_D = 32.0
RET_GAMMA = [1.0 - 2.0 ** (-5.0 - h) for h in range(4)]
LAM_INIT1 = 0.8 - 0.6 * math.exp(-0.3 * 1)
NCORES = 8
DEBUG_BARRIER = False


class Op:
    __slots__ = ("eng", "fn", "reads", "writes", "dma", "ndma", "deps", "sig", "waits", "needs_sig",
                 "idx", "extra", "tag")


class Ring:
    def __init__(self, prog, name, n, eng):
        self.prog, self.name, self.n, self.eng = prog, name, n, eng
        self.tiles = []
        self.start = len(prog.ops)

    def get(self, fn, ndma=1):
        k = len(self.tiles)
        self.tiles.append({"fn": fn, "ndma": ndma, "rel": None})
        return k, k % self.n

    def res(self, k):
        return (self.name, k % self.n)

    def release(self, k):
        self.tiles[k]["rel"] = len(self.prog.ops)


class Prog:
    EPOCH = 4000
    NDSEM = 28

    def __init__(self):
        self.ops = []
        self.rings = []

    def add(self, eng, fn, reads=(), writes=(), dma=False, ndma=1, extra=(), tag=None):
        op = Op()
        reads, writes = list(reads), list(writes)
        for r in list(reads):
            if isinstance(r, tuple) and r[0] == "ps":
                reads.remove(r)
                if r not in writes:
                    writes.append(r)
        op.eng, op.fn, op.reads, op.writes = eng, fn, reads, writes
        op.dma, op.ndma, op.extra, op.tag = dma, ndma, list(extra), tag
        self.ops.append(op)
        return op

    def ring(self, name, n, eng):
        r = Ring(self, name, n, eng)
        self.rings.append(r)
        return r

    def barrier(self, engines=("pe", "act", "dve", "pool", "sp")):
        for e in engines:
            self.add(e, lambda eng: eng.nop(), writes=[("bar", e)], tag="bar")
        for e in engines:
            self.add(e, lambda eng: eng.nop(), reads=[("bar", f) for f in engines], writes=[("gate", e)],
                     tag="gate")

    def finalize(self, nc, es):
        ins = {}
        for r in self.rings:
            for k, t in enumerate(r.tiles):
                pos = r.start if k < r.n else r.tiles[k - r.n]["rel"]
                assert pos is not None, (r.name, k)
                op = Op()
                slot = k % r.n
                op.eng, op.fn = r.eng, (lambda eng, f=t["fn"], s=slot: f(eng, s))
                op.reads, op.writes, op.dma, op.ndma, op.extra, op.tag = [], [(r.name, slot)], True, t["ndma"], [], "ring"
                ins.setdefault(pos, []).append(op)
        ops = []
        for i, op in enumerate(self.ops):
            ops.extend(ins.get(i, []))
            ops.append(op)
        ops.extend(ins.get(len(self.ops), []))
        for i, op in enumerate(ops):
            op.idx = i
            op.needs_sig = False
            op.sig = None
        last_w, readers = {}, {}
        outstanding_dma = {}
        last_op = {}
        for op in ops:
            deps = {}
            for r in op.reads:
                w = last_w.get(r)
                if w is not None:
                    deps[w] = "raw"
            for w_ in op.writes:
                w = last_w.get(w_)
                if w is not None and w not in deps:
                    deps[w] = "waw"
                for rd in readers.get(w_, ()):
                    if rd not in deps:
                        deps[rd] = "war"
            if op.tag == "bar":
                for d in outstanding_dma.get(op.eng, ()):
                    deps[d] = "raw"
                outstanding_dma[op.eng] = []
                if op.eng in last_op:
                    deps[last_op[op.eng]] = "raw"
            if not op.dma:
                last_op[op.eng] = op.idx
            for r in op.reads:
                readers.setdefault(r, []).append(op.idx)
            for w_ in op.writes:
                last_w[w_] = op.idx
                readers[w_] = []
            if op.dma:
                outstanding_dma.setdefault(op.eng, []).append(op.idx)
            keep = []
            for d, kind in deps.items():
                if d == op.idx:
                    continue
                p = ops[d]
                if p.eng == op.eng and not p.dma and not op.dma and kind != "raw":
                    continue
                if p.eng == op.eng and not p.dma and op.dma and kind != "raw":
                    pass
                keep.append(d)
            op.deps = keep
            for d in keep:
                ops[d].needs_sig = True
        cnt = {}
        dma_j = {}
        qbase = {"sp": 0, "pool": 16, "act": 24}
        qn = {"sp": 16, "pool": 8, "act": 4}
        dsem_hist = [[] for _ in range(self.NDSEM)]
        for op in ops:
            if op.dma:
                jq = dma_j.get(op.eng, 0)
                dma_j[op.eng] = jq + 1
                s = qbase[op.eng] + jq % qn[op.eng]
                if dsem_hist[s]:
                    prev = dsem_hist[s][-1]
                    if prev not in op.deps:
                        op.deps.append(prev)
                tot = sum(ops[i].ndma for i in dsem_hist[s]) + op.ndma
                dsem_hist[s].append(op.idx)
                op.sig = ("d", s, 16 * tot)
                assert 16 * tot < 30000
            elif op.needs_sig:
                n = cnt.get(op.eng, 0) + 1
                cnt[op.eng] = n
                op.sig = ("c", op.eng, n)
        sems = {}
        for e, n in cnt.items():
            ne = (n - 1) // self.EPOCH + 1
            sems[e] = [es.enter_context(nc.semaphore("s_%s_%d" % (e, i))) for i in range(ne)]
        dsems = [es.enter_context(nc.semaphore("s_dma_%d" % i)) for i in range(self.NDSEM)]
        seen_c = {}
        seen_d = {}
        for op in ops:
            e = op.eng
            sc = seen_c.setdefault(e, {})
            sd = seen_d.setdefault(e, set())
            need_c = {}
            waits = []
            for d in op.deps:
                p = ops[d]
                if p.sig[0] == "d":
                    if d not in sd:
                        sd.add(d)
                        waits.append((dsems[p.sig[1]], p.sig[2]))
                else:
                    f, n = p.sig[1], p.sig[2]
                    if n > sc.get(f, 0) and n > need_c.get(f, 0):
                        need_c[f] = n
            for f, n in need_c.items():
                sc[f] = n
                ep, val = (n - 1) // self.EPOCH, (n - 1) % self.EPOCH + 1
                waits.append((sems[f][ep], val))
            op.waits = waits
        self.final_ops = ops
        self.nsig = cnt

        def emit(ename, eng):
            for op in ops:
                if op.eng != ename:
                    continue
                for s, v in op.waits:
                    eng.wait_ge(s, v)
                r = op.fn(eng)
                if op.sig is None:
                    continue
                if op.sig[0] == "d":
                    lst = r if isinstance(r, (list, tuple)) else [r]
                    assert len(lst) == op.ndma, (len(lst), op.ndma)
                    for ins_ in lst:
                        ins_.then_inc(dsems[op.sig[1]], 16)
                else:
                    n = op.sig[2]
                    r.then_inc(sems[op.sig[1]][(n - 1) // self.EPOCH], 1)

        with nc.Block() as block:
            @block.tensor
            def _(e):
                emit("pe", e)

            @block.scalar
            def _(e):
                emit("act", e)

            @block.vector
            def _(e):
                emit("dve", e)

            @block.gpsimd
            def _(e):
                emit("pool", e)

            @block.sync
            def _(e):
                emit("sp", e)


class Carve:
    def __init__(self, region, nwords):
        self.region, self.n, self.off = region, nwords, 0

    def reset(self):
        self.off = 0

    def alloc(self, shape, dtype):
        nel = int(np.prod(shape[1:]))
        nbytes = nel * (4 if dtype == F32 else 2)
        nw = (nbytes + 31) // 32 * 8
        assert self.off + nw <= self.n, ("phase region overflow", self.off, nw, self.n)
        v = self.region[:, self.off:self.off + nw]
        self.off += nw
        if dtype != F32:
            v = v.bitcast(dtype)
        v = v[:, 0:nel]
        if len(shape) == 3:
            v = v.rearrange("p (a b) -> p a b", a=shape[1])
        elif len(shape) == 4:
            v = v.rearrange("p (a b c) -> p a b c", a=shape[1], b=shape[2])
        return v


def build(stop_after=None, dbg=None):
    nc = bass.Bass("TRN2", target_bir_lowering=False)
    dt_in = lambda name, shape: nc.dram_tensor(name, list(shape), F32, kind="ExternalInput").ap()
    x_d = dt_in("x", [S, D])
    cT_d = dt_in("cT", [128, 8])
    w_ada_d = dt_in("w_ada", [2, D, 6 * D])
    b_ada_d = dt_in("b_ada_fm", [128, 2, 48])
    gain_d = dt_in("gain_fm", [128, 2, 2, 8])
    kvgain_d = dt_in("kvgain_fm", [128, 8])
    fgain_d = dt_in("fgain_fm", [128, 8])
    kv_w_ada_d = dt_in("kv_w_ada", [D, 2 * D])
    kv_b_ada_d = dt_in("kv_b_ada_fm", [128, 16])
    ret_w_in_d = dt_in("ret_w_in", [D, 6144])
    ret_w_out_d = dt_in("ret_w_out", [2048, D])
    ffn_w_in_d = dt_in("ffn_w_in", [2, D, 2 * DFF])
    ffn_w_down_d = dt_in("ffn_w_down", [2, DFF, D])
    convw_d = dt_in("convw_fm", [128, 2, 3, NFC])
    convb_d = dt_in("convb_fm", [128, 2, NFC])
    w_kv_d = dt_in("w_kv", [D, 2 * D])
    diff_w_q_d = dt_in("diff_w_q", [D, D])
    diff_w_out_d = dt_in("diff_w_out", [D, D])
    lam_d = dt_in("diff_lambda", [4, 128])
    subln_d = dt_in("diff_subln_gain", [1, 256])
    rope_ret_d = dt_in("rope_ret", [S, 256])
    rope_dif_d = dt_in("rope_dif", [S, 128])
    maskp_d = dt_in("maskp", [128, 4, 128])
    kdec_d = dt_in("kdec", [128, 4])
    epsq_d = dt_in("epsq", [128, 4])
    tri_d = dt_in("tri01", [128, 128])
    identf_d = dt_in("identf", [128, 128])
    out_d = nc.dram_tensor("out", [S, D], F32, kind="ExternalOutput").ap()
    dbg_d = None
    if dbg is not None:
        dbg_d = nc.dram_tensor("dbg", list(dbg), F32, kind="ExternalOutput").ap()

    es = ExitStack()
    with es:
        sb = lambda name, shape, dt: es.enter_context(nc.sbuf_tensor(name, list(shape), dt))
        xT = sb("xT", [128, 8, S], F32)
        hT = sb("hT", [128, 8, S], BF16)
        wring = sb("wring", [128, 4, 4096], BF16)
        identb = sb("identb", [128, 128], BF16)
        identf = sb("identf_sb", [128, 128], F32)
        onesb = sb("onesb", [128, 128], BF16)
        tri = sb("tri", [128, 128], F32)
        maskp = sb("maskp_sb", [128, 4, 128], F32)
        kdec = sb("kdec_sb", [128, 4], F32)
        epsq = sb("epsq_sb", [128, 4], F32)
        modv = sb("modv", [128, 2, 48], F32)
        modkv = sb("modkv", [128, 16], F32)
        Amod = sb("Amod", [128, 6, 8], F32)
        gains = sb("gains", [128, 6, 8], F32)
        convw = sb("convw", [128, 2, 3, NFC], F32)
        convb = sb("convb", [128, 2, NFC], F32)
        neglam = sb("neglam", [128, 1], F32)
        gsub = sb("gsub", [128, 256], F32)
        epsc = sb("epsc", [128, 2], F32)
        NREG = 18300
        region = sb("region", [128, NREG], F32)
        ps = [es.enter_context(nc.psum_tensor("ps%d" % i, [128, 512], F32)) for i in range(8)]
        PSR = [("ps", i) for i in range(8)]

        P = Prog()
        cv = Carve(region, NREG)
        W = P.ring("wring", 4, "pool")

        def wslot(slot):
            return wring[:, slot, :]

        def ld(dst, src, res):
            P.add("sp", lambda e: [e.dma_start(out=dst, in_=src)], writes=[res], dma=True)

        ld(identf[:], identf_d, "identf")
        ld(tri[:], tri_d, "tri")
        ld(maskp[:], maskp_d, "maskp")
        ld(kdec[:], kdec_d, "kdec")
        ld(epsq[:], epsq_d, "epsq")
        ld(convw[:], convw_d, "convw")
        ld(convb[:], convb_d, "convb")
        ld(gains[:, 0:4, :], gain_d.rearrange("p a b c -> p (a b) c"), "gains")
        ld(gains[:, 4, :], kvgain_d, "gains4")
        ld(gains[:, 5, :], fgain_d, "gains5")
        P.add("dve", lambda e: e.tensor_copy(out=identb[:], in_=identf[:]), reads=["identf"], writes=["identb"])
        P.add("dve", lambda e: e.memset(onesb[:], 1.0), writes=["onesb"])
        P.add("dve", lambda e: e.memset(epsc[:, 0:1], EPS), writes=["epsc"])
        P.add("dve", lambda e: e.memset(epsc[:, 1:2], 1.0), writes=["epsc1"])

        cv.reset()
        cTs = cv.alloc([128, 8], F32)
        scs = cv.alloc([128, 8], F32)
        b_ada = cv.alloc([128, 2, 48], F32)
        b_kv = cv.alloc([128, 16], F32)
        lamt = cv.alloc([128, 4, 128], F32)
        lamp = cv.alloc([128, 2, 128], F32)
        lams = cv.alloc([128, 2], F32)
        lame = cv.alloc([128, 2], F32)
        xin = cv.alloc([128, 2, D], F32)
        wst = cv.alloc([128, 2, 8 * 512], F32)

        ld(cTs, cT_d, "cTs")
        ld(b_ada, b_ada_d, "b_ada")
        ld(b_kv, kv_b_ada_d, "b_kv")
        ld(lamt, lam_d.partition_broadcast(128), "lamt")
        ld(gsub[:], subln_d.partition_broadcast(128), "gsub_raw")
        P.add("act", lambda e: e.activation(out=scs, in_=cTs, func=AF.Silu), reads=["cTs"], writes=["scs"])
        P.add("dve", lambda e: e.tensor_tensor(out=lamp, in0=lamt[:, 0:4:2, :], in1=lamt[:, 1:4:2, :], op=ALU.mult),
              reads=["lamt"], writes=["lamp"])
        P.add("dve", lambda e: e.tensor_reduce(out=lams, in_=lamp, axis=mybir.AxisListType.X, op=ALU.add),
              reads=["lamp"], writes=["lams"])
        P.add("act", lambda e: e.activation(out=lame, in_=lams, func=AF.Exp), reads=["lams"], writes=["lame"])
        P.add("dve", lambda e: e.tensor_tensor(out=neglam[:], in0=lame[:, 1:2], in1=lame[:, 0:1], op=ALU.subtract),
              reads=["lame"], writes=["neglam0"])
        P.add("dve", lambda e: e.tensor_scalar(out=neglam[:], in0=neglam[:], scalar1=-LAM_INIT1, scalar2=None,
                                               op0=ALU.add), reads=["neglam0"], writes=["neglam"])
        P.add("dve", lambda e: e.tensor_scalar(out=gsub[:], in0=gsub[:], scalar1=(1.0 - LAM_INIT1), scalar2=None,
                                               op0=ALU.mult), reads=["gsub_raw"], writes=["gsub"])

        AR = P.ring("wst", 2, "sp")
        rowsb = cv.alloc([1, 6144], F32)
        pcn = [0]

        def adaln(w_d, ncols, bank, out_ap, bias_ap, res_out):
            npieces = ncols // 512
            todo_ = []
            for pc in range(npieces):
                def piece(pc=pc):
                    k, slot = AR.get(lambda e, s, pc=pc: [e.dma_start(
                        out=wst[:, s, :].rearrange("p (a b) -> p a b", a=8),
                        in_=w_d[:, pc * 512:(pc + 1) * 512].rearrange("(kc p) f -> p kc f", p=128))])
                    rb = 5 + pcn[0] % 2
                    pcn[0] += 1

                    def mmf(e, slot=slot, rb=rb):
                        r = None
                        wv = wst[:, slot, :].rearrange("p (a b) -> p a b", a=8)
                        for kc in range(8):
                            r = e.matmul(ps[rb][0:1, 0:512], lhsT=scs[:, kc:kc + 1], rhs=wv[:, kc, :],
                                         start=(kc == 0), stop=(kc == 7))
                        return r
                    P.add("pe", mmf, reads=[AR.res(k), "scs"], writes=[PSR[rb]])
                    AR.release(k)
                    rkey = ("rowsb", id(bank), pc)
                    P.add("act", lambda e, rb=rb, pc=pc: e.activation(out=rowsb[0:1, pc * 512:(pc + 1) * 512],
                                                                      in_=ps[rb][0:1, 0:512], func=AF.Copy),
                          reads=[PSR[rb]], writes=[("rowsb", pc)])

                    def redis(e, pc=pc):
                        r = None
                        for j in range(4):
                            col = pc * 4 + j
                            r = e.matmul(bank[:, col:col + 1], lhsT=rowsb[0:1, col * 128:(col + 1) * 128],
                                         rhs=identf[0:1, 0:1], start=True, stop=True)
                        return r
                    P.add("pe", redis, reads=[("rowsb", pc), "identf"], writes=[("adaps", id(bank), pc)])
                todo_.append(piece)
            nj = ncols // 128

            def fin():
                P.add("dve", lambda e: e.tensor_tensor(out=out_ap, in0=bank[:, 0:nj], in1=bias_ap, op=ALU.add),
                      reads=[("adaps", id(bank), pc) for pc in range(npieces)], writes=[res_out])
            todo_.append(fin)
            return todo_

        todo = (adaln(w_ada_d[0], 6144, ps[2], modv[:, 0, :], b_ada[:, 0, :], "modv0") +
                adaln(w_ada_d[1], 6144, ps[3], modv[:, 1, :], b_ada[:, 1, :], "modv1") +
                adaln(kv_w_ada_d, 2048, ps[4], modkv[:], b_kv, "modkv"))

        XR = P.ring("xin", 2, "act")
        for t in range(NT):
            k, slot = XR.get(lambda e, s, t=t: [e.dma_start(out=xin[:, s, :], in_=x_d[t * 128:(t + 1) * 128, :])])
            for half in range(2):
                bank = ps[half]

                def tr(e, slot=slot, half=half, bank=bank):
                    r = None
                    for j in range(4):
                        c = half * 4 + j
                        r = e.transpose(out=bank[:, j * 128:(j + 1) * 128], in_=xin[:, slot, c * 128:(c + 1) * 128],
                                        identity=identf[:])
                    return r
                P.add("pe", tr, reads=[XR.res(k), "identf"], writes=[PSR[half]])
                dst = xT[:, half * 4:(half + 1) * 4, t * 128:(t + 1) * 128]
                src = bank[:].rearrange("p (a b) -> p a b", a=4)
                wr = [("xT", c, t) for c in range(half * 4, half * 4 + 4)]
                if half == 0:
                    P.add("act", lambda e, dst=dst, src=src: e.activation(out=dst, in_=src, func=AF.Copy),
                          reads=[PSR[half]], writes=wr)
                else:
                    P.add("dve", lambda e, dst=dst, src=src: e.tensor_copy(out=dst, in_=src),
                          reads=[PSR[half]], writes=wr)
            XR.release(k)
            for _ in range(2):
                if todo:
                    todo.pop(0)()

        while todo:
            todo.pop(0)()

        def mkA(idx, sc_ap, res_in):
            P.add("dve", lambda e: e.tensor_scalar(out=Amod[:, idx, :], in0=sc_ap, scalar1=1.0, scalar2=None,
                                                   op0=ALU.add), reads=[res_in], writes=[("A0", idx)])
            P.add("dve", lambda e: e.tensor_tensor(out=Amod[:, idx, :], in0=Amod[:, idx, :], in1=gains[:, idx, :],
                                                   op=ALU.mult),
                  reads=[("A0", idx), "gains", "gains4", "gains5"], writes=[("A", idx)])
        mkA(0, modv[:, 0, 8:16], "modv0")
        mkA(1, modv[:, 0, 32:40], "modv0")
        mkA(2, modv[:, 1, 8:16], "modv1")
        mkA(3, modv[:, 1, 32:40], "modv1")
        mkA(4, modkv[:, 8:16], "modkv")
        P.add("dve", lambda e: e.tensor_copy(out=Amod[:, 5, :], in_=gains[:, 5, :]), reads=["gains5"], writes=[("A", 5)])

        def modnorm(Aidx, B_ap, B_res, out_fn=None, nt=2):
            sq = cv.alloc([128, 2, 512], BF16)
            rstd = cv.alloc([128, 1, 512], F32)
            tmp = cv.alloc([128, nt, 512], F32) if out_fn is None else None
            for blk in range(4):
                tsl = slice(blk * 512, (blk + 1) * 512)
                tcells = lambda c: [("xT", c, t) for t in range(blk * 4, blk * 4 + 4)]
                for c in range(8):
                    P.add("act", lambda e, c=c, tsl=tsl: e.activation(out=sq[:, c % 2, :], in_=xT[:, c, tsl],
                                                                        func=AF.Square),
                          reads=tcells(c), writes=[("sq", c % 2)])
                    P.add("pe", lambda e, c=c: e.matmul(ps[7][:], lhsT=onesb[:], rhs=sq[:, c % 2, :],
                                                        start=(c == 0), stop=(c == 7)),
                          reads=[("sq", c % 2), "onesb"], writes=[PSR[7]])
                rs = rstd[:, 0, :]
                P.add("act", lambda e, rs=rs: e.activation(out=rs, in_=ps[7][:], func=AF.Ln, bias=epsc[:, 0:1],
                                                           scale=1.0 / D),
                      reads=[PSR[7], "epsc"], writes=[("rstd", 0)])
                P.add("act", lambda e, rs=rs: e.activation(out=rs, in_=rs, func=AF.Exp, scale=-0.5),
                      reads=[("rstd", 0)], writes=[("rstd", 0)])
                for c in range(8):
                    if out_fn is not None:
                        tm, tres = out_fn(blk, c, None, None)
                    else:
                        tm, tres = tmp[:, c % nt, :], ("nrm_tmp", c % nt)
                    P.add("dve", lambda e, c=c, tm=tm, rs=rs, tsl=tsl: e.scalar_tensor_tensor(
                        out=tm, in0=xT[:, c, tsl], scalar=Amod[:, Aidx, c:c + 1], in1=rs, op0=ALU.mult, op1=ALU.mult),
                        reads=tcells(c) + [("rstd", 0), ("A", Aidx)], writes=[tres])
                    if out_fn is not None:
                        out_fn(blk, c, tm, tres)
                    else:
                        P.add("dve", lambda e, c=c, tm=tm, tsl=tsl: e.tensor_scalar(
                            out=hT[:, c, tsl], in0=tm, scalar1=B_ap[:, c:c + 1], scalar2=None, op0=ALU.add),
                            reads=[tres, B_res], writes=[("hT", c, t) for t in range(blk * 4, blk * 4 + 4)])

        def dump_dbg(src_ap_list):
            off = 0
            for ap_, n in src_ap_list:
                P.add("sp", lambda e, ap_=ap_, off=off, n=n: [e.dma_start(out=dbg_d[:, off:off + n], in_=ap_)],
                      reads=[], writes=[("dbgout", off)], dma=True)
                off += n

        if stop_after == "x0":
            P.barrier()
            off = 0
            for b_ in [0, 3]:
                P.add("sp", lambda e, b_=b_, off=off: [e.dma_start(
                    out=dbg_d[:, off:off + 4096].rearrange("p (a b) -> p a b", a=8),
                    in_=xT[:, :, b_ * 512:(b_ + 1) * 512])], writes=[("dbgout", off)], dma=True)
                off += 4096
            P.barrier()
            P.finalize(nc, es)
            return nc
        P.barrier()
        cv.reset()
        modnorm(0, modv[:, 0, 0:8], "modv0")

        if stop_after == "h0":
            P.barrier()
            P.finalize(nc, es)
            return nc

        def dump_xT(blocks):
            P.barrier()
            off = 0
            for b_ in blocks:
                P.add("sp", lambda e, b_=b_, off=off: [e.dma_start(
                    out=dbg_d[:, off:off + 4096].rearrange("p (a b) -> p a b", a=8),
                    in_=xT[:, :, b_ * 512:(b_ + 1) * 512])], writes=[("dbgout", off)], dma=True)
                off += 4096
            P.barrier()

        cs = cv.alloc([128, 3, 256], F32)
        t1 = cv.alloc([128, 2, 512], F32)
        t2 = cv.alloc([128, 2, 512], F32)
        qkr = cv.alloc([128, 2, 512], BF16)
        kd = cv.alloc([128, 3, 256], BF16)
        v_sb = cv.alloc([128, 3, 512], BF16)
        sg = cv.alloc([128, 3, 512], F32)
        eg = cv.alloc([128, 2, 512], F32)
        qkT = cv.alloc([128, 2, 512], BF16)
        innerm = cv.alloc([128, 2, 128], BF16)
        ssq = cv.alloc([128, 2], F32)
        rst = cv.alloc([128, 2], F32)
        rst2 = cv.alloc([128, 2], F32)
        junk = cv.alloc([128, 512], BF16)
        og = cv.alloc([128, 2, 512], BF16)
        ogT = cv.alloc([128, 2, 2048], BF16)
        T32 = cv.alloc([128, 2, 512], F32)
        state_bf = cv.alloc([128, 2, 512], BF16)
        qkT_ps = ps[6][:, 0:256].bitcast(BF16).rearrange("p (a b) -> p a b", a=4)
        ogT_ps = ps[5][:, 0:256].bitcast(BF16).rearrange("p (a b) -> p a b", a=4)
        CS = P.ring("cs", 3, "sp")
        g1_0 = modv[:, 0, 16:24]

        def wview(slot, a):
            return wring[:, slot, :].rearrange("p (a b) -> p a b", a=a)

        def wdma_cols(w_d, col_ranges, width):
            def fn(e, s):
                r = []
                wv = wview(s, 8)
                o = 0
                for (c0, n_) in col_ranges:
                    r.append(e.dma_start(out=wv[:, :, o:o + n_],
                                         in_=w_d[:, c0:c0 + n_].rearrange("(kc p) f -> p kc f", p=128)))
                    o += n_
                return r
            return fn, len(col_ranges)

        def wdma_rows(w_d, r0, nchunks):
            def fn(e, s):
                wv = wring[:, s, 0:nchunks * 1024].rearrange("p (a b) -> p a b", a=nchunks)
                return [e.dma_start(out=wv, in_=w_d[r0:r0 + nchunks * 128, :].rearrange("(c p) f -> p c f", p=128))]
            return fn, 1

        def proj_tok(bank, bres, wk, wslot_, t, col0=0, ncol=512):
            def fn(e):
                r = None
                wv = wview(wslot_, 8)
                for kc in range(8):
                    r = e.matmul(bank[:, 0:ncol], lhsT=hT[:, kc, t * 128:(t + 1) * 128], rhs=wv[:, kc, col0:col0 + ncol],
                                 start=(kc == 0), stop=(kc == 7))
                return r
            P.add("pe", fn, reads=[("hT", kc, t) for kc in range(8)] + [W.res(wk)], writes=[bres])

        tiles = {}
        for h in range(4):
            f, nd = wdma_cols(ret_w_in_d, [(h * 256, 256), (1024 + h * 256, 256)], 512)
            tiles[("qk", h)] = W.get(f, nd)
            f, nd = wdma_cols(ret_w_in_d, [(2048 + h * 512, 512)], 512)
            tiles[("v", h)] = W.get(f, nd)
            f, nd = wdma_cols(ret_w_in_d, [(4096 + h * 512, 512)], 512)
            tiles[("g", h)] = W.get(f, nd)
            f, nd = wdma_rows(ret_w_out_d, h * 512, 4)
            tiles[("o", h)] = W.get(f, nd)

        NIT = 64

        def s1_pe_qk(i):
            h, n = divmod(i, 16)
            kcs, cslot = CS.get(lambda e, s, n=n: [e.dma_start(out=cs[:, s, :], in_=rope_ret_d[n * 128:(n + 1) * 128, :])])
            wk, ws = tiles[("qk", h)]
            proj_tok(ps[0], PSR[0], wk, ws, n)
            if n == 15:
                W.release(wk)
            return kcs, cslot

        def s1_rope(i, kcs, cslot):
            h, n = divmod(i, 16)
            par = i % 2
            qk3 = ps[0][:].rearrange("p (a b) -> p a b", a=4)
            qk4 = ps[0][:].rearrange("p (j h f) -> p j h f", j=2, h=2)
            cosb = cs[:, cslot, 0:128].unsqueeze(1).broadcast_to([128, 4, 128])
            sinb = cs[:, cslot, 128:256].unsqueeze(1).broadcast_to([128, 2, 128])
            t1v3 = t1[:, par, :].rearrange("p (a b) -> p a b", a=4)
            t1v4 = t1[:, par, :].rearrange("p (j h f) -> p j h f", j=2, h=2)
            t2a = t2[:, par, 0:256].rearrange("p (a b) -> p a b", a=2)
            t2b = t2[:, par, 256:512].rearrange("p (a b) -> p a b", a=2)
            qkr4 = qkr[:, par, :].rearrange("p (j h f) -> p j h f", j=2, h=2)
            csr = CS.res(kcs)
            P.add("dve", lambda e: e.tensor_tensor(out=t1v3, in0=qk3, in1=cosb, op=ALU.mult),
                  reads=[PSR[0], csr], writes=[("t1", par)])
            P.add("dve", lambda e: e.tensor_tensor(out=t2a, in0=qk4[:, :, 1, :], in1=sinb, op=ALU.mult),
                  reads=[PSR[0], csr], writes=[("t2a", par)])
            P.add("dve", lambda e: e.tensor_tensor(out=t2b, in0=qk4[:, :, 0, :], in1=sinb, op=ALU.mult),
                  reads=[PSR[0], csr], writes=[("t2b", par)])
            CS.release(kcs)
            P.add("dve", lambda e: e.tensor_tensor(out=qkr4[:, :, 0, :], in0=t1v4[:, :, 0, :], in1=t2a, op=ALU.subtract),
                  reads=[("t1", par), ("t2a", par)], writes=[("qkr0", par)])
            P.add("dve", lambda e: e.tensor_tensor(out=qkr4[:, :, 1, :], in0=t1v4[:, :, 1, :], in1=t2b, op=ALU.add),
                  reads=[("t1", par), ("t2b", par)], writes=[("qkr1", par)])

        def s1_kd(i):
            h, n = divmod(i, 16)
            par, p3 = i % 2, i % 3
            P.add("act", lambda e: e.activation(out=kd[:, p3, :], in_=qkr[:, par, 256:512], func=AF.Copy,
                                                scale=kdec[:, h:h + 1]),
                  reads=[("qkr0", par), ("qkr1", par), "kdec"], writes=[("kd", p3)])

        def s1_pe_v(i):
            h, n = divmod(i, 16)
            wk, ws = tiles[("v", h)]
            proj_tok(ps[1], PSR[1], wk, ws, n)
            if n == 15:
                W.release(wk)

        def s1_act_v(i):
            p3 = i % 3
            P.add("act", lambda e: e.activation(out=v_sb[:, p3, :], in_=ps[1][:], func=AF.Copy),
                  reads=[PSR[1]], writes=[("v_sb", p3)])

        def s1_pe_g(i):
            h, n = divmod(i, 16)
            wk, ws = tiles[("g", h)]
            proj_tok(ps[2], PSR[2], wk, ws, n)
            if n == 15:
                W.release(wk)

        def s1_act_g(i):
            par = i % 2
            P.add("act", lambda e: e.activation(out=eg[:, par, :], in_=ps[2][:], func=AF.Exp, scale=-1.0),
                  reads=[PSR[2]], writes=[("eg", par)])
            P.add("act", lambda e: e.activation(out=eg[:, par, :], in_=eg[:, par, :], func=AF.Ln, bias=epsc[:, 1:2]),
                  reads=[("eg", par), "epsc1"], writes=[("eg", par)])
            P.add("act", lambda e: e.activation(out=eg[:, par, :], in_=eg[:, par, :], func=AF.Exp, scale=-1.0),
                  reads=[("eg", par)], writes=[("eg", par)])

        def s1_dve_g(i):
            par, p3 = i % 2, i % 3
            P.add("dve", lambda e: e.tensor_tensor(out=sg[:, p3, :], in0=ps[2][:], in1=eg[:, par, :], op=ALU.mult),
                  reads=[PSR[2], ("eg", par)], writes=[("sg", p3)])

        def a_pe_tr(i):
            par = i % 2
            qkr4 = qkr[:, par, :].rearrange("p (j h f) -> p j h f", j=2, h=2)

            def trq(e):
                r = None
                for idx, (j, hf) in enumerate([(0, 0), (0, 1), (1, 0), (1, 1)]):
                    r = e.transpose(out=qkT_ps[:, idx, :], in_=qkr4[:, j, hf, :], identity=identb[:])
                return r
            P.add("pe", trq, reads=[("qkr0", par), ("qkr1", par), "identb"], writes=[PSR[6]])

        def a_act_cp(i):
            par = i % 2
            P.add("act", lambda e: e.activation(out=qkT[:, par, :].rearrange("p (a b) -> p a b", a=4),
                                                in_=qkT_ps, func=AF.Copy),
                  reads=[PSR[6]], writes=[("qkT", par)])

        def a_pe_inner(i):
            par = i % 2

            def inner(e):
                r = None
                for hf in range(2):
                    r = e.matmul(ps[6][:, 256:384], lhsT=qkT[:, par, (2 + hf) * 128:(3 + hf) * 128],
                                 rhs=qkT[:, par, hf * 128:(hf + 1) * 128], start=(hf == 0), stop=(hf == 1))
                return r
            P.add("pe", inner, reads=[("qkT", par)], writes=[PSR[6]])

        def a_dve_mask(i):
            h, n = divmod(i, 16)
            par = i % 2
            P.add("dve", lambda e: e.tensor_tensor(out=innerm[:, par, :], in0=ps[6][:, 256:384], in1=maskp[:, h, :],
                                                   op=ALU.mult),
                  reads=[PSR[6], "maskp"], writes=[("innerm", par)])

        def b_pe_p(i):
            h, n = divmod(i, 16)
            par, p3 = i % 2, i % 3

            def pmm(e):
                r = e.matmul(ps[3][:], lhsT=innerm[:, par, :], rhs=v_sb[:, p3, :], start=True, stop=(n == 0))
                if n > 0:
                    for hf in range(2):
                        r = e.matmul(ps[3][:], lhsT=qkT[:, par, hf * 128:(hf + 1) * 128], rhs=state_bf[:, hf, :],
                                     start=False, stop=(hf == 1))
                return r
            P.add("pe", pmm, reads=[("innerm", par), ("v_sb", p3), ("qkT", par)] +
                  ([("state_bf", 0), ("state_bf", 1)] if n > 0 else []), writes=[PSR[3]])

        def b_pe_st(i, hf):
            h, n = divmod(i, 16)
            p3 = i % 3
            if n == 15:
                return
            bank, bres = (ps[4], PSR[4]) if hf == 0 else (ps[7], PSR[7])
            P.add("pe", lambda e: e.matmul(bank[:], lhsT=kd[:, p3, hf * 128:(hf + 1) * 128], rhs=v_sb[:, p3, :],
                                           start=True, stop=True),
                  reads=[("kd", p3), ("v_sb", p3)], writes=[bres])

        def b_dve_T(i, hf):
            h, n = divmod(i, 16)
            if n == 15:
                return
            cdec = RET_GAMMA[h] ** 128
            bank, bres = (ps[4], PSR[4]) if hf == 0 else (ps[7], PSR[7])
            if n == 0:
                P.add("dve", lambda e: e.tensor_copy(out=T32[:, hf, :], in_=bank[:]),
                      reads=[bres], writes=[("T32", hf)])
            else:
                P.add("dve", lambda e: e.scalar_tensor_tensor(out=T32[:, hf, :], in0=T32[:, hf, :], scalar=cdec,
                                                              in1=bank[:], op0=ALU.mult, op1=ALU.add),
                      reads=[bres, ("T32", hf)], writes=[("T32", hf)])

        def b_act_state(i, hf):
            h, n = divmod(i, 16)
            if n == 15:
                return
            P.add("act", lambda e: e.activation(out=state_bf[:, hf, :], in_=T32[:, hf, :], func=AF.Copy),
                  reads=[("T32", hf)], writes=[("state_bf", hf)])

        def b_dve_ms(i):
            par = i % 2
            P.add("dve", lambda e: e.memset(ssq[:, par:par + 1], 0.0), writes=[("ssq", par)])

        def b_act_norm(i):
            h, n = divmod(i, 16)
            par = i % 2
            P.add("act", lambda e: e.activation(out=junk, in_=ps[3][:], func=AF.Square, accum_out=ssq[:, par:par + 1]),
                  reads=[PSR[3], ("ssq", par)], writes=[("ssq", par), "junk"])
            P.add("act", lambda e: e.activation(out=rst[:, par:par + 1], in_=ssq[:, par:par + 1], func=AF.Ln,
                                                bias=epsq[:, h:h + 1], scale=1.0 / 512.0),
                  reads=[("ssq", par), "epsq"], writes=[("rst", par)])
            P.add("act", lambda e: e.activation(out=rst2[:, par:par + 1], in_=rst[:, par:par + 1], func=AF.Exp,
                                                scale=-0.5),
                  reads=[("rst", par)], writes=[("rst2", par)])

        def b_dve_og(i):
            par, p3 = i % 2, i % 3
            P.add("dve", lambda e: e.scalar_tensor_tensor(out=og[:, par, :], in0=ps[3][:], scalar=rst2[:, par:par + 1],
                                                          in1=sg[:, p3, :], op0=ALU.mult, op1=ALU.mult),
                  reads=[PSR[3], ("rst2", par), ("sg", p3)], writes=[("og", par)])

        def c_pe_tr(i):
            par = i % 2

            def trog(e):
                r = None
                for dvc in range(4):
                    r = e.transpose(out=ogT_ps[:, dvc, :], in_=og[:, par, dvc * 128:(dvc + 1) * 128], identity=identb[:])
                return r
            P.add("pe", trog, reads=[("og", par), "identb"], writes=[PSR[5]])

        def c_act_cp(i):
            h, n = divmod(i, 16)
            hb = (i // 4) % 2
            ogTv = ogT[:, hb, :].rearrange("p (a b) -> p a b", a=4)
            P.add("act", lambda e: e.activation(out=ogTv[:, :, (n % 4) * 128:(n % 4 + 1) * 128], in_=ogT_ps, func=AF.Copy),
                  reads=[PSR[5]], writes=[("ogT", hb, n % 4)])

        opq = []

        def c_outproj(i):
            h, n = divmod(i, 16)
            if n % 4 != 3:
                return
            for dc in range(8):
                opq.append((h, n // 4, (i // 4) % 2, dc))

        def outproj_one():
            if not opq:
                return
            h, blk, hb, dc = opq.pop(0)
            ogTv = ogT[:, hb, :].rearrange("p (a b) -> p a b", a=4)
            wk, ws = tiles[("o", h)]
            wo = wring[:, ws, :].rearrange("p (a b) -> p a b", a=4)
            bank, bres = ps[1], PSR[1]

            def opj(e):
                r = None
                for dvc in range(4):
                    r = e.matmul(bank[:], lhsT=wo[:, dvc, dc * 128:(dc + 1) * 128], rhs=ogTv[:, dvc, :],
                                 start=(dvc == 0), stop=(dvc == 3))
                return r
            P.add("pe", opj, reads=[W.res(wk)] + [("ogT", hb, q) for q in range(4)], writes=[bres])
            xs = xT[:, dc, blk * 512:(blk + 1) * 512]
            cells = [("xT", dc, t) for t in range(blk * 4, blk * 4 + 4)]
            P.add("dve", lambda e: e.scalar_tensor_tensor(
                out=xs, in0=bank[:], scalar=g1_0[:, dc:dc + 1], in1=xs, op0=ALU.mult, op1=ALU.add),
                reads=[bres, "modv0"] + cells, writes=cells)
            if blk == 3 and dc == 7:
                W.release(wk)

        def S1_all(i):
            kcs, cslot = s1_pe_qk(i)
            s1_pe_v(i); s1_pe_g(i)
            s1_act_v(i); s1_act_g(i)
            s1_rope(i, kcs, cslot)
            s1_dve_g(i); s1_kd(i)
        S1_all(0)
        S1_all(1)
        a_pe_tr(0); a_act_cp(0); a_pe_inner(0); a_dve_mask(0)
        for j in range(NIT):
            if DEBUG_BARRIER:
                P.barrier()
            has_a = j + 1 < NIT
            has_s = j + 2 < NIT
            has_c = j >= 1
            i2 = j + 2
            if has_s:
                kcs, cslot = s1_pe_qk(i2)
                s1_pe_v(i2)
                s1_pe_g(i2)
            b_dve_ms(j)
            if has_s:
                s1_rope(i2, kcs, cslot)
                s1_act_v(i2)
            if has_a:
                a_pe_tr(j + 1); a_act_cp(j + 1)
            b_pe_p(j)
            if has_s:
                s1_act_g(i2)
            b_act_norm(j)
            b_pe_st(j, 0); b_dve_T(j, 0)
            b_pe_st(j, 1); b_dve_T(j, 1)
            outproj_one()
            b_act_state(j, 0); b_act_state(j, 1)
            if has_c:
                c_pe_tr(j - 1); c_act_cp(j - 1)
            if has_a:
                a_pe_inner(j + 1); a_dve_mask(j + 1)
            outproj_one()
            if has_s:
                s1_dve_g(i2)
            b_dve_og(j)
            if has_c:
                c_outproj(j - 1)
            if has_s:
                s1_kd(i2)
        c_pe_tr(NIT - 1); c_act_cp(NIT - 1); c_outproj(NIT - 1)
        while opq:
            outproj_one()

        if stop_after == "mix0":
            dump_xT([0, 3])
            P.finalize(nc, es)
            return nc

        def ffn(l):
            P.barrier()
            cv.reset()
            modnorm(1 + 2 * l, modv[:, l, 24:32], "modv%d" % l)
            m_buf = cv.alloc([128, NFC, 1024], BF16)
            a_full = cv.alloc([128, 2, 1026], F32)
            u = cv.alloc([128, 2, 512], F32)
            u2 = cv.alloc([128, 2, 512], F32)
            halo = cv.alloc([128, NFC, 2], F32)
            g2 = modv[:, l, 40:48]
            P.add("dve", lambda e: e.memset(halo, 0.0), writes=["halo"])
            wt = {}
            for half in range(2):
                for un in range(11):
                    f, nd = wdma_cols(ffn_w_in_d[l], [(un * 256, 256), (DFF + un * 256, 256)], 512)
                    wt[("in", half, un)] = W.get(f, nd)
                for dc in range(8):
                    def fn(e, s_, dc=dc):
                        wv = wring[:, s_, 0:NFC * 128].rearrange("p (a b) -> p a b", a=NFC)
                        src = ffn_w_down_d[l][:, dc * 128:(dc + 1) * 128].rearrange("(fc p) d -> p fc d", p=128)
                        return [e.dma_start(out=wv[:, 0:11, :], in_=src[:, 0:11, :]),
                                e.dma_start(out=wv[:, 11:22, :], in_=src[:, 11:22, :])]
                    wt[("dn", half, dc)] = W.get(fn, 2)
            it = 0
            for half in range(2):
                for un in range(11):
                    wk, ws = wt[("in", half, un)]
                    wv = wview(ws, 8)
                    for fcl in range(2):
                        fc = un * 2 + fcl
                        sl = fc % 2
                        P.add("act", lambda e, sl=sl, fc=fc: e.activation(out=a_full[:, sl, 0:2], in_=halo[:, fc, :],
                                                                          func=AF.Copy),
                              reads=["halo", ("halo", fc)], writes=[("a_full_h", sl)])
                        for tb in range(2):
                            gb = half * 2 + tb
                            tsl = slice(gb * 512, (gb + 1) * 512)
                            pa, pg = ps[(it % 2) * 2], ps[(it % 2) * 2 + 1]
                            ra, rg = PSR[(it % 2) * 2], PSR[(it % 2) * 2 + 1]
                            ub = it % 2
                            it += 1

                            def mma(e, bank=pa, c0=fcl * 128, tsl=tsl, wv=wv):
                                r = None
                                for kc in range(8):
                                    r = e.matmul(bank[:], lhsT=wv[:, kc, c0:c0 + 128], rhs=hT[:, kc, tsl],
                                                 start=(kc == 0), stop=(kc == 7))
                                return r
                            hreads = [("hT", kc, t) for kc in range(8) for t in range(gb * 4, gb * 4 + 4)]
                            P.add("pe", mma, reads=hreads + [W.res(wk)], writes=[ra])
                            P.add("pe", lambda e, bank=pg, c0=256 + fcl * 128, tsl=tsl, wv=wv: mma(e, bank, c0, tsl, wv),
                                  reads=hreads + [W.res(wk)], writes=[rg])
                            off = tb * 512
                            P.add("act", lambda e, sl=sl, off=off, pa=pa: e.activation(
                                out=a_full[:, sl, 2 + off:2 + off + 512], in_=pa[:], func=AF.Copy),
                                reads=[ra], writes=[("a_full", sl, tb)])
                            P.add("act", lambda e, pa=pa, ub=ub, fc=fc: e.activation(
                                out=u[:, ub, :], in_=pa[:], func=AF.Identity, bias=convb[:, l, fc:fc + 1],
                                scale=convw[:, l, 2, fc:fc + 1]),
                                reads=[ra, "convw", "convb"], writes=[("u", ub)])
                            prev = [("a_full", sl, tb - 1)] if tb > 0 else [("a_full_h", sl)]
                            P.add("dve", lambda e, sl=sl, off=off, ub=ub, fc=fc: e.scalar_tensor_tensor(
                                out=u[:, ub, :], in0=a_full[:, sl, 1 + off:1 + off + 512],
                                scalar=convw[:, l, 1, fc:fc + 1], in1=u[:, ub, :], op0=ALU.mult, op1=ALU.add),
                                reads=[("a_full", sl, tb), ("u", ub), "convw"] + prev, writes=[("u", ub)])
                            P.add("dve", lambda e, sl=sl, off=off, ub=ub, fc=fc: e.scalar_tensor_tensor(
                                out=u[:, ub, :], in0=a_full[:, sl, off:off + 512],
                                scalar=convw[:, l, 0, fc:fc + 1], in1=u[:, ub, :], op0=ALU.mult, op1=ALU.add),
                                reads=[("a_full", sl, tb), ("u", ub), "convw"] + prev, writes=[("u", ub)])
                            P.add("act", lambda e, ub=ub: e.activation(out=u2[:, ub, :], in_=u[:, ub, :], func=AF.Gelu),
                                  reads=[("u", ub)], writes=[("u2", ub)])
                            P.add("dve", lambda e, ub=ub, fc=fc, off=off, pg=pg: e.tensor_tensor(
                                out=m_buf[:, fc, off:off + 512], in0=u2[:, ub, :], in1=pg[:], op=ALU.mult),
                                reads=[("u2", ub), rg], writes=[("m", fc, tb)])
                        if half == 0:
                            P.add("act", lambda e, sl=sl, fc=fc: e.activation(out=halo[:, fc, :],
                                                                              in_=a_full[:, sl, 1024:1026], func=AF.Copy),
                                  reads=[("a_full", sl, 1)], writes=[("halo", fc)])
                    W.release(wk)
                for dc in range(8):
                    wk, ws = wt[("dn", half, dc)]
                    wd = wring[:, ws, 0:NFC * 128].rearrange("p (a b) -> p a b", a=NFC)
                    for tb in range(2):
                        gb = half * 2 + tb
                        bank, br = ps[4 + (dc * 2 + tb) % 2], PSR[4 + (dc * 2 + tb) % 2]

                        def dmm(e, bank=bank, wd=wd, tb=tb):
                            r = None
                            for fc in range(NFC):
                                r = e.matmul(bank[:], lhsT=wd[:, fc, :], rhs=m_buf[:, fc, tb * 512:(tb + 1) * 512],
                                             start=(fc == 0), stop=(fc == NFC - 1))
                            return r
                        P.add("pe", dmm, reads=[W.res(wk)] + [("m", fc, tb) for fc in range(NFC)], writes=[br])
                        xs = xT[:, dc, gb * 512:(gb + 1) * 512]
                        cells = [("xT", dc, t) for t in range(gb * 4, gb * 4 + 4)]
                        P.add("dve", lambda e, xs=xs, bank=bank, dc=dc: e.scalar_tensor_tensor(
                            out=xs, in0=bank[:], scalar=g2[:, dc:dc + 1], in1=xs, op0=ALU.mult, op1=ALU.add),
                            reads=[br, "modv%d" % l] + cells, writes=cells)
                    W.release(wk)

        ffn(0)
        if stop_after == "ffn0":
            dump_xT([0, 3])
            P.finalize(nc, es)
            return nc

        P.barrier()
        cv.reset()
        KT = cv.alloc([128, 8, S], BF16)
        Vaug = cv.alloc([128, NT, 4 * 258], BF16)
        mark_kv = cv.off
        modnorm(4, modkv[:, 0:8], "modkv", nt=1)
        P.barrier()
        cv.off = mark_kv
        csd = cv.alloc([128, 2, 128], F32)
        d1 = cv.alloc([128, 1, 512], F32)
        d2 = cv.alloc([128, 1, 512], F32)
        rr = cv.alloc([128, 2, 512], BF16)
        Vv = Vaug.rearrange("p t (h c) -> p t h c", h=4)
        P.add("dve", lambda e: e.memset(Vaug, 1.0), writes=["Vaug_init"])
        CD = P.ring("csd", 2, "sp")
        trT_ps = ps[6][:, 0:256].bitcast(BF16).rearrange("p (a b) -> p a b", a=4)

        def rope_tile(bank, bres, par, cslot, csr):
            b3 = bank[:].rearrange("p (a b) -> p a b", a=8)
            b4 = bank[:].rearrange("p (u h f) -> p u h f", u=4, h=2)
            cosb = csd[:, cslot, 0:64].unsqueeze(1).broadcast_to([128, 8, 64])
            sinb = csd[:, cslot, 64:128].unsqueeze(1).broadcast_to([128, 4, 64])
            d1v3 = d1[:, 0, :].rearrange("p (a b) -> p a b", a=8)
            d1v4 = d1[:, 0, :].rearrange("p (u h f) -> p u h f", u=4, h=2)
            d2a = d2[:, 0, 0:256].rearrange("p (a b) -> p a b", a=4)
            d2b = d2[:, 0, 256:512].rearrange("p (a b) -> p a b", a=4)
            rr4 = rr[:, par, :].rearrange("p (u h f) -> p u h f", u=4, h=2)
            P.add("dve", lambda e: e.tensor_tensor(out=d1v3, in0=b3, in1=cosb, op=ALU.mult),
                  reads=[bres, csr], writes=[("d1", 0)])
            P.add("dve", lambda e: e.tensor_tensor(out=d2a, in0=b4[:, :, 1, :], in1=sinb, op=ALU.mult),
                  reads=[bres, csr], writes=[("d2a", 0)])
            P.add("dve", lambda e: e.tensor_tensor(out=d2b, in0=b4[:, :, 0, :], in1=sinb, op=ALU.mult),
                  reads=[bres, csr], writes=[("d2b", 0)])
            P.add("dve", lambda e: e.tensor_tensor(out=rr4[:, :, 0, :], in0=d1v4[:, :, 0, :], in1=d2a, op=ALU.subtract),
                  reads=[("d1", 0), ("d2a", 0)], writes=[("rr0", par)])
            P.add("dve", lambda e: e.tensor_tensor(out=rr4[:, :, 1, :], in0=d1v4[:, :, 1, :], in1=d2b, op=ALU.add),
                  reads=[("d1", 0), ("d2b", 0)], writes=[("rr1", par)])

        def proj_rope_T(wk, ws, t, it, dstT, dres, u0):
            par = it % 2
            bank, bres = ps[par], PSR[par]
            kcs, cslot = CD.get(lambda e, s_, t=t: [e.dma_start(out=csd[:, s_, :], in_=rope_dif_d[t * 128:(t + 1) * 128, :])])
            proj_tok(bank, bres, wk, ws, t)
            rope_tile(bank, bres, par, cslot, CD.res(kcs))
            CD.release(kcs)

            def trr(e, par=par):
                r = None
                for uu in range(4):
                    r = e.transpose(out=trT_ps[:, uu, :], in_=rr[:, par, uu * 128:(uu + 1) * 128], identity=identb[:])
                return r
            P.add("pe", trr, reads=[("rr0", par), ("rr1", par), "identb"], writes=[PSR[6]])
            P.add("act", lambda e: e.activation(out=dstT[:, u0:u0 + 4, t * 128:(t + 1) * 128], in_=trT_ps, func=AF.Copy),
                  reads=[PSR[6]], writes=[(dres, u, t) for u in range(u0, u0 + 4)])

        kvt = []
        for j in range(4):
            f, nd = wdma_cols(w_kv_d, [(j * 512, 512)], 512)
            kvt.append(W.get(f, nd))
        kitems = [(j, t) for j in range(2) for t in range(NT)]

        def k_proj(i):
            j, t = kitems[i]
            wk, ws = kvt[j]
            proj_tok(ps[i % 2], PSR[i % 2], wk, ws, t)
            if t == NT - 1:
                W.release(wk)

        def k_rest(i):
            j, t = kitems[i]
            par = i % 2
            kcs, cslot = CD.get(lambda e, s_, t=t: [e.dma_start(out=csd[:, s_, :], in_=rope_dif_d[t * 128:(t + 1) * 128, :])])
            rope_tile(ps[par], PSR[par], par, cslot, CD.res(kcs))
            CD.release(kcs)

            def trr(e):
                r = None
                for uu in range(4):
                    r = e.transpose(out=trT_ps[:, uu, :], in_=rr[:, par, uu * 128:(uu + 1) * 128], identity=identb[:])
                return r
            P.add("pe", trr, reads=[("rr0", par), ("rr1", par), "identb"], writes=[PSR[6]])
            P.add("act", lambda e: e.activation(out=KT[:, j * 4:j * 4 + 4, t * 128:(t + 1) * 128], in_=trT_ps, func=AF.Copy),
                  reads=[PSR[6]], writes=[("KT", u, t) for u in range(j * 4, j * 4 + 4)])
        k_proj(0)
        for i in range(len(kitems)):
            if i + 1 < len(kitems):
                k_proj(i + 1)
            k_rest(i)
        for j in range(2):
            wk, ws = kvt[2 + j]
            for t in range(NT):
                bank, bres = ps[2 + t % 2], PSR[2 + t % 2]
                proj_tok(bank, bres, wk, ws, t)
                P.add("act", lambda e, bank=bank, t=t, j=j: e.activation(
                    out=Vv[:, t, 2 * j:2 * j + 2, 0:256], in_=bank[:].rearrange("p (a b) -> p a b", a=2), func=AF.Copy),
                    reads=[bres, "Vaug_init"], writes=[("V", t, j)])
            W.release(wk)

        P.barrier()
        cv.off = mark_kv
        modnorm(2, modv[:, 1, 0:8], "modv1", nt=1)
        P.barrier()
        cv.off = mark_kv
        csd = cv.alloc([128, 2, 128], F32)
        d1 = cv.alloc([128, 1, 512], F32)
        d2 = cv.alloc([128, 1, 512], F32)
        rr = cv.alloc([128, 2, 512], BF16)
        CD = P.ring("csd2", 2, "sp")
        qt_ = []
        for j in range(2):
            f, nd = wdma_cols(diff_w_q_d, [(j * 512, 512)], 512)
            qt_.append(W.get(f, nd))
        def q_proj(t):
            for j in range(2):
                b = (t % 2) * 2 + j
                proj_tok(ps[b], PSR[b], qt_[j][0], qt_[j][1], t)

        def q_rest(t):
            kcs, cslot = CD.get(lambda e, s_, t=t: [e.dma_start(out=csd[:, s_, :], in_=rope_dif_d[t * 128:(t + 1) * 128, :])])
            for j in range(2):
                b = (t % 2) * 2 + j
                par = j
                rope_tile(ps[b], PSR[b], par, cslot, CD.res(kcs))

                def trr(e, par=par):
                    r = None
                    for uu in range(4):
                        r = e.transpose(out=trT_ps[:, uu, :], in_=rr[:, par, uu * 128:(uu + 1) * 128], identity=identb[:])
                    return r
                P.add("pe", trr, reads=[("rr0", par), ("rr1", par), "identb"], writes=[PSR[6]])
                P.add("act", lambda e, j=j, t=t: e.activation(out=hT[:, j * 4:j * 4 + 4, t * 128:(t + 1) * 128],
                                                              in_=trT_ps, func=AF.Copy),
                      reads=[PSR[6]], writes=[("hT", u, t) for u in range(j * 4, j * 4 + 4)])
            CD.release(kcs)
        q_proj(0)
        for t in range(NT):
            if t + 1 < NT:
                q_proj(t + 1)
            q_rest(t)
        for j in range(2):
            W.release(qt_[j][0])
        QT = hT

        P.barrier()
        cv.off = mark_kv
        eT = cv.alloc([128, 2, 512], BF16)
        rec2 = cv.alloc([128, 2, 2], F32)
        r1n2 = cv.alloc([128, 2], F32)
        facc0 = cv.alloc([128, 2, 258], F32)
        ssa2 = cv.alloc([128, 2], F32)
        rsa_2 = cv.alloc([128, 2], F32)
        rsa2_2 = cv.alloc([128, 2], F32)
        on2 = cv.alloc([128, 2, 256], BF16)
        oTb = cv.alloc([128, 2 * 512], BF16)
        g1_1 = modv[:, 1, 16:24]
        oT_ps = ps[6][:, 0:128].bitcast(BF16).rearrange("p (a b) -> p a b", a=2)
        wot = []
        for h in range(4):
            def fn(e, s_, h=h):
                wv = wring[:, s_, 0:2048].rearrange("p (a b) -> p a b", a=2)
                return [e.dma_start(out=wv, in_=diff_w_out_d[h * 256:(h + 1) * 256, :].rearrange("(c p) f -> p c f", p=128))]
            wot.append(W.get(fn, 1))
        SCALE = 128.0 ** -0.5
        items = [(h, qb, kt) for h in range(4) for qb in range(8) for kt in range(2 * qb + 2)]
        oTv = oTb.rearrange("p (a b) -> p a b", a=2)
        SB = [0, 1]
        deferred = []
        opq2 = []

        def subs_of(qb, kt):
            return [0, 1] if kt <= 2 * qb else [1]

        def att_score(k):
            h, qb, kt = items[k]
            sl = k % 2
            sbank, sres = ps[SB[sl]], PSR[SB[sl]]
            subs = subs_of(qb, kt)
            c0 = subs[0] * 128
            q0 = qb * 256 + c0
            nq = 128 * len(subs)

            def smm(e):
                r = None
                for half in range(2):
                    r = e.matmul(sbank[:, half * 256 + c0:half * 256 + c0 + nq],
                                 lhsT=KT[:, h * 2 + half, kt * 128:(kt + 1) * 128],
                                 rhs=QT[:, h * 2 + half, q0:q0 + nq], start=True, stop=True)
                return r
            qtiles = [2 * qb + s_ for s_ in subs]
            P.add("pe", smm, reads=[("KT", h * 2, kt), ("KT", h * 2 + 1, kt)] +
                  [("hT", h * 2 + half, qt) for half in range(2) for qt in qtiles], writes=[sres])
            s3 = sbank[:].rearrange("p (a b) -> p a b", a=2)
            e3 = eT[:, sl, :].rearrange("p (a b) -> p a b", a=2)
            P.add("act", lambda e: e.activation(out=e3[:, :, c0:c0 + nq], in_=s3[:, :, c0:c0 + nq], func=AF.Exp,
                                                scale=SCALE),
                  reads=[sres], writes=[("eT", sl)])

        fcount = [0]
        pend = [0]

        def finalize(h, qt, sub):
            a0, a1 = ps[2 + sub], ps[4 + sub]
            r0, r1 = PSR[2 + sub], PSR[4 + sub]
            fp = fcount[0] % 2
            fcount[0] += 1
            if fp == 0:
                facc, fres = facc0, []
            else:
                k3, s3_ = wot[3]
                facc = wring[:, s3_, 2048:4096].bitcast(F32)[:, 0:516].rearrange("p (a b) -> p a b", a=2)
                fres = [W.res(k3)]
            rec = rec2[:, fp, :]
            r1n = r1n2[:, fp:fp + 1]
            ssa = ssa2[:, fp:fp + 1]
            rsa = rsa_2[:, fp:fp + 1]
            rsa2 = rsa2_2[:, fp:fp + 1]
            on = on2[:, fp, :]
            F0, F1 = ("facc", fp, 0), ("facc", fp, 1)

            def st1():
                P.add("act", lambda e: e.activation(out=facc[:, 0, 0:257], in_=a0[:, 0:257], func=AF.Copy),
                      reads=[r0] + fres, writes=[F0])
                P.add("dve", lambda e: e.tensor_copy(out=facc[:, 1, 0:257], in_=a1[:, 0:257]),
                      reads=[r1] + fres, writes=[F1])
                P.add("dve", lambda e: e.reciprocal(out=rec[:, 1:2], in_=facc[:, 1, 256:257]),
                      reads=[F1], writes=[("rec1", fp)])
                P.add("dve", lambda e: e.tensor_tensor(out=r1n, in0=rec[:, 1:2], in1=neglam[:], op=ALU.mult),
                      reads=[("rec1", fp), "neglam"], writes=[("r1n", fp)])
                P.add("dve", lambda e: e.reciprocal(out=rec[:, 0:1], in_=facc[:, 0, 256:257]),
                      reads=[F0], writes=[("rec0", fp)])
                P.add("dve", lambda e: e.tensor_scalar(out=facc[:, 1, 0:256], in0=facc[:, 1, 0:256], scalar1=r1n,
                                                       scalar2=None, op0=ALU.mult),
                      reads=[F1, ("r1n", fp)] + fres, writes=[F1])
                P.add("dve", lambda e: e.scalar_tensor_tensor(out=facc[:, 0, 0:256], in0=facc[:, 0, 0:256],
                                                              scalar=rec[:, 0:1], in1=facc[:, 1, 0:256],
                                                              op0=ALU.mult, op1=ALU.add),
                      reads=[F0, F1, ("rec0", fp)] + fres, writes=[F0])
                P.add("dve", lambda e: e.memset(ssa, 0.0), writes=[("ssa", fp)])

            def st2():
                P.add("act", lambda e: e.activation(out=on, in_=facc[:, 0, 0:256], func=AF.Square, accum_out=ssa),
                      reads=[F0, ("ssa", fp)] + fres, writes=[("ssa", fp), ("on", fp)])
                P.add("act", lambda e: e.activation(out=rsa, in_=ssa, func=AF.Ln, bias=epsc[:, 0:1],
                                                    scale=1.0 / 256.0), reads=[("ssa", fp), "epsc"], writes=[("rsa", fp)])
                P.add("act", lambda e: e.activation(out=rsa2, in_=rsa, func=AF.Exp, scale=-0.5),
                      reads=[("rsa", fp)], writes=[("rsa2", fp)])
                P.add("dve", lambda e: e.scalar_tensor_tensor(out=on, in0=facc[:, 0, 0:256], scalar=rsa2,
                                                              in1=gsub[:], op0=ALU.mult, op1=ALU.mult),
                      reads=[F0, ("rsa2", fp), "gsub"] + fres, writes=[("on", fp)])

            def st3():
                def tro(e):
                    r = None
                    for j in range(2):
                        r = e.transpose(out=oT_ps[:, j, :], in_=on[:, j * 128:(j + 1) * 128], identity=identb[:])
                    return r
                if qt % 4 == 0:
                    flush_opq2()
                P.add("pe", tro, reads=[("on", fp), "identb"], writes=[PSR[6]])
                pend[0] += 1

            def st4():
                P.add("act", lambda e: e.activation(out=oTv[:, :, (qt % 4) * 128:(qt % 4 + 1) * 128], in_=oT_ps,
                                                    func=AF.Copy),
                      reads=[PSR[6]], writes=[("oTb", qt % 4)])
                pend[0] -= 1
                if qt % 4 == 3:
                    for dc in range(8):
                        opq2.append((h, qt // 4, dc))
            for d in [d for d in deferred if d[2] == fp]:
                if d in deferred:
                    deferred.remove(d)
                    d[1]()
            st1()
            deferred.append([1, st2, fp, "a"])
            deferred.append([2, st3, fp, "b"])
            deferred.append([3, st4, fp, "c"])

        def flush_opq2():
            while opq2:
                if pend[0] > 0 and opq2[0][2] % 2 == 1:
                    for d in [d for d in deferred if d[3] == "c"]:
                        if not any(x[2] == d[2] and x[3] == "b" for x in deferred):
                            deferred.remove(d)
                            d[1]()
                    assert pend[0] == 0
                outproj2()

        def run_deferred(flush=False):
            while True:
                ready = [d for d in deferred if d[0] <= 0 or flush]
                if not ready:
                    break
                d = ready[0]
                deferred.remove(d)
                d[1]()
            for d in deferred:
                d[0] -= 1

        def outproj2():
            if not opq2:
                return
            if pend[0] > 0 and opq2[0][2] % 2 == 1:
                return
            h, blk, dc = opq2.pop(0)
            wk, ws = wot[h]
            wo = wring[:, ws, 0:2048].rearrange("p (a b) -> p a b", a=2)
            bank, bres = (ps[7], PSR[7]) if dc % 2 == 0 else (ps[6], PSR[6])

            def opj(e):
                r = None
                for j in range(2):
                    r = e.matmul(bank[:], lhsT=wo[:, j, dc * 128:(dc + 1) * 128], rhs=oTv[:, j, :],
                                 start=(j == 0), stop=(j == 1))
                return r
            P.add("pe", opj, reads=[W.res(wk)] + [("oTb", q) for q in range(4)], writes=[bres])
            xs = xT[:, dc, blk * 512:(blk + 1) * 512]
            cells = [("xT", dc, t) for t in range(blk * 4, blk * 4 + 4)]
            P.add("dve", lambda e: e.scalar_tensor_tensor(
                out=xs, in0=bank[:], scalar=g1_1[:, dc:dc + 1], in1=xs, op0=ALU.mult, op1=ALU.add),
                reads=[bres, "modv1"] + cells, writes=cells)
            if blk == 3 and dc == 7:
                W.release(wk)

        def att_av(k):
            h, qb, kt = items[k]
            sl = k % 2
            e3 = eT[:, sl, :].rearrange("p (a b) -> p a b", a=2)
            subs = subs_of(qb, kt)
            if kt >= 2 * qb:
                dsub = kt - 2 * qb
                P.add("dve", lambda e: e.tensor_tensor(
                    out=e3[:, :, dsub * 128:(dsub + 1) * 128], in0=e3[:, :, dsub * 128:(dsub + 1) * 128],
                    in1=tri[:].unsqueeze(1).broadcast_to([128, 2, 128]), op=ALU.mult),
                    reads=[("eT", sl), "tri"], writes=[("eT", sl)])
            fins = []
            for sub in subs:
                last = (kt == 2 * qb + sub)

                def avm(e, sub=sub, last=last):
                    r = None
                    for half in range(2):
                        acc = ps[2 + half * 2 + sub]
                        r = e.matmul(acc[:, 0:257], lhsT=e3[:, half, sub * 128:(sub + 1) * 128],
                                     rhs=Vv[:, kt, h, 0:257], start=(kt == 0), stop=last)
                    return r
                P.add("pe", avm, reads=[("eT", sl), ("V", kt, h // 2), "Vaug_init"],
                      writes=[PSR[2 + sub], PSR[4 + sub]])
                if last:
                    fins.append((h, 2 * qb + sub, sub))
            outproj2()
            run_deferred()
            for f_ in fins:
                finalize(*f_)
                outproj2()
                outproj2()

        att_score(0)
        for k in range(len(items)):
            if k + 1 < len(items):
                att_score(k + 1)
            att_av(k)
        for _ in range(6):
            run_deferred()
        run_deferred(flush=True)
        flush_opq2()

        if stop_after == "mix1":
            dump_xT([0, 3])
            P.finalize(nc, es)
            return nc

        ffn(1)
        if stop_after == "ffn1":
            dump_xT([0, 3])
            P.finalize(nc, es)
            return nc

        P.barrier()
        cv.reset()
        yT = cv.alloc([128, 8, 512], F32)
        y_sb = cv.alloc([128, 2, 1024], F32)

        def fin_out(blk, c, tm, tres):
            if tm is None:
                return yT[:, c, :], ("yT", c)
            if c == 7:
                for tt in range(4):
                    t = blk * 4 + tt
                    ys = (blk * 4 + tt) % 2
                    for half in range(2):
                        bank, bres = ps[half], PSR[half]

                        def trf(e, bank=bank, half=half, tt=tt):
                            r = None
                            for j in range(4):
                                cc = half * 4 + j
                                r = e.transpose(out=bank[:, j * 128:(j + 1) * 128], in_=yT[:, cc, tt * 128:(tt + 1) * 128],
                                                identity=identf[:])
                            return r
                        P.add("pe", trf, reads=[("yT", cc) for cc in range(half * 4, half * 4 + 4)] + ["identf"],
                              writes=[bres])
                        if half == 0:
                            P.add("act", lambda e, bank=bank, ys=ys: e.activation(out=y_sb[:, ys, 0:512], in_=bank[:],
                                                                                 func=AF.Copy),
                                  reads=[bres], writes=[("y_sb", ys, 0)])
                        else:
                            P.add("dve", lambda e, bank=bank, ys=ys: e.tensor_copy(out=y_sb[:, ys, 512:1024], in_=bank[:]),
                                  reads=[bres], writes=[("y_sb", ys, 1)])
                    P.add("sp", lambda e, t=t, ys=ys: [e.dma_start(out=out_d[t * 128:(t + 1) * 128, :], in_=y_sb[:, ys, :])],
                          reads=[("y_sb", ys, 0), ("y_sb", ys, 1)], writes=[("out", t)], dma=True)
        modnorm(5, None, ("A", 5), out_fn=fin_out)
        P.barrier()
        P.finalize(nc, es)
        return nc

        raise NotImplementedError
    return nc


def fm(v):
    v = np.asarray(v, np.float32)
    n = v.shape[-1] // 128
    r = v.reshape(v.shape[:-1] + (n, 128))
    return np.ascontiguousarray(np.moveaxis(r, -1, 0))


def const_tables():
    pos = np.arange(S, dtype=np.float32)
    f_ret = (1.0 / (np.float32(10000.0) ** np.linspace(0.0, 1.0, 128, dtype=np.float32))).astype(np.float32)
    ang = (pos[:, None] * f_ret[None, :]).astype(np.float32)
    rope_ret = np.concatenate([np.cos(ang), np.sin(ang)], axis=1).astype(np.float32)
    f_dif = (1.0 / (np.float32(10000.0) ** (np.arange(0, 128, 2, dtype=np.float32) / np.float32(128)))).astype(np.float32)
    ang = (pos[:, None] * f_dif[None, :]).astype(np.float32)
    rope_dif = np.concatenate([np.cos(ang), np.sin(ang)], axis=1).astype(np.float32)
    i = np.arange(128, dtype=np.float64)
    scale = 256.0 ** -0.5
    maskp = np.zeros((128, 4, 128), np.float32)
    kdec = np.zeros((128, 4), np.float32)
    epsq = np.zeros((128, 4), np.float32)
    causal = (i[:, None] <= i[None, :])
    for h in range(4):
        lg = math.log(RET_GAMMA[h])
        maskp[:, h, :] = (scale * np.exp(-lg * (i[:, None] + 1.0)) * causal).astype(np.float32)
        kdec[:, h] = scale * np.exp(lg * (127.0 - i))
        epsq[:, h] = EPS * np.exp(-2.0 * lg * (i + 1.0))
    tri01 = causal.astype(np.float32)
    identf = np.eye(128, dtype=np.float32)
    return dict(rope_ret=rope_ret, rope_dif=rope_dif, maskp=maskp, kdec=kdec, epsq=epsq, tri01=tri01, identf=identf)


def make_in_maps(inputs, cores):
    g = {k: np.asarray(v, np.float32) for k, v in inputs.items()}
    shared = dict(
        w_ada=g["w_ada"], b_ada_fm=fm(g["b_ada"]), gain_fm=fm(g["norm_gain"]),
        kvgain_fm=fm(g["kv_norm_gain"]), fgain_fm=fm(g["final_norm_gain"]),
        kv_w_ada=g["kv_w_ada"], kv_b_ada_fm=fm(g["kv_b_ada"]),
        ret_w_in=g["ret_w_in"][0], ret_w_out=g["ret_w_out"][0],
        ffn_w_in=g["ffn_w_in"], ffn_w_down=g["ffn_w_down"],
        convw_fm=fm(g["ffn_w_conv"]), convb_fm=fm(g["ffn_b_conv"]),
        w_kv=g["w_kv"], diff_w_q=g["diff_w_q"][0], diff_w_out=g["diff_w_out"][0],
        diff_lambda=g["diff_lambda"][0], diff_subln_gain=g["diff_subln_gain"],
    )
    shared.update(const_tables())
    maps = []
    for b in cores:
        m = dict(shared)
        m["x"] = np.ascontiguousarray(g["x"][b])
        m["cT"] = fm(g["c"][b])
        maps.append(m)
    return maps


_NC_CACHE = {}


def kernel(**inputs):
    if "nc" not in _NC_CACHE:
        _NC_CACHE["nc"] = build()
    nc = _NC_CACHE["nc"]
    maps = make_in_maps(inputs, list(range(NCORES)))
    res = run_bass_kernel_spmd(nc, maps, core_ids=list(range(NCORES)))
    return np.stack([np.asarray(r["out"], np.float32) for r in res.results], axis=0)
```

```python
import math
from contextlib import ExitStack

import numpy as np
import concourse.bass as bass
import concourse.mybir as mybir
from concourse.bass_utils import run_bass_kernel_spmd

F32 = mybir.dt.float32
BF16 = mybir.dt.bfloat16
AF = mybir.ActivationFunctionType
ALU = mybir.AluOpType

D = 1024
S = 2048
NT = 16
NC8 = 8
DFF = 2816
NFC = 22
EPS = 1e-6
SQRT_D = 32.0
RET_GAMMA = [1.0 - 2.0 ** (-5.0 - h) for h in range(4)]
LAM_INIT1 = 0.8 - 0.6 * math.exp(-0.3 * 1)
NCORES = 8
DEBUG_BARRIER = False


class Op:
    __slots__ = ("eng", "fn", "reads", "writes", "dma", "ndma", "deps", "sig", "waits", "needs_sig",
                 "idx", "extra", "tag")


class Ring:
    def __init__(self, prog, name, n, eng):
        self.prog, self.name, self.n, self.eng = prog, name, n, eng
        self.tiles = []
        self.start = len(prog.ops)

    def get(self, fn, ndma=1):
        k = len(self.tiles)
        self.tiles.append({"fn": fn, "ndma": ndma, "rel": None})
        return k, k % self.n

    def res(self, k):
        return (self.name, k % self.n)

    def release(self, k):
        self.tiles[k]["rel"] = len(self.prog.ops)


class Prog:
    EPOCH = 4000
    NDSEM = 28

    def __init__(self):
        self.ops = []
        self.rings = []

    def add(self, eng, fn, reads=(), writes=(), dma=False, ndma=1, extra=(), tag=None):
        op = Op()
        reads, writes = list(reads), list(writes)
        for r in list(reads):
            if isinstance(r, tuple) and r[0] == "ps":
                reads.remove(r)
                if r not in writes:
                    writes.append(r)
        op.eng, op.fn, op.reads, op.writes = eng, fn, reads, writes
        op.dma, op.ndma, op.extra, op.tag = dma, ndma, list(extra), tag
        self.ops.append(op)
        return op

    def ring(self, name, n, eng):
        r = Ring(self, name, n, eng)
        self.rings.append(r)
        return r

    def barrier(self, engines=("pe", "act", "dve", "pool", "sp")):
        for e in engines:
            self.add(e, lambda eng: eng.nop(), writes=[("bar", e)], tag="bar")
        for e in engines:
            self.add(e, lambda eng: eng.nop(), reads=[("bar", f) for f in engines], writes=[("gate", e)],
                     tag="gate")

    def finalize(self, nc, es):
        ins = {}
        for r in self.rings:
            for k, t in enumerate(r.tiles):
                pos = r.start if k < r.n else r.tiles[k - r.n]["rel"]
                assert pos is not None, (r.name, k)
                op = Op()
                slot = k % r.n
                op.eng, op.fn = r.eng, (lambda eng, f=t["fn"], s=slot: f(eng, s))
                op.reads, op.writes, op.dma, op.ndma, op.extra, op.tag = [], [(r.name, slot)], True, t["ndma"], [], "ring"
                ins.setdefault(pos, []).append(op)
        ops = []
        for i, op in enumerate(self.ops):
            ops.extend(ins.get(i, []))
            ops.append(op)
        ops.extend(ins.get(len(self.ops), []))
        for i, op in enumerate(ops):
            op.idx = i
            op.needs_sig = False
            op.sig = None
        last_w, readers = {}, {}
        outstanding_dma = {}
        last_op = {}
        for op in ops:
            deps = {}
            for r in op.reads:
                w = last_w.get(r)
                if w is not None:
                    deps[w] = "raw"
            for w_ in op.writes:
                w = last_w.get(w_)
                if w is not None and w not in deps:
                    deps[w] = "waw"
                for rd in readers.get(w_, ()):
                    if rd not in deps:
                        deps[rd] = "war"
            if op.tag == "bar":
                for d in outstanding_dma.get(op.eng, ()):
                    deps[d] = "raw"
                outstanding_dma[op.eng] = []
                if op.eng in last_op:
                    deps[last_op[op.eng]] = "raw"
            if not op.dma:
                last_op[op.eng] = op.idx
            for r in op.reads:
                readers.setdefault(r, []).append(op.idx)
            for w_ in op.writes:
                last_w[w_] = op.idx
                readers[w_] = []
            if op.dma:
                outstanding_dma.setdefault(op.eng, []).append(op.idx)
            keep = []
            for d, kind in deps.items():
                if d == op.idx:
                    continue
                p = ops[d]
                if p.eng == op.eng and not p.dma and not op.dma and kind != "raw":
                    continue
                if p.eng == op.eng and not p.dma and op.dma and kind != "raw":
                    pass
                keep.append(d)
            op.deps = keep
            for d in keep:
                ops[d].needs_sig = True
        cnt = {}
        dma_j = {}
        qbase = {"sp": 0, "pool": 16, "act": 24}
        qn = {"sp": 16, "pool": 8, "act": 4}
        dsem_hist = [[] for _ in range(self.NDSEM)]
        for op in ops:
            if op.dma:
                jq = dma_j.get(op.eng, 0)
                dma_j[op.eng] = jq + 1
                s = qbase[op.eng] + jq % qn[op.eng]
                if dsem_hist[s]:
                    prev = dsem_hist[s][-1]
                    if prev not in op.deps:
                        op.deps.append(prev)
                tot = sum(ops[i].ndma for i in dsem_hist[s]) + op.ndma
                dsem_hist[s].append(op.idx)
                op.sig = ("d", s, 16 * tot)
                assert 16 * tot < 30000
            elif op.needs_sig:
                n = cnt.get(op.eng, 0) + 1
                cnt[op.eng] = n
                op.sig = ("c", op.eng, n)
        sems = {}
        for e, n in cnt.items():
            ne = (n - 1) // self.EPOCH + 1
            sems[e] = [es.enter_context(nc.semaphore("s_%s_%d" % (e, i))) for i in range(ne)]
        dsems = [es.enter_context(nc.semaphore("s_dma_%d" % i)) for i in range(self.NDSEM)]
        seen_c = {}
        seen_d = {}
        for op in ops:
            e = op.eng
            sc = seen_c.setdefault(e, {})
            sd = seen_d.setdefault(e, set())
            need_c = {}
            waits = []
            for d in op.deps:
                p = ops[d]
                if p.sig[0] == "d":
                    if d not in sd:
                        sd.add(d)
                        waits.append((dsems[p.sig[1]], p.sig[2]))
                else:
                    f, n = p.sig[1], p.sig[2]
                    if n > sc.get(f, 0) and n > need_c.get(f, 0):
                        need_c[f] = n
            for f, n in need_c.items():
                sc[f] = n
                ep, val = (n - 1) // self.EPOCH, (n - 1) % self.EPOCH + 1
                waits.append((sems[f][ep], val))
            op.waits = waits
        self.final_ops = ops
        self.nsig = cnt

        def emit(ename, eng):
            for op in ops:
                if op.eng != ename:
                    continue
                for s, v in op.waits:
                    eng.wait_ge(s, v)
                r = op.fn(eng)
                if op.sig is None:
                    continue
                if op.sig[0] == "d":
                    lst = r if isinstance(r, (list, tuple)) else [r]
                    assert len(lst) == op.ndma, (len(lst), op.ndma)
                    for ins_ in lst:
                        ins_.then_inc(dsems[op.sig[1]], 16)
                else:
                    n = op.sig[2]
                    r.then_inc(sems[op.sig[1]][(n - 1) // self.EPOCH], 1)

        with nc.Block() as block:
            @block.tensor
            def _(e):
                emit("pe", e)

            @block.scalar
            def _(e):
                emit("act", e)

            @block.vector
            def _(e):
                emit("dve", e)

            @block.gpsimd
            def _(e):
                emit("pool", e)

            @block.sync
            def _(e):
                emit("sp", e)


class Carve:
    def __init__(self, region, nwords):
        self.region, self.n, self.off = region, nwords, 0

    def reset(self):
        self.off = 0

    def alloc(self, shape, dtype):
        nel = int(np.prod(shape[1:]))
        nbytes = nel * (4 if dtype == F32 else 2)
        nw = (nbytes + 31) // 32 * 8
        assert self.off + nw <= self.n, ("phase region overflow", self.off, nw, self.n)
        v = self.region[:, self.off:self.off + nw]
        self.off += nw
        if dtype != F32:
            v = v.bitcast(dtype)
        v = v[:, 0:nel]
        if len(shape) == 3:
            v = v.rearrange("p (a b) -> p a b", a=shape[1])
        elif len(shape) == 4:
            v = v.rearrange("p (a b c) -> p a b c", a=shape[1], b=shape[2])
        return v


def build(stop_after=None, dbg=None):
    nc = bass.Bass("TRN2", target_bir_lowering=False)
    dt_in = lambda name, shape: nc.dram_tensor(name, list(shape), F32, kind="ExternalInput").ap()
    x_d = dt_in("x", [S, D])
    cT_d = dt_in("cT", [128, 8])
    w_ada_d = dt_in("w_ada", [2, D, 6 * D])
    b_ada_d = dt_in("b_ada_fm", [128, 2, 48])
    gain_d = dt_in("gain_fm", [128, 2, 2, 8])
    kvgain_d = dt_in("kvgain_fm", [128, 8])
    fgain_d = dt_in("fgain_fm", [128, 8])
    kv_w_ada_d = dt_in("kv_w_ada", [D, 2 * D])
    kv_b_ada_d = dt_in("kv_b_ada_fm", [128, 16])
    ret_w_in_d = dt_in("ret_w_in", [D, 6144])
    ret_w_out_d = dt_in("ret_w_out", [2048, D])
    ffn_w_in_d = dt_in("ffn_w_in", [2, D, 2 * DFF])
    ffn_w_down_d = dt_in("ffn_w_down", [2, DFF, D])
    convw_d = dt_in("convw_fm", [128, 2, 3, NFC])
    convb_d = dt_in("convb_fm", [128, 2, NFC])
    w_kv_d = dt_in("w_kv", [D, 2 * D])
    diff_w_q_d = dt_in("diff_w_q", [D, D])
    diff_w_out_d = dt_in("diff_w_out", [D, D])
    lam_d = dt_in("diff_lambda", [4, 128])
    subln_d = dt_in("diff_subln_gain", [1, 256])
    rope_ret_d = dt_in("rope_ret", [S, 256])
    rope_dif_d = dt_in("rope_dif", [S, 128])
    maskp_d = dt_in("maskp", [128, 4, 128])
    kdec_d = dt_in("kdec", [128, 4])
    epsq_d = dt_in("epsq", [128, 4])
    tri_d = dt_in("tri01", [128, 128])
    identf_d = dt_in("identf", [128, 128])
    out_d = nc.dram_tensor("out", [S, D], F32, kind="ExternalOutput").ap()
    dbg_d = None
    if dbg is not None:
        dbg_d = nc.dram_tensor("dbg", list(dbg), F32, kind="ExternalOutput").ap()

    es = ExitStack()
    with es:
        sb = lambda name, shape, dt: es.enter_context(nc.sbuf_tensor(name, list(shape), dt))
        xT = sb("xT", [128, 8, S], F32)
        hT = sb("hT", [128, 8, S], BF16)
        wring = sb("wring", [128, 4, 4096], BF16)
        identb = sb("identb", [128, 128], BF16)
        identf = sb("identf_sb", [128, 128], F32)
        onesb = sb("onesb", [128, 128], BF16)
        tri = sb("tri", [128, 128], F32)
        maskp = sb("maskp_sb", [128, 4, 128], F32)
        kdec = sb("kdec_sb", [128, 4], F32)
        epsq = sb("epsq_sb", [128, 4], F32)
        modv = sb("modv", [128, 2, 48], F32)
        modkv = sb("modkv", [128, 16], F32)
        Amod = sb("Amod", [128, 6, 8], F32)
        gains = sb("gains", [128, 6, 8], F32)
        convw = sb("convw", [128, 2, 3, NFC], F32)
        convb = sb("convb", [128, 2, NFC], F32)
        neglam = sb("neglam", [128, 1], F32)
        gsub = sb("gsub", [128, 256], F32)
        epsc = sb("epsc", [128, 2], F32)
        NREG = 18300
        region = sb("region", [128, NREG], F32)
        ps = [es.enter_context(nc.psum_tensor("ps%d" % i, [128, 512], F32)) for i in range(8)]
        PSR = [("ps", i) for i in range(8)]

        P = Prog()
        cv = Carve(region, NREG)
        W = P.ring("wring", 4, "pool")

        def wslot(slot):
            return wring[:, slot, :]

        def ld(dst, src, res):
            P.add("sp", lambda e: [e.dma_start(out=dst, in_=src)], writes=[res], dma=True)

        ld(identf[:], identf_d, "identf")
        ld(tri[:], tri_d, "tri")
        ld(maskp[:], maskp_d, "maskp")
        ld(kdec[:], kdec_d, "kdec")
        ld(epsq[:], epsq_d, "epsq")
        ld(convw[:], convw_d, "convw")
        ld(convb[:], convb_d, "convb")
        ld(gains[:, 0:4, :], gain_d.rearrange("p a b c -> p (a b) c"), "gains")
        ld(gains[:, 4, :], kvgain_d, "gains4")
        ld(gains[:, 5, :], fgain_d, "gains5")
        P.add("dve", lambda e: e.tensor_copy(out=identb[:], in_=identf[:]), reads=["identf"], writes=["identb"])
        P.add("dve", lambda e: e.memset(onesb[:], 1.0), writes=["onesb"])
        P.add("dve", lambda e: e.memset(epsc[:, 0:1], EPS), writes=["epsc"])
        P.add("dve", lambda e: e.memset(epsc[:, 1:2], 1.0), writes=["epsc1"])

        cv.reset()
        cTs = cv.alloc([128, 8], F32)
        scs = cv.alloc([128, 8], F32)
        b_ada = cv.alloc([128, 2, 48], F32)
        b_kv = cv.alloc([128, 16], F32)
        lamt = cv.alloc([128, 4, 128], F32)
        lamp = cv.alloc([128, 2, 128], F32)
        lams = cv.alloc([128, 2], F32)
        lame = cv.alloc([128, 2], F32)
        xin = cv.alloc([128, 2, D], F32)
        wst = cv.alloc([128, 2, 8 * 512], F32)

        ld(cTs, cT_d, "cTs")
        ld(b_ada, b_ada_d, "b_ada")
        ld(b_kv, kv_b_ada_d, "b_kv")
        ld(lamt, lam_d.partition_broadcast(128), "lamt")
        ld(gsub[:], subln_d.partition_broadcast(128), "gsub_raw")
        P.add("act", lambda e: e.activation(out=scs, in_=cTs, func=AF.Silu), reads=["cTs"], writes=["scs"])
        P.add("dve", lambda e: e.tensor_tensor(out=lamp, in0=lamt[:, 0:4:2, :], in1=lamt[:, 1:4:2, :], op=ALU.mult),
              reads=["lamt"], writes=["lamp"])
        P.add("dve", lambda e: e.tensor_reduce(out=lams, in_=lamp, axis=mybir.AxisListType.X, op=ALU.add),
              reads=["lamp"], writes=["lams"])
        P.add("act", lambda e: e.activation(out=lame, in_=lams, func=AF.Exp), reads=["lams"], writes=["lame"])
        P.add("dve", lambda e: e.tensor_tensor(out=neglam[:], in0=lame[:, 1:2], in1=lame[:, 0:1], op=ALU.subtract),
              reads=["lame"], writes=["neglam0"])
        P.add("dve", lambda e: e.tensor_scalar(out=neglam[:], in0=neglam[:], scalar1=-LAM_INIT1, scalar2=None,
                                               op0=ALU.add), reads=["neglam0"], writes=["neglam"])
        P.add("dve", lambda e: e.tensor_scalar(out=gsub[:], in0=gsub[:], scalar1=(1.0 - LAM_INIT1), scalar2=None,
                                               op0=ALU.mult), reads=["gsub_raw"], writes=["gsub"])

        AR = P.ring("wst", 2, "sp")
        rowsb = cv.alloc([1, 6144], F32)
        pcn = [0]

        def adaln(w_d, ncols, bank, out_ap, bias_ap, res_out):
            npieces = ncols // 512
            todo_ = []
            for pc in range(npieces):
                def piece(pc=pc):
                    k, slot = AR.get(lambda e, s, pc=pc: [e.dma_start(
                        out=wst[:, s, :].rearrange("p (a b) -> p a b", a=8),
                        in_=w_d[:, pc * 512:(pc + 1) * 512].rearrange("(kc p) f -> p kc f", p=128))])
                    rb = 5 + pcn[0] % 2
                    pcn[0] += 1

                    def mmf(e, slot=slot, rb=rb):
                        r = None
                        wv = wst[:, slot, :].rearrange("p (a b) -> p a b", a=8)
                        for kc in range(8):
                            r = e.matmul(ps[rb][0:1, 0:512], lhsT=scs[:, kc:kc + 1], rhs=wv[:, kc, :],
                                         start=(kc == 0), stop=(kc == 7))
                        return r
                    P.add("pe", mmf, reads=[AR.res(k), "scs"], writes=[PSR[rb]])
                    AR.release(k)
                    rkey = ("rowsb", id(bank), pc)
                    P.add("act", lambda e, rb=rb, pc=pc: e.activation(out=rowsb[0:1, pc * 512:(pc + 1) * 512],
                                                                      in_=ps[rb][0:1, 0:512], func=AF.Copy),
                          reads=[PSR[rb]], writes=[("rowsb", pc)])

                    def redis(e, pc=pc):
                        r = None
                        for j in range(4):
                            col = pc * 4 + j
                            r = e.matmul(bank[:, col:col + 1], lhsT=rowsb[0:1, col * 128:(col + 1) * 128],
                                         rhs=identf[0:1, 0:1], start=True, stop=True)
                        return r
                    P.add("pe", redis, reads=[("rowsb", pc), "identf"], writes=[("adaps", id(bank), pc)])
                todo_.append(piece)
            nj = ncols // 128

            def fin():
                P.add("dve", lambda e: e.tensor_tensor(out=out_ap, in0=bank[:, 0:nj], in1=bias_ap, op=ALU.add),
                      reads=[("adaps", id(bank), pc) for pc in range(npieces)], writes=[res_out])
            todo_.append(fin)
            return todo_

        todo = (adaln(w_ada_d[0], 6144, ps[2], modv[:, 0, :], b_ada[:, 0, :], "modv0") +
                adaln(w_ada_d[1], 6144, ps[3], modv[:, 1, :], b_ada[:, 1, :], "modv1") +
                adaln(kv_w_ada_d, 2048, ps[4], modkv[:], b_kv, "modkv"))

        XR = P.ring("xin", 2, "act")
        for t in range(NT):
            k, slot = XR.get(lambda e, s, t=t: [e.dma_start(out=xin[:, s, :], in_=x_d[t * 128:(t + 1) * 128, :])])
            for half in range(2):
                bank = ps[half]

                def tr(e, slot=slot, half=half, bank=bank):
                    r = None
                    for j in range(4):
                        c = half * 4 + j
                        r = e.transpose(out=bank[:, j * 128:(j + 1) * 128], in_=xin[:, slot, c * 128:(c + 1) * 128],
                                        identity=identf[:])
                    return r
                P.add("pe", tr, reads=[XR.res(k), "identf"], writes=[PSR[half]])
                dst = xT[:, half * 4:(half + 1) * 4, t * 128:(t + 1) * 128]
                src = bank[:].rearrange("p (a b) -> p a b", a=4)
                wr = [("xT", c, t) for c in range(half * 4, half * 4 + 4)]
                if half == 0:
                    P.add("act", lambda e, dst=dst, src=src: e.activation(out=dst, in_=src, func=AF.Copy),
                          reads=[PSR[half]], writes=wr)
                else:
                    P.add("dve", lambda e, dst=dst, src=src: e.tensor_copy(out=dst, in_=src),
                          reads=[PSR[half]], writes=wr)
            XR.release(k)
            for _ in range(2):
                if todo:
                    todo.pop(0)()

        while todo:
            todo.pop(0)()

        def mkA(idx, sc_ap, res_in):
            P.add("dve", lambda e: e.tensor_scalar(out=Amod[:, idx, :], in0=sc_ap, scalar1=1.0, scalar2=None,
                                                   op0=ALU.add), reads=[res_in], writes=[("A0", idx)])
            P.add("dve", lambda e: e.tensor_tensor(out=Amod[:, idx, :], in0=Amod[:, idx, :], in1=gains[:, idx, :],
                                                   op=ALU.mult),
                  reads=[("A0", idx), "gains", "gains4", "gains5"], writes=[("A", idx)])
        mkA(0, modv[:, 0, 8:16], "modv0")
        mkA(1, modv[:, 0, 32:40], "modv0")
        mkA(2, modv[:, 1, 8:16], "modv1")
        mkA(3, modv[:, 1, 32:40], "modv1")
        mkA(4, modkv[:, 8:16], "modkv")
        P.add("dve", lambda e: e.tensor_copy(out=Amod[:, 5, :], in_=gains[:, 5, :]), reads=["gains5"], writes=[("A", 5)])

        def modnorm(Aidx, B_ap, B_res, out_fn=None, nt=2):
            sq = cv.alloc([128, 2, 512], BF16)
            rstd = cv.alloc([128, 1, 512], F32)
            tmp = cv.alloc([128, nt, 512], F32) if out_fn is None else None
            for blk in range(4):
                tsl = slice(blk * 512, (blk + 1) * 512)
                tcells = lambda c: [("xT", c, t) for t in range(blk * 4, blk * 4 + 4)]
                for c in range(8):
                    P.add("act", lambda e, c=c, tsl=tsl: e.activation(out=sq[:, c % 2, :], in_=xT[:, c, tsl],
                                                                        func=AF.Square),
                          reads=tcells(c), writes=[("sq", c % 2)])
                    P.add("pe", lambda e, c=c: e.matmul(ps[7][:], lhsT=onesb[:], rhs=sq[:, c % 2, :],
                                                        start=(c == 0), stop=(c == 7)),
                          reads=[("sq", c % 2), "onesb"], writes=[PSR[7]])
                rs = rstd[:, 0, :]
                P.add("act", lambda e, rs=rs: e.activation(out=rs, in_=ps[7][:], func=AF.Ln, bias=epsc[:, 0:1],
                                                           scale=1.0 / D),
                      reads=[PSR[7], "epsc"], writes=[("rstd", 0)])
                P.add("act", lambda e, rs=rs: e.activation(out=rs, in_=rs, func=AF.Exp, scale=-0.5),
                      reads=[("rstd", 0)], writes=[("rstd", 0)])
                for c in range(8):
                    if out_fn is not None:
                        tm, tres = out_fn(blk, c, None, None)
                    else:
                        tm, tres = tmp[:, c % nt, :], ("nrm_tmp", c % nt)
                    P.add("dve", lambda e, c=c, tm=tm, rs=rs, tsl=tsl: e.scalar_tensor_tensor(
                        out=tm, in0=xT[:, c, tsl], scalar=Amod[:, Aidx, c:c + 1], in1=rs, op0=ALU.mult, op1=ALU.mult),
                        reads=tcells(c) + [("rstd", 0), ("A", Aidx)], writes=[tres])
                    if out_fn is not None:
                        out_fn(blk, c, tm, tres)
                    else:
                        P.add("dve", lambda e, c=c, tm=tm, tsl=tsl: e.tensor_scalar(
                            out=hT[:, c, tsl], in0=tm, scalar1=B_ap[:, c:c + 1], scalar2=None, op0=ALU.add),
                            reads=[tres, B_res], writes=[("hT", c, t) for t in range(blk * 4, blk * 4 + 4)])

        def dump_dbg(src_ap_list):
            off = 0
            for ap_, n in src_ap_list:
                P.add("sp", lambda e, ap_=ap_, off=off, n=n: [e.dma_start(out=dbg_d[:, off:off + n], in_=ap_)],
                      reads=[], writes=[("dbgout", off)], dma=True)
                off += n

        if stop_after == "x0":
            P.barrier()
            off = 0
            for b_ in [0, 3]:
                P.add("sp", lambda e, b_=b_, off=off: [e.dma_start(
                    out=dbg_d[:, off:off + 4096].rearrange("p (a b) -> p a b", a=8),
                    in_=xT[:, :, b_ * 512:(b_ + 1) * 512])], writes=[("dbgout", off)], dma=True)
                off += 4096
            P.barrier()
            P.finalize(nc, es)
            return nc
        P.barrier()
        cv.reset()
        modnorm(0, modv[:, 0, 0:8], "modv0")

        if stop_after == "h0":
            P.barrier()
            P.finalize(nc, es)
            return nc

        def dump_xT(blocks):
            P.barrier()
            off = 0
            for b_ in blocks:
                P.add("sp", lambda e, b_=b_, off=off: [e.dma_start(
                    out=dbg_d[:, off:off + 4096].rearrange("p (a b) -> p a b", a=8),
                    in_=xT[:, :, b_ * 512:(b_ + 1) * 512])], writes=[("dbgout", off)], dma=True)
                off += 4096
            P.barrier()

        cs = cv.alloc([128, 3, 256], F32)
        t1 = cv.alloc([128, 2, 512], F32)
        t2 = cv.alloc([128, 2, 512], F32)
        qkr = cv.alloc([128, 2, 512], BF16)
        kd = cv.alloc([128, 3, 256], BF16)
        v_sb = cv.alloc([128, 3, 512], BF16)
        sg = cv.alloc([128, 3, 512], F32)
        eg = cv.alloc([128, 2, 512], F32)
        qkT = cv.alloc([128, 2, 512], BF16)
        innerm = cv.alloc([128, 2, 128], BF16)
        ssq = cv.alloc([128, 2], F32)
        rst = cv.alloc([128, 2], F32)
        rst2 = cv.alloc([128, 2], F32)
        junk = cv.alloc([128, 512], BF16)
        og = cv.alloc([128, 2, 512], BF16)
        ogT = cv.alloc([128, 2, 2048], BF16)
        T32 = cv.alloc([128, 2, 512], F32)
        state_bf = cv.alloc([128, 2, 512], BF16)
        qkT_ps = ps[6][:, 0:256].bitcast(BF16).rearrange("p (a b) -> p a b", a=4)
        ogT_ps = ps[5][:, 0:256].bitcast(BF16).rearrange("p (a b) -> p a b", a=4)
        CS = P.ring("cs", 3, "sp")
        g1_0 = modv[:, 0, 16:24]

        def wview(slot, a):
            return wring[:, slot, :].rearrange("p (a b) -> p a b", a=a)

        def wdma_cols(w_d, col_ranges, width):
            def fn(e, s):
                r = []
                wv = wview(s, 8)
                o = 0
                for (c0, n_) in col_ranges:
                    r.append(e.dma_start(out=wv[:, :, o:o + n_],
                                         in_=w_d[:, c0:c0 + n_].rearrange("(kc p) f -> p kc f", p=128)))
                    o += n_
                return r
            return fn, len(col_ranges)

        def wdma_rows(w_d, r0, nchunks):
            def fn(e, s):
                wv = wring[:, s, 0:nchunks * 1024].rearrange("p (a b) -> p a b", a=nchunks)
                return [e.dma_start(out=wv, in_=w_d[r0:r0 + nchunks * 128, :].rearrange("(c p) f -> p c f", p=128))]
            return fn, 1

        def proj_tok(bank, bres, wk, wslot_, t, col0=0, ncol=512):
            def fn(e):
                r = None
                wv = wview(wslot_, 8)
                for kc in range(8):
                    r = e.matmul(bank[:, 0:ncol], lhsT=hT[:, kc, t * 128:(t + 1) * 128], rhs=wv[:, kc, col0:col0 + ncol],
                                 start=(kc == 0), stop=(kc == 7))
                return r
            P.add("pe", fn, reads=[("hT", kc, t) for kc in range(8)] + [W.res(wk)], writes=[bres])

        tiles = {}
        for h in range(4):
            f, nd = wdma_cols(ret_w_in_d, [(h * 256, 256), (1024 + h * 256, 256)], 512)
            tiles[("qk", h)] = W.get(f, nd)
            f, nd = wdma_cols(ret_w_in_d, [(2048 + h * 512, 512)], 512)
            tiles[("v", h)] = W.get(f, nd)
            f, nd = wdma_cols(ret_w_in_d, [(4096 + h * 512, 512)], 512)
            tiles[("g", h)] = W.get(f, nd)
            f, nd = wdma_rows(ret_w_out_d, h * 512, 4)
            tiles[("o", h)] = W.get(f, nd)

        NIT = 64

        def s1_pe_qk(i):
            h, n = divmod(i, 16)
            kcs, cslot = CS.get(lambda e, s, n=n: [e.dma_start(out=cs[:, s, :], in_=rope_ret_d[n * 128:(n + 1) * 128, :])])
            wk, ws = tiles[("qk", h)]
            proj_tok(ps[0], PSR[0], wk, ws, n)
            if n == 15:
                W.release(wk)
            return kcs, cslot

        def s1_rope(i, kcs, cslot):
            h, n = divmod(i, 16)
            par = i % 2
            qk3 = ps[0][:].rearrange("p (a b) -> p a b", a=4)
            qk4 = ps[0][:].rearrange("p (j h f) -> p j h f", j=2, h=2)
            cosb = cs[:, cslot, 0:128].unsqueeze(1).broadcast_to([128, 4, 128])
            sinb = cs[:, cslot, 128:256].unsqueeze(1).broadcast_to([128, 2, 128])
            t1v3 = t1[:, par, :].rearrange("p (a b) -> p a b", a=4)
            t1v4 = t1[:, par, :].rearrange("p (j h f) -> p j h f", j=2, h=2)
            t2a = t2[:, par, 0:256].rearrange("p (a b) -> p a b", a=2)
            t2b = t2[:, par, 256:512].rearrange("p (a b) -> p a b", a=2)
            qkr4 = qkr[:, par, :].rearrange("p (j h f) -> p j h f", j=2, h=2)
            csr = CS.res(kcs)
            P.add("dve", lambda e: e.tensor_tensor(out=t1v3, in0=qk3, in1=cosb, op=ALU.mult),
                  reads=[PSR[0], csr], writes=[("t1", par)])
            P.add("dve", lambda e: e.tensor_tensor(out=t2a, in0=qk4[:, :, 1, :], in1=sinb, op=ALU.mult),
                  reads=[PSR[0], csr], writes=[("t2a", par)])
            P.add("dve", lambda e: e.tensor_tensor(out=t2b, in0=qk4[:, :, 0, :], in1=sinb, op=ALU.mult),
                  reads=[PSR[0], csr], writes=[("t2b", par)])
            CS.release(kcs)
            P.add("dve", lambda e: e.tensor_tensor(out=qkr4[:, :, 0, :], in0=t1v4[:, :, 0, :], in1=t2a, op=ALU.subtract),
                  reads=[("t1", par), ("t2a", par)], writes=[("qkr0", par)])
            P.add("dve", lambda e: e.tensor_tensor(out=qkr4[:, :, 1, :], in0=t1v4[:, :, 1, :], in1=t2b, op=ALU.add),
                  reads=[("t1", par), ("t2b", par)], writes=[("qkr1", par)])

        def s1_kd(i):
            h, n = divmod(i, 16)
            par, p3 = i % 2, i % 3
            P.add("act", lambda e: e.activation(out=kd[:, p3, :], in_=qkr[:, par, 256:512], func=AF.Copy,
                                                scale=kdec[:, h:h + 1]),
                  reads=[("qkr0", par), ("qkr1", par), "kdec"], writes=[("kd", p3)])

        def s1_pe_v(i):
            h, n = divmod(i, 16)
            wk, ws = tiles[("v", h)]
            proj_tok(ps[1], PSR[1], wk, ws, n)
            if n == 15:
                W.release(wk)

        def s1_act_v(i):
            p3 = i % 3
            P.add("act", lambda e: e.activation(out=v_sb[:, p3, :], in_=ps[1][:], func=AF.Copy),
                  reads=[PSR[1]], writes=[("v_sb", p3)])

        def s1_pe_g(i):
            h, n = divmod(i, 16)
            wk, ws = tiles[("g", h)]
            proj_tok(ps[2], PSR[2], wk, ws, n)
            if n == 15:
                W.release(wk)

        def s1_act_g(i):
            par = i % 2
            P.add("act", lambda e: e.activation(out=eg[:, par, :], in_=ps[2][:], func=AF.Exp, scale=-1.0),
                  reads=[PSR[2]], writes=[("eg", par)])
            P.add("act", lambda e: e.activation(out=eg[:, par, :], in_=eg[:, par, :], func=AF.Ln, bias=epsc[:, 1:2]),
                  reads=[("eg", par), "epsc1"], writes=[("eg", par)])
            P.add("act", lambda e: e.activation(out=eg[:, par, :], in_=eg[:, par, :], func=AF.Exp, scale=-1.0),
                  reads=[("eg", par)], writes=[("eg", par)])

        def s1_dve_g(i):
            par, p3 = i % 2, i % 3
            P.add("dve", lambda e: e.tensor_tensor(out=sg[:, p3, :], in0=ps[2][:], in1=eg[:, par, :], op=ALU.mult),
                  reads=[PSR[2], ("eg", par)], writes=[("sg", p3)])

        def a_pe_tr(i):
            par = i % 2
            qkr4 = qkr[:, par, :].rearrange("p (j h f) -> p j h f", j=2, h=2)

            def trq(e):
                r = None
                for idx, (j, hf) in enumerate([(0, 0), (0, 1), (1, 0), (1, 1)]):
                    r = e.transpose(out=qkT_ps[:, idx, :], in_=qkr4[:, j, hf, :], identity=identb[:])
                return r
            P.add("pe", trq, reads=[("qkr0", par), ("qkr1", par), "identb"], writes=[PSR[6]])

        def a_act_cp(i):
            par = i % 2
            P.add("act", lambda e: e.activation(out=qkT[:, par, :].rearrange("p (a b) -> p a b", a=4),
                                                in_=qkT_ps, func=AF.Copy),
                  reads=[PSR[6]], writes=[("qkT", par)])

        def a_pe_inner(i):
            par = i % 2

            def inner(e):
                r = None
                for hf in range(2):
                    r = e.matmul(ps[6][:, 256:384], lhsT=qkT[:, par, (2 + hf) * 128:(3 + hf) * 128],
                                 rhs=qkT[:, par, hf * 128:(hf + 1) * 128], start=(hf == 0), stop=(hf == 1))
                return r
            P.add("pe", inner, reads=[("qkT", par)], writes=[PSR[6]])

        def a_dve_mask(i):
            h, n = divmod(i, 16)
            par = i % 2
            P.add("dve", lambda e: e.tensor_tensor(out=innerm[:, par, :], in0=ps[6][:, 256:384], in1=maskp[:, h, :],
                                                   op=ALU.mult),
                  reads=[PSR[6], "maskp"], writes=[("innerm", par)])

        def b_pe_p(i):
            h, n = divmod(i, 16)
            par, p3 = i % 2, i % 3

            def pmm(e):
                r = e.matmul(ps[3][:], lhsT=innerm[:, par, :], rhs=v_sb[:, p3, :], start=True, stop=(n == 0))
                if n > 0:
                    for hf in range(2):
                        r = e.matmul(ps[3][:], lhsT=qkT[:, par, hf * 128:(hf + 1) * 128], rhs=state_bf[:, hf, :],
                                     start=False, stop=(hf == 1))
                return r
            P.add("pe", pmm, reads=[("innerm", par), ("v_sb", p3), ("qkT", par)] +
                  ([("state_bf", 0), ("state_bf", 1)] if n > 0 else []), writes=[PSR[3]])

        def b_pe_st(i, hf):
            h, n = divmod(i, 16)
            p3 = i % 3
            if n == 15:
                return
            bank, bres = (ps[4], PSR[4]) if hf == 0 else (ps[7], PSR[7])
            P.add("pe", lambda e: e.matmul(bank[:], lhsT=kd[:, p3, hf * 128:(hf + 1) * 128], rhs=v_sb[:, p3, :],
                                           start=True, stop=True),
                  reads=[("kd", p3), ("v_sb", p3)], writes=[bres])

        def b_dve_T(i, hf):
            h, n = divmod(i, 16)
            if n == 15:
                return
            cdec = RET_GAMMA[h] ** 128
            bank, bres = (ps[4], PSR[4]) if hf == 0 else (ps[7], PSR[7])
            if n == 0:
                P.add("dve", lambda e: e.tensor_copy(out=T32[:, hf, :], in_=bank[:]),
                      reads=[bres], writes=[("T32", hf)])
            else:
                P.add("dve", lambda e: e.scalar_tensor_tensor(out=T32[:, hf, :], in0=T32[:, hf, :], scalar=cdec,
                                                              in1=bank[:], op0=ALU.mult, op1=ALU.add),
                      reads=[bres, ("T32", hf)], writes=[("T32", hf)])

        def b_act_state(i, hf):
            h, n = divmod(i, 16)
            if n == 15:
                return
            P.add("act", lambda e: e.activation(out=state_bf[:, hf, :], in_=T32[:, hf, :], func=AF.Copy),
                  reads=[("T32", hf)], writes=[("state_bf", hf)])

        def b_dve_ms(i):
            par = i % 2
            P.add("dve", lambda e: e.memset(ssq[:, par:par + 1], 0.0), writes=[("ssq", par)])

        def b_act_norm(i):
            h, n = divmod(i, 16)
            par = i % 2
            P.add("act", lambda e: e.activation(out=junk, in_=ps[3][:], func=AF.Square, accum_out=ssq[:, par:par + 1]),
                  reads=[PSR[3], ("ssq", par)], writes=[("ssq", par), "junk"])
            P.add("act", lambda e: e.activation(out=rst[:, par:par + 1], in_=ssq[:, par:par + 1], func=AF.Ln,
                                                bias=epsq[:, h:h + 1], scale=1.0 / 512.0),
                  reads=[("ssq", par), "epsq"], writes=[("rst", par)])
            P.add("act", lambda e: e.activation(out=rst2[:, par:par + 1], in_=rst[:, par:par + 1], func=AF.Exp,
                                                scale=-0.5),
                  reads=[("rst", par)], writes=[("rst2", par)])

        def b_dve_og(i):
            par, p3 = i % 2, i % 3
            P.add("dve", lambda e: e.scalar_tensor_tensor(out=og[:, par, :], in0=ps[3][:], scalar=rst2[:, par:par + 1],
                                                          in1=sg[:, p3, :], op0=ALU.mult, op1=ALU.mult),
                  reads=[PSR[3], ("rst2", par), ("sg", p3)], writes=[("og", par)])

        def c_pe_tr(i):
            par = i % 2

            def trog(e):
                r = None
                for dvc in range(4):
                    r = e.transpose(out=ogT_ps[:, dvc, :], in_=og[:, par, dvc * 128:(dvc + 1) * 128], identity=identb[:])
                return r
            P.add("pe", trog, reads=[("og", par), "identb"], writes=[PSR[5]])

        def c_act_cp(i):
            h, n = divmod(i, 16)
            hb = (i // 4) % 2
            ogTv = ogT[:, hb, :].rearrange("p (a b) -> p a b", a=4)
            P.add("act", lambda e: e.activation(out=ogTv[:, :, (n % 4) * 128:(n % 4 + 1) * 128], in_=ogT_ps, func=AF.Copy),
                  reads=[PSR[5]], writes=[("ogT", hb, n % 4)])

        opq = []

        def c_outproj(i):
            h, n = divmod(i, 16)
            if n % 4 != 3:
                return
            for dc in range(8):
                opq.append((h, n // 4, (i // 4) % 2, dc))

        def outproj_one():
            if not opq:
                return
            h, blk, hb, dc = opq.pop(0)
            ogTv = ogT[:, hb, :].rearrange("p (a b) -> p a b", a=4)
            wk, ws = tiles[("o", h)]
            wo = wring[:, ws, :].rearrange("p (a b) -> p a b", a=4)
            bank, bres = ps[1], PSR[1]

            def opj(e):
                r = None
                for dvc in range(4):
                    r = e.matmul(bank[:], lhsT=wo[:, dvc, dc * 128:(dc + 1) * 128], rhs=ogTv[:, dvc, :],
                                 start=(dvc == 0), stop=(dvc == 3))
                return r
            P.add("pe", opj, reads=[W.res(wk)] + [("ogT", hb, q) for q in range(4)], writes=[bres])
            xs = xT[:, dc, blk * 512:(blk + 1) * 512]
            cells = [("xT", dc, t) for t in range(blk * 4, blk * 4 + 4)]
            P.add("dve", lambda e: e.scalar_tensor_tensor(
                out=xs, in0=bank[:], scalar=g1_0[:, dc:dc + 1], in1=xs, op0=ALU.mult, op1=ALU.add),
                reads=[bres, "modv0"] + cells, writes=cells)
            if blk == 3 and dc == 7:
                W.release(wk)

        def S1_all(i):
            kcs, cslot = s1_pe_qk(i)
            s1_pe_v(i); s1_pe_g(i)
            s1_act_v(i); s1_act_g(i)
            s1_rope(i, kcs, cslot)
            s1_dve_g(i); s1_kd(i)
        S1_all(0)
        S1_all(1)
        a_pe_tr(0); a_act_cp(0); a_pe_inner(0); a_dve_mask(0)
        for j in range(NIT):
            if DEBUG_BARRIER:
                P.barrier()
            has_a = j + 1 < NIT
            has_s = j + 2 < NIT
            has_c = j >= 1
            i2 = j + 2
            if has_s:
                kcs, cslot = s1_pe_qk(i2)
                s1_pe_v(i2)
                s1_pe_g(i2)
            b_dve_ms(j)
            if has_s:
                s1_rope(i2, kcs, cslot)
                s1_act_v(i2)
            if has_a:
                a_pe_tr(j + 1); a_act_cp(j + 1)
            b_pe_p(j)
            if has_s:
                s1_act_g(i2)
            b_act_norm(j)
            b_pe_st(j, 0); b_dve_T(j, 0)
            b_pe_st(j, 1); b_dve_T(j, 1)
            outproj_one()
            b_act_state(j, 0); b_act_state(j, 1)
            if has_c:
                c_pe_tr(j - 1); c_act_cp(j - 1)
            if has_a:
                a_pe_inner(j + 1); a_dve_mask(j + 1)
            outproj_one()
            if has_s:
                s1_dve_g(i2)
            b_dve_og(j)
            if has_c:
                c_outproj(j - 1)
            if has_s:
                s1_kd(i2)
        c_pe_tr(NIT - 1); c_act_cp(NIT - 1); c_outproj(NIT - 1)
        while opq:
            outproj_one()

        if stop_after == "mix0":
            dump_xT([0, 3])
            P.finalize(nc, es)
            return nc

        def ffn(l):
            P.barrier()
            cv.reset()
            modnorm(1 + 2 * l, modv[:, l, 24:32], "modv%d" % l)
            m_buf = cv.alloc([128, NFC, 1024], BF16)
            a_full = cv.alloc([128, 2, 1026], F32)
            u = cv.alloc([128, 2, 512], F32)
            u2 = cv.alloc([128, 2, 512], F32)
            halo = cv.alloc([128, NFC, 2], F32)
            g2 = modv[:, l, 40:48]
            P.add("dve", lambda e: e.memset(halo, 0.0), writes=["halo"])
            wt = {}
            for half in range(2):
                for un in range(11):
                    f, nd = wdma_cols(ffn_w_in_d[l], [(un * 256, 256), (DFF + un * 256, 256)], 512)
                    wt[("in", half, un)] = W.get(f, nd)
                for dc in range(8):
                    def fn(e, s_, dc=dc):
                        wv = wring[:, s_, 0:NFC * 128].rearrange("p (a b) -> p a b", a=NFC)
                        src = ffn_w_down_d[l][:, dc * 128:(dc + 1) * 128].rearrange("(fc p) d -> p fc d", p=128)
                        return [e.dma_start(out=wv[:, 0:11, :], in_=src[:, 0:11, :]),
                                e.dma_start(out=wv[:, 11:22, :], in_=src[:, 11:22, :])]
                    wt[("dn", half, dc)] = W.get(fn, 2)
            it = 0
            for half in range(2):
                for un in range(11):
                    wk, ws = wt[("in", half, un)]
                    wv = wview(ws, 8)
                    for fcl in range(2):
                        fc = un * 2 + fcl
                        sl = fc % 2
                        P.add("act", lambda e, sl=sl, fc=fc: e.activation(out=a_full[:, sl, 0:2], in_=halo[:, fc, :],
                                                                          func=AF.Copy),
                              reads=["halo", ("halo", fc)], writes=[("a_full_h", sl)])
                        for tb in range(2):
                            gb = half * 2 + tb
                            tsl = slice(gb * 512, (gb + 1) * 512)
                            pa, pg = ps[(it % 2) * 2], ps[(it % 2) * 2 + 1]
                            ra, rg = PSR[(it % 2) * 2], PSR[(it % 2) * 2 + 1]
                            ub = it % 2
                            it += 1

                            def mma(e, bank=pa, c0=fcl * 128, tsl=tsl, wv=wv):
                                r = None
                                for kc in range(8):
                                    r = e.matmul(bank[:], lhsT=wv[:, kc, c0:c0 + 128], rhs=hT[:, kc, tsl],
                                                 start=(kc == 0), stop=(kc == 7))
                                return r
                            hreads = [("hT", kc, t) for kc in range(8) for t in range(gb * 4, gb * 4 + 4)]
                            P.add("pe", mma, reads=hreads + [W.res(wk)], writes=[ra])
                            P.add("pe", lambda e, bank=pg, c0=256 + fcl * 128, tsl=tsl, wv=wv: mma(e, bank, c0, tsl, wv),
                                  reads=hreads + [W.res(wk)], writes=[rg])
                            off = tb * 512
                            P.add("act", lambda e, sl=sl, off=off, pa=pa: e.activation(
                                out=a_full[:, sl, 2 + off:2 + off + 512], in_=pa[:], func=AF.Copy),
                                reads=[ra], writes=[("a_full", sl, tb)])
                            P.add("act", lambda e, pa=pa, ub=ub, fc=fc: e.activation(
                                out=u[:, ub, :], in_=pa[:], func=AF.Identity, bias=convb[:, l, fc:fc + 1],
                                scale=convw[:, l, 2, fc:fc + 1]),
                                reads=[ra, "convw", "convb"], writes=[("u", ub)])
                            prev = [("a_full", sl, tb - 1)] if tb > 0 else [("a_full_h", sl)]
                            P.add("dve", lambda e, sl=sl, off=off, ub=ub, fc=fc: e.scalar_tensor_tensor(
                                out=u[:, ub, :], in0=a_full[:, sl, 1 + off:1 + off + 512],
                                scalar=convw[:, l, 1, fc:fc + 1], in1=u[:, ub, :], op0=ALU.mult, op1=ALU.add),
                                reads=[("a_full", sl, tb), ("u", ub), "convw"] + prev, writes=[("u", ub)])
                            P.add("dve", lambda e, sl=sl, off=off, ub=ub, fc=fc: e.scalar_tensor_tensor(
                                out=u[:, ub, :], in0=a_full[:, sl, off:off + 512],
                                scalar=convw[:, l, 0, fc:fc + 1], in1=u[:, ub, :], op0=ALU.mult, op1=ALU.add),
                                reads=[("a_full", sl, tb), ("u", ub), "convw"] + prev, writes=[("u", ub)])
                            P.add("act", lambda e, ub=ub: e.activation(out=u2[:, ub, :], in_=u[:, ub, :], func=AF.Gelu),
                                  reads=[("u", ub)], writes=[("u2", ub)])
                            P.add("dve", lambda e, ub=ub, fc=fc, off=off, pg=pg: e.tensor_tensor(
                                out=m_buf[:, fc, off:off + 512], in0=u2[:, ub, :], in1=pg[:], op=ALU.mult),
                                reads=[("u2", ub), rg], writes=[("m", fc, tb)])
                        if half == 0:
                            P.add("act", lambda e, sl=sl, fc=fc: e.activation(out=halo[:, fc, :],
                                                                              in_=a_full[:, sl, 1024:1026], func=AF.Copy),
                                  reads=[("a_full", sl, 1)], writes=[("halo", fc)])
                    W.release(wk)
                for dc in range(8):
                    wk, ws = wt[("dn", half, dc)]
                    wd = wring[:, ws, 0:NFC * 128].rearrange("p (a b) -> p a b", a=NFC)
                    for tb in range(2):
                        gb = half * 2 + tb
                        bank, br = ps[4 + (dc * 2 + tb) % 2], PSR[4 + (dc * 2 + tb) % 2]

                        def dmm(e, bank=bank, wd=wd, tb=tb):
                            r = None
                            for fc in range(NFC):
                                r = e.matmul(bank[:], lhsT=wd[:, fc, :], rhs=m_buf[:, fc, tb * 512:(tb + 1) * 512],
                                             start=(fc == 0), stop=(fc == NFC - 1))
                            return r
                        P.add("pe", dmm, reads=[W.res(wk)] + [("m", fc, tb) for fc in range(NFC)], writes=[br])
                        xs = xT[:, dc, gb * 512:(gb + 1) * 512]
                        cells = [("xT", dc, t) for t in range(gb * 4, gb * 4 + 4)]
                        P.add("dve", lambda e, xs=xs, bank=bank, dc=dc: e.scalar_tensor_tensor(
                            out=xs, in0=bank[:], scalar=g2[:, dc:dc + 1], in1=xs, op0=ALU.mult, op1=ALU.add),
                            reads=[br, "modv%d" % l] + cells, writes=cells)
                    W.release(wk)

        ffn(0)
        if stop_after == "ffn0":
            dump_xT([0, 3])
            P.finalize(nc, es)
            return nc

        P.barrier()
        cv.reset()
        KT = cv.alloc([128, 8, S], BF16)
        Vaug = cv.alloc([128, NT, 4 * 258], BF16)
        mark_kv = cv.off
        modnorm(4, modkv[:, 0:8], "modkv", nt=1)
        P.barrier()
        cv.off = mark_kv
        csd = cv.alloc([128, 2, 128], F32)
        d1 = cv.alloc([128, 1, 512], F32)
        d2 = cv.alloc([128, 1, 512], F32)
        rr = cv.alloc([128, 2, 512], BF16)
        Vv = Vaug.rearrange("p t (h c) -> p t h c", h=4)
        P.add("dve", lambda e: e.memset(Vaug, 1.0), writes=["Vaug_init"])
        CD = P.ring("csd", 2, "sp")
        trT_ps = ps[6][:, 0:256].bitcast(BF16).rearrange("p (a b) -> p a b", a=4)

        def rope_tile(bank, bres, par, cslot, csr):
            b3 = bank[:].rearrange("p (a b) -> p a b", a=8)
            b4 = bank[:].rearrange("p (u h f) -> p u h f", u=4, h=2)
            cosb = csd[:, cslot, 0:64].unsqueeze(1).broadcast_to([128, 8, 64])
            sinb = csd[:, cslot, 64:128].unsqueeze(1).broadcast_to([128, 4, 64])
            d1v3 = d1[:, 0, :].rearrange("p (a b) -> p a b", a=8)
            d1v4 = d1[:, 0, :].rearrange("p (u h f) -> p u h f", u=4, h=2)
            d2a = d2[:, 0, 0:256].rearrange("p (a b) -> p a b", a=4)
            d2b = d2[:, 0, 256:512].rearrange("p (a b) -> p a b", a=4)
            rr4 = rr[:, par, :].rearrange("p (u h f) -> p u h f", u=4, h=2)
            P.add("dve", lambda e: e.tensor_tensor(out=d1v3, in0=b3, in1=cosb, op=ALU.mult),
                  reads=[bres, csr], writes=[("d1", 0)])
            P.add("dve", lambda e: e.tensor_tensor(out=d2a, in0=b4[:, :, 1, :], in1=sinb, op=ALU.mult),
                  reads=[bres, csr], writes=[("d2a", 0)])
            P.add("dve", lambda e: e.tensor_tensor(out=d2b, in0=b4[:, :, 0, :], in1=sinb, op=ALU.mult),
                  reads=[bres, csr], writes=[("d2b", 0)])
            P.add("dve", lambda e: e.tensor_tensor(out=rr4[:, :, 0, :], in0=d1v4[:, :, 0, :], in1=d2a, op=ALU.subtract),
                  reads=[("d1", 0), ("d2a", 0)], writes=[("rr0", par)])
            P.add("dve", lambda e: e.tensor_tensor(out=rr4[:, :, 1, :], in0=d1v4[:, :, 1, :], in1=d2b, op=ALU.add),
                  reads=[("d1", 0), ("d2b", 0)], writes=[("rr1", par)])

        def proj_rope_T(wk, ws, t, it, dstT, dres, u0):
            par = it % 2
            bank, bres = ps[par], PSR[par]
            kcs, cslot = CD.get(lambda e, s_, t=t: [e.dma_start(out=csd[:, s_, :], in_=rope_dif_d[t * 128:(t + 1) * 128, :])])
            proj_tok(bank, bres, wk, ws, t)
            rope_tile(bank, bres, par, cslot, CD.res(kcs))
            CD.release(kcs)

            def trr(e, par=par):
                r = None
                for uu in range(4):
                    r = e.transpose(out=trT_ps[:, uu, :], in_=rr[:, par, uu * 128:(uu + 1) * 128], identity=identb[:])
                return r
            P.add("pe", trr, reads=[("rr0", par), ("rr1", par), "identb"], writes=[PSR[6]])
            P.add("act", lambda e: e.activation(out=dstT[:, u0:u0 + 4, t * 128:(t + 1) * 128], in_=trT_ps, func=AF.Copy),
                  reads=[PSR[6]], writes=[(dres, u, t) for u in range(u0, u0 + 4)])

        kvt = []
        for j in range(4):
            f, nd = wdma_cols(w_kv_d, [(j * 512, 512)], 512)
            kvt.append(W.get(f, nd))
        kitems = [(j, t) for j in range(2) for t in range(NT)]

        def k_proj(i):
            j, t = kitems[i]
            wk, ws = kvt[j]
            proj_tok(ps[i % 2], PSR[i % 2], wk, ws, t)
            if t == NT - 1:
                W.release(wk)

        def k_rest(i):
            j, t = kitems[i]
            par = i % 2
            kcs, cslot = CD.get(lambda e, s_, t=t: [e.dma_start(out=csd[:, s_, :], in_=rope_dif_d[t * 128:(t + 1) * 128, :])])
            rope_tile(ps[par], PSR[par], par, cslot, CD.res(kcs))
            CD.release(kcs)

            def trr(e):
                r = None
                for uu in range(4):
                    r = e.transpose(out=trT_ps[:, uu, :], in_=rr[:, par, uu * 128:(uu + 1) * 128], identity=identb[:])
                return r
            P.add("pe", trr, reads=[("rr0", par), ("rr1", par), "identb"], writes=[PSR[6]])
            P.add("act", lambda e: e.activation(out=KT[:, j * 4:j * 4 + 4, t * 128:(t + 1) * 128], in_=trT_ps, func=AF.Copy),
                  reads=[PSR[6]], writes=[("KT", u, t) for u in range(j * 4, j * 4 + 4)])
        k_proj(0)
        for i in range(len(kitems)):
            if i + 1 < len(kitems):
                k_proj(i + 1)
            k_rest(i)
        for j in range(2):
            wk, ws = kvt[2 + j]
            for t in range(NT):
                bank, bres = ps[2 + t % 2], PSR[2 + t % 2]
                proj_tok(bank, bres, wk, ws, t)
                P.add("act", lambda e, bank=bank, t=t, j=j: e.activation(
                    out=Vv[:, t, 2 * j:2 * j + 2, 0:256], in_=bank[:].rearrange("p (a b) -> p a b", a=2), func=AF.Copy),
                    reads=[bres, "Vaug_init"], writes=[("V", t, j)])
            W.release(wk)

        P.barrier()
        cv.off = mark_kv
        modnorm(2, modv[:, 1, 0:8], "modv1", nt=1)
        P.barrier()
        cv.off = mark_kv
        csd = cv.alloc([128, 2, 128], F32)
        d1 = cv.alloc([128, 1, 512], F32)
        d2 = cv.alloc([128, 1, 512], F32)
        rr = cv.alloc([128, 2, 512], BF16)
        CD = P.ring("csd2", 2, "sp")
        qt_ = []
        for j in range(2):
            f, nd = wdma_cols(diff_w_q_d, [(j * 512, 512)], 512)
            qt_.append(W.get(f, nd))
        def q_proj(t):
            for j in range(2):
                b = (t % 2) * 2 + j
                proj_tok(ps[b], PSR[b], qt_[j][0], qt_[j][1], t)

        def q_rest(t):
            kcs, cslot = CD.get(lambda e, s_, t=t: [e.dma_start(out=csd[:, s_, :], in_=rope_dif_d[t * 128:(t + 1) * 128, :])])
            for j in range(2):
                b = (t % 2) * 2 + j
                par = j
                rope_tile(ps[b], PSR[b], par, cslot, CD.res(kcs))

                def trr(e, par=par):
                    r = None
                    for uu in range(4):
                        r = e.transpose(out=trT_ps[:, uu, :], in_=rr[:, par, uu * 128:(uu + 1) * 128], identity=identb[:])
                    return r
                P.add("pe", trr, reads=[("rr0", par), ("rr1", par), "identb"], writes=[PSR[6]])
                P.add("act", lambda e, j=j, t=t: e.activation(out=hT[:, j * 4:j * 4 + 4, t * 128:(t + 1) * 128],
                                                              in_=trT_ps, func=AF.Copy),
                      reads=[PSR[6]], writes=[("hT", u, t) for u in range(j * 4, j * 4 + 4)])
            CD.release(kcs)
        q_proj(0)
        for t in range(NT):
            if t + 1 < NT:
                q_proj(t + 1)
            q_rest(t)
        for j in range(2):
            W.release(qt_[j][0])
        QT = hT

        P.barrier()
        cv.off = mark_kv
        eT = cv.alloc([128, 2, 512], BF16)
        rec2 = cv.alloc([128, 2, 2], F32)
        r1n2 = cv.alloc([128, 2], F32)
        facc0 = cv.alloc([128, 2, 258], F32)
        ssa2 = cv.alloc([128, 2], F32)
        rsa_2 = cv.alloc([128, 2], F32)
        rsa2_2 = cv.alloc([128, 2], F32)
        on2 = cv.alloc([128, 2, 256], BF16)
        oTb = cv.alloc([128, 2 * 512], BF16)
        g1_1 = modv[:, 1, 16:24]
        oT_ps = ps[6][:, 0:128].bitcast(BF16).rearrange("p (a b) -> p a b", a=2)
        wot = []
        for h in range(4):
            def fn(e, s_, h=h):
                wv = wring[:, s_, 0:2048].rearrange("p (a b) -> p a b", a=2)
                return [e.dma_start(out=wv, in_=diff_w_out_d[h * 256:(h + 1) * 256, :].rearrange("(c p) f -> p c f", p=128))]
            wot.append(W.get(fn, 1))
        SCALE = 128.0 ** -0.5
        items = [(h, qb, kt) for h in range(4) for qb in range(8) for kt in range(2 * qb + 2)]
        oTv = oTb.rearrange("p (a b) -> p a b", a=2)
        SB = [0, 1]
        deferred = []
        opq2 = []

        def subs_of(qb, kt):
            return [0, 1] if kt <= 2 * qb else [1]

        def att_score(k):
            h, qb, kt = items[k]
            sl = k % 2
            sbank, sres = ps[SB[sl]], PSR[SB[sl]]
            subs = subs_of(qb, kt)
            c0 = subs[0] * 128
            q0 = qb * 256 + c0
            nq = 128 * len(subs)

            def smm(e):
                r = None
                for half in range(2):
                    r = e.matmul(sbank[:, half * 256 + c0:half * 256 + c0 + nq],
                                 lhsT=KT[:, h * 2 + half, kt * 128:(kt + 1) * 128],
                                 rhs=QT[:, h * 2 + half, q0:q0 + nq], start=True, stop=True)
                return r
            qtiles = [2 * qb + s_ for s_ in subs]
            P.add("pe", smm, reads=[("KT", h * 2, kt), ("KT", h * 2 + 1, kt)] +
                  [("hT", h * 2 + half, qt) for half in range(2) for qt in qtiles], writes=[sres])
            s3 = sbank[:].rearrange("p (a b) -> p a b", a=2)
            e3 = eT[:, sl, :].rearrange("p (a b) -> p a b", a=2)
            P.add("act", lambda e: e.activation(out=e3[:, :, c0:c0 + nq], in_=s3[:, :, c0:c0 + nq], func=AF.Exp,
                                                scale=SCALE),
                  reads=[sres], writes=[("eT", sl)])

        fcount = [0]
        pend = [0]

        def finalize(h, qt, sub):
            a0, a1 = ps[2 + sub], ps[4 + sub]
            r0, r1 = PSR[2 + sub], PSR[4 + sub]
            fp = fcount[0] % 2
            fcount[0] += 1
            if fp == 0:
                facc, fres = facc0, []
            else:
                k3, s3_ = wot[3]
                facc = wring[:, s3_, 2048:4096].bitcast(F32)[:, 0:516].rearrange("p (a b) -> p a b", a=2)
                fres = [W.res(k3)]
            rec = rec2[:, fp, :]
            r1n = r1n2[:, fp:fp + 1]
            ssa = ssa2[:, fp:fp + 1]
            rsa = rsa_2[:, fp:fp + 1]
            rsa2 = rsa2_2[:, fp:fp + 1]
            on = on2[:, fp, :]
            F0, F1 = ("facc", fp, 0), ("facc", fp, 1)

            def st1():
                P.add("act", lambda e: e.activation(out=facc[:, 0, 0:257], in_=a0[:, 0:257], func=AF.Copy),
                      reads=[r0] + fres, writes=[F0])
                P.add("dve", lambda e: e.tensor_copy(out=facc[:, 1, 0:257], in_=a1[:, 0:257]),
                      reads=[r1] + fres, writes=[F1])
                P.add("dve", lambda e: e.reciprocal(out=rec[:, 1:2], in_=facc[:, 1, 256:257]),
                      reads=[F1], writes=[("rec1", fp)])
                P.add("dve", lambda e: e.tensor_tensor(out=r1n, in0=rec[:, 1:2], in1=neglam[:], op=ALU.mult),
                      reads=[("rec1", fp), "neglam"], writes=[("r1n", fp)])
                P.add("dve", lambda e: e.reciprocal(out=rec[:, 0:1], in_=facc[:, 0, 256:257]),
                      reads=[F0], writes=[("rec0", fp)])
                P.add("dve", lambda e: e.tensor_scalar(out=facc[:, 1, 0:256], in0=facc[:, 1, 0:256], scalar1=r1n,
                                                       scalar2=None, op0=ALU.mult),
                      reads=[F1, ("r1n", fp)] + fres, writes=[F1])
                P.add("dve", lambda e: e.scalar_tensor_tensor(out=facc[:, 0, 0:256], in0=facc[:, 0, 0:256],
                                                              scalar=rec[:, 0:1], in1=facc[:, 1, 0:256],
                                                              op0=ALU.mult, op1=ALU.add),
                      reads=[F0, F1, ("rec0", fp)] + fres, writes=[F0])
                P.add("dve", lambda e: e.memset(ssa, 0.0), writes=[("ssa", fp)])

            def st2():
                P.add("act", lambda e: e.activation(out=on, in_=facc[:, 0, 0:256], func=AF.Square, accum_out=ssa),
                      reads=[F0, ("ssa", fp)] + fres, writes=[("ssa", fp), ("on", fp)])
                P.add("act", lambda e: e.activation(out=rsa, in_=ssa, func=AF.Ln, bias=epsc[:, 0:1],
                                                    scale=1.0 / 256.0), reads=[("ssa", fp), "epsc"], writes=[("rsa", fp)])
                P.add("act", lambda e: e.activation(out=rsa2, in_=rsa, func=AF.Exp, scale=-0.5),
                      reads=[("rsa", fp)], writes=[("rsa2", fp)])
                P.add("dve", lambda e: e.scalar_tensor_tensor(out=on, in0=facc[:, 0, 0:256], scalar=rsa2,
                                                              in1=gsub[:], op0=ALU.mult, op1=ALU.mult),
                      reads=[F0, ("rsa2", fp), "gsub"] + fres, writes=[("on", fp)])

            def st3():
                def tro(e):
                    r = None
                    for j in range(2):
                        r = e.transpose(out=oT_ps[:, j, :], in_=on[:, j * 128:(j + 1) * 128], identity=identb[:])
                    return r
                if qt % 4 == 0:
                    flush_opq2()
                P.add("pe", tro, reads=[("on", fp), "identb"], writes=[PSR[6]])
                pend[0] += 1

            def st4():
                P.add("act", lambda e: e.activation(out=oTv[:, :, (qt % 4) * 128:(qt % 4 + 1) * 128], in_=oT_ps,
                                                    func=AF.Copy),
                      reads=[PSR[6]], writes=[("oTb", qt % 4)])
                pend[0] -= 1
                if qt % 4 == 3:
                    for dc in range(8):
                        opq2.append((h, qt // 4, dc))
            for d in [d for d in deferred if d[2] == fp]:
                if d in deferred:
                    deferred.remove(d)
                    d[1]()
            st1()
            deferred.append([1, st2, fp, "a"])
            deferred.append([2, st3, fp, "b"])
            deferred.append([3, st4, fp, "c"])

        def flush_opq2():
            while opq2:
                if pend[0] > 0 and opq2[0][2] % 2 == 1:
                    for d in [d for d in deferred if d[3] == "c"]:
                        if not any(x[2] == d[2] and x[3] == "b" for x in deferred):
                            deferred.remove(d)
                            d[1]()
                    assert pend[0] == 0
                outproj2()

        def run_deferred(flush=False):
            while True:
                ready = [d for d in deferred if d[0] <= 0 or flush]
                if not ready:
                    break
                d = ready[0]
                deferred.remove(d)
                d[1]()
            for d in deferred:
                d[0] -= 1

        def outproj2():
            if not opq2:
                return
            if pend[0] > 0 and opq2[0][2] % 2 == 1:
                return
            h, blk, dc = opq2.pop(0)
            wk, ws = wot[h]
            wo = wring[:, ws, 0:2048].rearrange("p (a b) -> p a b", a=2)
            bank, bres = (ps[7], PSR[7]) if dc % 2 == 0 else (ps[6], PSR[6])

            def opj(e):
                r = None
                for j in range(2):
                    r = e.matmul(bank[:], lhsT=wo[:, j, dc * 128:(dc + 1) * 128], rhs=oTv[:, j, :],
                                 start=(j == 0), stop=(j == 1))
                return r
            P.add("pe", opj, reads=[W.res(wk)] + [("oTb", q) for q in range(4)], writes=[bres])
            xs = xT[:, dc, blk * 512:(blk + 1) * 512]
            cells = [("xT", dc, t) for t in range(blk * 4, blk * 4 + 4)]
            P.add("dve", lambda e: e.scalar_tensor_tensor(
                out=xs, in0=bank[:], scalar=g1_1[:, dc:dc + 1], in1=xs, op0=ALU.mult, op1=ALU.add),
                reads=[bres, "modv1"] + cells, writes=cells)
            if blk == 3 and dc == 7:
                W.release(wk)

        def att_av(k):
            h, qb, kt = items[k]
            sl = k % 2
            e3 = eT[:, sl, :].rearrange("p (a b) -> p a b", a=2)
            subs = subs_of(qb, kt)
            if kt >= 2 * qb:
                dsub = kt - 2 * qb
                P.add("dve", lambda e: e.tensor_tensor(
                    out=e3[:, :, dsub * 128:(dsub + 1) * 128], in0=e3[:, :, dsub * 128:(dsub + 1) * 128],
                    in1=tri[:].unsqueeze(1).broadcast_to([128, 2, 128]), op=ALU.mult),
                    reads=[("eT", sl), "tri"], writes=[("eT", sl)])
            fins = []
            for sub in subs:
                last = (kt == 2 * qb + sub)

                def avm(e, sub=sub, last=last):
                    r = None
                    for half in range(2):
                        acc = ps[2 + half * 2 + sub]
                        r = e.matmul(acc[:, 0:257], lhsT=e3[:, half, sub * 128:(sub + 1) * 128],
                                     rhs=Vv[:, kt, h, 0:257], start=(kt == 0), stop=last)
                    return r
                P.add("pe", avm, reads=[("eT", sl), ("V", kt, h // 2), "Vaug_init"],
                      writes=[PSR[2 + sub], PSR[4 + sub]])
                if last:
                    fins.append((h, 2 * qb + sub, sub))
            outproj2()
            run_deferred()
            for f_ in fins:
                finalize(*f_)
                outproj2()
                outproj2()

        att_score(0)
        for k in range(len(items)):
            if k + 1 < len(items):
                att_score(k + 1)
            att_av(k)
        for _ in range(6):
            run_deferred()
        run_deferred(flush=True)
        flush_opq2()

        if stop_after == "mix1":
            dump_xT([0, 3])
            P.finalize(nc, es)
            return nc

        ffn(1)
        if stop_after == "ffn1":
            dump_xT([0, 3])
            P.finalize(nc, es)
            return nc

        P.barrier()
        cv.reset()
        yT = cv.alloc([128, 2, 8, 512], F32)
        y_sb = cv.alloc([128, 4, 1024], F32)

        def fin_out(blk, c, tm, tres):
            if tm is None:
                return yT[:, blk % 2, c, :], ("yT", blk % 2, c)
            if c == 7:
                for tt in range(4):
                    t = blk * 4 + tt
                    ys = (blk * 4 + tt) % 4
                    for half in range(2):
                        bank, bres = ps[half], PSR[half]

                        def trf(e, bank=bank, half=half, tt=tt, blk=blk):
                            r = None
                            for j in range(4):
                                cc = half * 4 + j
                                r = e.transpose(out=bank[:, j * 128:(j + 1) * 128], in_=yT[:, blk % 2, cc, tt * 128:(tt + 1) * 128],
                                                identity=identf[:])
                            return r
                        P.add("pe", trf, reads=[("yT", blk % 2, cc) for cc in range(half * 4, half * 4 + 4)] + ["identf"],
                              writes=[bres])
                        if half == 0:
                            P.add("act", lambda e, bank=bank, ys=ys: e.activation(out=y_sb[:, ys, 0:512], in_=bank[:],
                                                                                 func=AF.Copy),
                                  reads=[bres], writes=[("y_sb", ys, 0)])
                        else:
                            P.add("dve", lambda e, bank=bank, ys=ys: e.tensor_copy(out=y_sb[:, ys, 512:1024], in_=bank[:]),
                                  reads=[bres], writes=[("y_sb", ys, 1)])
                    P.add("sp", lambda e, t=t, ys=ys: [e.dma_start(out=out_d[t * 128:(t + 1) * 128, :], in_=y_sb[:, ys, :])],
                          reads=[("y_sb", ys, 0), ("y_sb", ys, 1)], writes=[("out", t)], dma=True)
        modnorm(5, None, ("A", 5), out_fn=fin_out)
        P.barrier()
        P.finalize(nc, es)
        return nc

        raise NotImplementedError
    return nc


def fm(v):
    v = np.asarray(v, np.float32)
    n = v.shape[-1] // 128
    r = v.reshape(v.shape[:-1] + (n, 128))
    return np.ascontiguousarray(np.moveaxis(r, -1, 0))


def const_tables():
    pos = np.arange(S, dtype=np.float32)
    f_ret = (1.0 / (np.float32(10000.0) ** np.linspace(0.0, 1.0, 128, dtype=np.float32))).astype(np.float32)
    ang = (pos[:, None] * f_ret[None, :]).astype(np.float32)
    rope_ret = np.concatenate([np.cos(ang), np.sin(ang)], axis=1).astype(np.float32)
    f_dif = (1.0 / (np.float32(10000.0) ** (np.arange(0, 128, 2, dtype=np.float32) / np.float32(128)))).astype(np.float32)
    ang = (pos[:, None] * f_dif[None, :]).astype(np.float32)
    rope_dif = np.concatenate([np.cos(ang), np.sin(ang)], axis=1).astype(np.float32)
    i = np.arange(128, dtype=np.float64)
    scale = 256.0 ** -0.5
    maskp = np.zeros((128, 4, 128), np.float32)
    kdec = np.zeros((128, 4), np.float32)
    epsq = np.zeros((128, 4), np.float32)
    causal = (i[:, None] <= i[None, :])
    for h in range(4):
        lg = math.log(RET_GAMMA[h])
        maskp[:, h, :] = (scale * np.exp(-lg * (i[:, None] + 1.0)) * causal).astype(np.float32)
        kdec[:, h] = scale * np.exp(lg * (127.0 - i))
        epsq[:, h] = EPS * np.exp(-2.0 * lg * (i + 1.0))
    tri01 = causal.astype(np.float32)
    identf = np.eye(128, dtype=np.float32)
    return dict(rope_ret=rope_ret, rope_dif=rope_dif, maskp=maskp, kdec=kdec, epsq=epsq, tri01=tri01, identf=identf)


def make_in_maps(inputs, cores):
    g = {k: np.asarray(v, np.float32) for k, v in inputs.items()}
    shared = dict(
        w_ada=g["w_ada"], b_ada_fm=fm(g["b_ada"]), gain_fm=fm(g["norm_gain"]),
        kvgain_fm=fm(g["kv_norm_gain"]), fgain_fm=fm(g["final_norm_gain"]),
        kv_w_ada=g["kv_w_ada"], kv_b_ada_fm=fm(g["kv_b_ada"]),
        ret_w_in=g["ret_w_in"][0], ret_w_out=g["ret_w_out"][0],
        ffn_w_in=g["ffn_w_in"], ffn_w_down=g["ffn_w_down"],
        convw_fm=fm(g["ffn_w_conv"]), convb_fm=fm(g["ffn_b_conv"]),
        w_kv=g["w_kv"], diff_w_q=g["diff_w_q"][0], diff_w_out=g["diff_w_out"][0],
        diff_lambda=g["diff_lambda"][0], diff_subln_gain=g["diff_subln_gain"],
    )
    shared.update(const_tables())
    maps = []
    for b in cores:
        m = dict(shared)
        m["x"] = np.ascontiguousarray(g["x"][b])
        m["cT"] = fm(g["c"][b])
        maps.append(m)
    return maps


_NC_CACHE = {}


def kernel(**inputs):
    if "nc" not in _NC_CACHE:
        _NC_CACHE["nc"] = build()
    nc = _NC_CACHE["nc"]
    maps = make_in_maps(inputs, list(range(NCORES)))
    res = run_bass_kernel_spmd(nc, maps, core_ids=list(range(NCORES)))
    return np.stack([np.asarray(r["out"], np.float32) for r in res.results], axis=0)
```

```python
import math
from contextlib import ExitStack

import numpy as np
import concourse.bass as bass
import concourse.mybir as mybir
from concourse.bass_utils import run_bass_kernel_spmd

F32 = mybir.dt.float32
BF16 = mybir.dt.bfloat16
AF = mybir.ActivationFunctionType
ALU = mybir.AluOpType

D = 1024
S = 2048
NT = 16
NC8 = 8
DFF = 2816
NFC = 22
EPS = 1e-6
SQRT_D = 32.0
RET_GAMMA = [1.0 - 2.0 ** (-5.0 - h) for h in range(4)]
LAM_INIT1 = 0.8 - 0.6 * math.exp(-0.3 * 1)
NCORES = 8
DEBUG_BARRIER = False


class Op:
    __slots__ = ("eng", "fn", "reads", "writes", "dma", "ndma", "deps", "sig", "waits", "needs_sig",
                 "idx", "extra", "tag")


class Ring:
    def __init__(self, prog, name, n, eng):
        self.prog, self.name, self.n, self.eng = prog, name, n, eng
        self.tiles = []
        self.start = len(prog.ops)

    def get(self, fn, ndma=1):
        k = len(self.tiles)
        self.tiles.append({"fn": fn, "ndma": ndma, "rel": None})
        return k, k % self.n

    def res(self, k):
        return (self.name, k % self.n)

    def release(self, k):
        self.tiles[k]["rel"] = len(self.prog.ops)


class Prog:
    EPOCH = 4000
    NDSEM = 28

    def __init__(self):
        self.ops = []
        self.rings = []

    def add(self, eng, fn, reads=(), writes=(), dma=False, ndma=1, extra=(), tag=None):
        op = Op()
        reads, writes = list(reads), list(writes)
        for r in list(reads):
            if isinstance(r, tuple) and r[0] == "ps":
                reads.remove(r)
                if r not in writes:
                    writes.append(r)
        op.eng, op.fn, op.reads, op.writes = eng, fn, reads, writes
        op.dma, op.ndma, op.extra, op.tag = dma, ndma, list(extra), tag
        self.ops.append(op)
        return op

    def ring(self, name, n, eng):
        r = Ring(self, name, n, eng)
        self.rings.append(r)
        return r

    def barrier(self, engines=("pe", "act", "dve", "pool", "sp")):
        for e in engines:
            self.add(e, lambda eng: eng.nop(), writes=[("bar", e)], tag="bar")
        for e in engines:
            self.add(e, lambda eng: eng.nop(), reads=[("bar", f) for f in engines], writes=[("gate", e)],
                     tag="gate")

    def finalize(self, nc, es):
        ins = {}
        for r in self.rings:
            for k, t in enumerate(r.tiles):
                pos = r.start if k < r.n else r.tiles[k - r.n]["rel"]
                assert pos is not None, (r.name, k)
                op = Op()
                slot = k % r.n
                op.eng, op.fn = r.eng, (lambda eng, f=t["fn"], s=slot: f(eng, s))
                op.reads, op.writes, op.dma, op.ndma, op.extra, op.tag = [], [(r.name, slot)], True, t["ndma"], [], "ring"
                ins.setdefault(pos, []).append(op)
        ops = []
        for i, op in enumerate(self.ops):
            ops.extend(ins.get(i, []))
            ops.append(op)
        ops.extend(ins.get(len(self.ops), []))
        for i, op in enumerate(ops):
            op.idx = i
            op.needs_sig = False
            op.sig = None
        last_w, readers = {}, {}
        outstanding_dma = {}
        last_op = {}
        for op in ops:
            deps = {}
            for r in op.reads:
                w = last_w.get(r)
                if w is not None:
                    deps[w] = "raw"
            for w_ in op.writes:
                w = last_w.get(w_)
                if w is not None and w not in deps:
                    deps[w] = "waw"
                for rd in readers.get(w_, ()):
                    if rd not in deps:
                        deps[rd] = "war"
            if op.tag == "bar":
                for d in outstanding_dma.get(op.eng, ()):
                    deps[d] = "raw"
                outstanding_dma[op.eng] = []
                if op.eng in last_op:
                    deps[last_op[op.eng]] = "raw"
            if not op.dma:
                last_op[op.eng] = op.idx
            for r in op.reads:
                readers.setdefault(r, []).append(op.idx)
            for w_ in op.writes:
                last_w[w_] = op.idx
                readers[w_] = []
            if op.dma:
                outstanding_dma.setdefault(op.eng, []).append(op.idx)
            keep = []
            for d, kind in deps.items():
                if d == op.idx:
                    continue
                p = ops[d]
                if p.eng == op.eng and not p.dma and not op.dma and kind != "raw":
                    continue
                if p.eng == op.eng and not p.dma and op.dma and kind != "raw":
                    pass
                keep.append(d)
            op.deps = keep
            for d in keep:
                ops[d].needs_sig = True
        cnt = {}
        dma_j = {}
        qbase = {"sp": 0, "pool": 16, "act": 24}
        qn = {"sp": 16, "pool": 8, "act": 4}
        dsem_hist = [[] for _ in range(self.NDSEM)]
        for op in ops:
            if op.dma:
                jq = dma_j.get(op.eng, 0)
                dma_j[op.eng] = jq + 1
                s = qbase[op.eng] + jq % qn[op.eng]
                if dsem_hist[s]:
                    prev = dsem_hist[s][-1]
                    if prev not in op.deps:
                        op.deps.append(prev)
                tot = sum(ops[i].ndma for i in dsem_hist[s]) + op.ndma
                dsem_hist[s].append(op.idx)
                op.sig = ("d", s, 16 * tot)
                assert 16 * tot < 30000
            elif op.needs_sig:
                n = cnt.get(op.eng, 0) + 1
                cnt[op.eng] = n
                op.sig = ("c", op.eng, n)
        sems = {}
        for e, n in cnt.items():
            ne = (n - 1) // self.EPOCH + 1
            sems[e] = [es.enter_context(nc.semaphore("s_%s_%d" % (e, i))) for i in range(ne)]
        dsems = [es.enter_context(nc.semaphore("s_dma_%d" % i)) for i in range(self.NDSEM)]
        seen_c = {}
        seen_d = {}
        for op in ops:
            e = op.eng
            sc = seen_c.setdefault(e, {})
            sd = seen_d.setdefault(e, set())
            need_c = {}
            waits = []
            for d in op.deps:
                p = ops[d]
                if p.sig[0] == "d":
                    if d not in sd:
                        sd.add(d)
                        waits.append((dsems[p.sig[1]], p.sig[2]))
                else:
                    f, n = p.sig[1], p.sig[2]
                    if n > sc.get(f, 0) and n > need_c.get(f, 0):
                        need_c[f] = n
            for f, n in need_c.items():
                sc[f] = n
                ep, val = (n - 1) // self.EPOCH, (n - 1) % self.EPOCH + 1
                waits.append((sems[f][ep], val))
            op.waits = waits
        self.final_ops = ops
        self.nsig = cnt

        def emit(ename, eng):
            for op in ops:
                if op.eng != ename:
                    continue
                for s, v in op.waits:
                    eng.wait_ge(s, v)
                r = op.fn(eng)
                if op.sig is None:
                    continue
                if op.sig[0] == "d":
                    lst = r if isinstance(r, (list, tuple)) else [r]
                    assert len(lst) == op.ndma, (len(lst), op.ndma)
                    for ins_ in lst:
                        ins_.then_inc(dsems[op.sig[1]], 16)
                else:
                    n = op.sig[2]
                    r.then_inc(sems[op.sig[1]][(n - 1) // self.EPOCH], 1)

        with nc.Block() as block:
            @block.tensor
            def _(e):
                emit("pe", e)

            @block.scalar
            def _(e):
                emit("act", e)

            @block.vector
            def _(e):
                emit("dve", e)

            @block.gpsimd
            def _(e):
                emit("pool", e)

            @block.sync
            def _(e):
                emit("sp", e)


class Carve:
    def __init__(self, region, nwords):
        self.region, self.n, self.off = region, nwords, 0

    def reset(self):
        self.off = 0

    def alloc(self, shape, dtype):
        nel = int(np.prod(shape[1:]))
        nbytes = nel * (4 if dtype == F32 else 2)
        nw = (nbytes + 31) // 32 * 8
        assert self.off + nw <= self.n, ("phase region overflow", self.off, nw, self.n)
        v = self.region[:, self.off:self.off + nw]
        self.off += nw
        if dtype != F32:
            v = v.bitcast(dtype)
        v = v[:, 0:nel]
        if len(shape) == 3:
            v = v.rearrange("p (a b) -> p a b", a=shape[1])
        elif len(shape) == 4:
            v = v.rearrange("p (a b c) -> p a b c", a=shape[1], b=shape[2])
        return v


def build(stop_after=None, dbg=None):
    nc = bass.Bass("TRN2", target_bir_lowering=False)
    dt_in = lambda name, shape: nc.dram_tensor(name, list(shape), F32, kind="ExternalInput").ap()
    x_d = dt_in("x", [S, D])
    cT_d = dt_in("cT", [128, 8])
    w_ada_d = dt_in("w_ada", [2, D, 6 * D])
    b_ada_d = dt_in("b_ada_fm", [128, 2, 48])
    gain_d = dt_in("gain_fm", [128, 2, 2, 8])
    kvgain_d = dt_in("kvgain_fm", [128, 8])
    fgain_d = dt_in("fgain_fm", [128, 8])
    kv_w_ada_d = dt_in("kv_w_ada", [D, 2 * D])
    kv_b_ada_d = dt_in("kv_b_ada_fm", [128, 16])
    ret_w_in_d = dt_in("ret_w_in", [D, 6144])
    ret_w_out_d = dt_in("ret_w_out", [2048, D])
    ffn_w_in_d = dt_in("ffn_w_in", [2, D, 2 * DFF])
    ffn_w_down_d = dt_in("ffn_w_down", [2, DFF, D])
    convw_d = dt_in("convw_fm", [128, 2, 3, NFC])
    convb_d = dt_in("convb_fm", [128, 2, NFC])
    w_kv_d = dt_in("w_kv", [D, 2 * D])
    diff_w_q_d = dt_in("diff_w_q", [D, D])
    diff_w_out_d = dt_in("diff_w_out", [D, D])
    lam_d = dt_in("diff_lambda", [4, 128])
    subln_d = dt_in("diff_subln_gain", [1, 256])
    rope_ret_d = dt_in("rope_ret", [S, 256])
    rope_dif_d = dt_in("rope_dif", [S, 128])
    maskp_d = dt_in("maskp", [128, 4, 128])
    kdec_d = dt_in("kdec", [128, 4])
    epsq_d = dt_in("epsq", [128, 4])
    tri_d = dt_in("tri01", [128, 128])
    identf_d = dt_in("identf", [128, 128])
    out_d = nc.dram_tensor("out", [S, D], F32, kind="ExternalOutput").ap()
    dbg_d = None
    if dbg is not None:
        dbg_d = nc.dram_tensor("dbg", list(dbg), F32, kind="ExternalOutput").ap()

    es = ExitStack()
    with es:
        sb = lambda name, shape, dt: es.enter_context(nc.sbuf_tensor(name, list(shape), dt))
        xT = sb("xT", [128, 8, S], F32)
        hT = sb("hT", [128, 8, S], BF16)
        wring = sb("wring", [128, 4, 4096], BF16)
        identb = sb("identb", [128, 128], BF16)
        identf = sb("identf_sb", [128, 128], F32)
        onesb = sb("onesb", [128, 128], BF16)
        tri = sb("tri", [128, 128], F32)
        maskp = sb("maskp_sb", [128, 4, 128], F32)
        kdec = sb("kdec_sb", [128, 4], F32)
        epsq = sb("epsq_sb", [128, 4], F32)
        modv = sb("modv", [128, 2, 48], F32)
        modkv = sb("modkv", [128, 16], F32)
        Amod = sb("Amod", [128, 6, 8], F32)
        gains = sb("gains", [128, 6, 8], F32)
        convw = sb("convw", [128, 2, 3, NFC], F32)
        convb = sb("convb", [128, 2, NFC], F32)
        neglam = sb("neglam", [128, 1], F32)
        gsub = sb("gsub", [128, 256], F32)
        epsc = sb("epsc", [128, 2], F32)
        NREG = 18300
        region = sb("region", [128, NREG], F32)
        ps = [es.enter_context(nc.psum_tensor("ps%d" % i, [128, 512], F32)) for i in range(8)]
        PSR = [("ps", i) for i in range(8)]

        P = Prog()
        cv = Carve(region, NREG)
        W = P.ring("wring", 4, "pool")

        def wslot(slot):
            return wring[:, slot, :]

        def ld(dst, src, res):
            P.add("sp", lambda e: [e.dma_start(out=dst, in_=src)], writes=[res], dma=True)

        ld(identf[:], identf_d, "identf")
        ld(tri[:], tri_d, "tri")
        ld(maskp[:], maskp_d, "maskp")
        ld(kdec[:], kdec_d, "kdec")
        ld(epsq[:], epsq_d, "epsq")
        ld(convw[:], convw_d, "convw")
        ld(convb[:], convb_d, "convb")
        ld(gains[:, 0:4, :], gain_d.rearrange("p a b c -> p (a b) c"), "gains")
        ld(gains[:, 4, :], kvgain_d, "gains4")
        ld(gains[:, 5, :], fgain_d, "gains5")
        P.add("dve", lambda e: e.tensor_copy(out=identb[:], in_=identf[:]), reads=["identf"], writes=["identb"])
        P.add("dve", lambda e: e.memset(onesb[:], 1.0), writes=["onesb"])
        P.add("dve", lambda e: e.memset(epsc[:, 0:1], EPS), writes=["epsc"])
        P.add("dve", lambda e: e.memset(epsc[:, 1:2], 1.0), writes=["epsc1"])

        cv.reset()
        cTs = cv.alloc([128, 8], F32)
        scs = cv.alloc([128, 8], F32)
        b_ada = cv.alloc([128, 2, 48], F32)
        b_kv = cv.alloc([128, 16], F32)
        lamt = cv.alloc([128, 4, 128], F32)
        lamp = cv.alloc([128, 2, 128], F32)
        lams = cv.alloc([128, 2], F32)
        lame = cv.alloc([128, 2], F32)
        xin = cv.alloc([128, 3, D], F32)
        wst = cv.alloc([128, 3, 8 * 512], F32)

        ld(cTs, cT_d, "cTs")
        ld(b_ada, b_ada_d, "b_ada")
        ld(b_kv, kv_b_ada_d, "b_kv")
        ld(lamt, lam_d.partition_broadcast(128), "lamt")
        ld(gsub[:], subln_d.partition_broadcast(128), "gsub_raw")
        P.add("act", lambda e: e.activation(out=scs, in_=cTs, func=AF.Silu), reads=["cTs"], writes=["scs"])
        P.add("dve", lambda e: e.tensor_tensor(out=lamp, in0=lamt[:, 0:4:2, :], in1=lamt[:, 1:4:2, :], op=ALU.mult),
              reads=["lamt"], writes=["lamp"])
        P.add("dve", lambda e: e.tensor_reduce(out=lams, in_=lamp, axis=mybir.AxisListType.X, op=ALU.add),
              reads=["lamp"], writes=["lams"])
        P.add("act", lambda e: e.activation(out=lame, in_=lams, func=AF.Exp), reads=["lams"], writes=["lame"])
        P.add("dve", lambda e: e.tensor_tensor(out=neglam[:], in0=lame[:, 1:2], in1=lame[:, 0:1], op=ALU.subtract),
              reads=["lame"], writes=["neglam0"])
        P.add("dve", lambda e: e.tensor_scalar(out=neglam[:], in0=neglam[:], scalar1=-LAM_INIT1, scalar2=None,
                                               op0=ALU.add), reads=["neglam0"], writes=["neglam"])
        P.add("dve", lambda e: e.tensor_scalar(out=gsub[:], in0=gsub[:], scalar1=(1.0 - LAM_INIT1), scalar2=None,
                                               op0=ALU.mult), reads=["gsub_raw"], writes=["gsub"])

        AR = P.ring("wst", 3, "sp")
        rowsb = cv.alloc([1, 2 * 512], F32)
        pcn = [0]

        def adaln(w_d, ncols, bank, out_ap, bias_ap, res_out):
            npieces = ncols // 512
            todo_ = []
            for pc in range(npieces):
                def piece(pc=pc):
                    k, slot = AR.get(lambda e, s, pc=pc: [e.dma_start(
                        out=wst[:, s, :].rearrange("p (a b) -> p a b", a=8),
                        in_=w_d[:, pc * 512:(pc + 1) * 512].rearrange("(kc p) f -> p kc f", p=128))])
                    rb = 5 + pcn[0] % 2
                    rs_ = pcn[0] % 2
                    pcn[0] += 1

                    def mmf(e, slot=slot, rb=rb):
                        r = None
                        wv = wst[:, slot, :].rearrange("p (a b) -> p a b", a=8)
                        for kc in range(8):
                            r = e.matmul(ps[rb][0:1, 0:512], lhsT=scs[:, kc:kc + 1], rhs=wv[:, kc, :],
                                         start=(kc == 0), stop=(kc == 7))
                        return r
                    P.add("pe", mmf, reads=[AR.res(k), "scs"], writes=[PSR[rb]])
                    AR.release(k)
                    rkey = ("rowsb", id(bank), pc)
                    P.add("act", lambda e, rb=rb, rs_=rs_: e.activation(out=rowsb[0:1, rs_ * 512:(rs_ + 1) * 512],
                                                                        in_=ps[rb][0:1, 0:512], func=AF.Copy),
                          reads=[PSR[rb]], writes=[("rowsb", rs_)])

                    def redis(e, pc=pc, rs_=rs_):
                        r = None
                        for j in range(4):
                            col = pc * 4 + j
                            r = e.matmul(bank[:, col:col + 1],
                                         lhsT=rowsb[0:1, rs_ * 512 + j * 128:rs_ * 512 + (j + 1) * 128],
                                         rhs=identf[0:1, 0:1], start=True, stop=True)
                        return r
                    P.add("pe", redis, reads=[("rowsb", rs_), "identf"], writes=[("adaps", id(bank), pc)])
                todo_.append(piece)
            nj = ncols // 128

            def fin():
                P.add("dve", lambda e: e.tensor_tensor(out=out_ap, in0=bank[:, 0:nj], in1=bias_ap, op=ALU.add),
                      reads=[("adaps", id(bank), pc) for pc in range(npieces)], writes=[res_out])
            todo_.append(fin)
            return todo_

        todo = (adaln(w_ada_d[0], 6144, ps[2], modv[:, 0, :], b_ada[:, 0, :], "modv0") +
                adaln(w_ada_d[1], 6144, ps[3], modv[:, 1, :], b_ada[:, 1, :], "modv1") +
                adaln(kv_w_ada_d, 2048, ps[4], modkv[:], b_kv, "modkv"))

        XR = P.ring("xin", 3, "act")
        for t in range(NT):
            k, slot = XR.get(lambda e, s, t=t: [e.dma_start(out=xin[:, s, :], in_=x_d[t * 128:(t + 1) * 128, :])])
            for half in range(2):
                bank = ps[half]

                def tr(e, slot=slot, half=half, bank=bank):
                    r = None
                    for j in range(4):
                        c = half * 4 + j
                        r = e.transpose(out=bank[:, j * 128:(j + 1) * 128], in_=xin[:, slot, c * 128:(c + 1) * 128],
                                        identity=identf[:])
                    return r
                P.add("pe", tr, reads=[XR.res(k), "identf"], writes=[PSR[half]])
                dst = xT[:, half * 4:(half + 1) * 4, t * 128:(t + 1) * 128]
                src = bank[:].rearrange("p (a b) -> p a b", a=4)
                wr = [("xT", c, t) for c in range(half * 4, half * 4 + 4)]
                if half == 0:
                    P.add("act", lambda e, dst=dst, src=src: e.activation(out=dst, in_=src, func=AF.Copy),
                          reads=[PSR[half]], writes=wr)
                else:
                    P.add("dve", lambda e, dst=dst, src=src: e.tensor_copy(out=dst, in_=src),
                          reads=[PSR[half]], writes=wr)
            XR.release(k)
            for _ in range(2):
                if todo:
                    todo.pop(0)()

        while todo:
            todo.pop(0)()

        def mkA(idx, sc_ap, res_in):
            P.add("dve", lambda e: e.tensor_scalar(out=Amod[:, idx, :], in0=sc_ap, scalar1=1.0, scalar2=None,
                                                   op0=ALU.add), reads=[res_in], writes=[("A0", idx)])
            P.add("dve", lambda e: e.tensor_tensor(out=Amod[:, idx, :], in0=Amod[:, idx, :], in1=gains[:, idx, :],
                                                   op=ALU.mult),
                  reads=[("A0", idx), "gains", "gains4", "gains5"], writes=[("A", idx)])
        mkA(0, modv[:, 0, 8:16], "modv0")
        mkA(1, modv[:, 0, 32:40], "modv0")
        mkA(2, modv[:, 1, 8:16], "modv1")
        mkA(3, modv[:, 1, 32:40], "modv1")
        mkA(4, modkv[:, 8:16], "modkv")
        P.add("dve", lambda e: e.tensor_copy(out=Amod[:, 5, :], in_=gains[:, 5, :]), reads=["gains5"], writes=[("A", 5)])

        def modnorm(Aidx, B_ap, B_res, out_fn=None, nt=2):
            sq = cv.alloc([128, 2, 512], BF16)
            rstd = cv.alloc([128, 1, 512], F32)
            tmp = cv.alloc([128, nt, 512], F32) if out_fn is None else None
            for blk in range(4):
                tsl = slice(blk * 512, (blk + 1) * 512)
                tcells = lambda c: [("xT", c, t) for t in range(blk * 4, blk * 4 + 4)]
                for c in range(8):
                    P.add("act", lambda e, c=c, tsl=tsl: e.activation(out=sq[:, c % 2, :], in_=xT[:, c, tsl],
                                                                        func=AF.Square),
                          reads=tcells(c), writes=[("sq", c % 2)])
                    P.add("pe", lambda e, c=c: e.matmul(ps[7][:], lhsT=onesb[:], rhs=sq[:, c % 2, :],
                                                        start=(c == 0), stop=(c == 7)),
                          reads=[("sq", c % 2), "onesb"], writes=[PSR[7]])
                rs = rstd[:, 0, :]
                P.add("act", lambda e, rs=rs: e.activation(out=rs, in_=ps[7][:], func=AF.Ln, bias=epsc[:, 0:1],
                                                           scale=1.0 / D),
                      reads=[PSR[7], "epsc"], writes=[("rstd", 0)])
                P.add("act", lambda e, rs=rs: e.activation(out=rs, in_=rs, func=AF.Exp, scale=-0.5),
                      reads=[("rstd", 0)], writes=[("rstd", 0)])
                for c in range(8):
                    if out_fn is not None:
                        tm, tres = out_fn(blk, c, None, None)
                    else:
                        tm, tres = tmp[:, c % nt, :], ("nrm_tmp", c % nt)
                    P.add("dve", lambda e, c=c, tm=tm, rs=rs, tsl=tsl: e.scalar_tensor_tensor(
                        out=tm, in0=xT[:, c, tsl], scalar=Amod[:, Aidx, c:c + 1], in1=rs, op0=ALU.mult, op1=ALU.mult),
                        reads=tcells(c) + [("rstd", 0), ("A", Aidx)], writes=[tres])
                    if out_fn is not None:
                        out_fn(blk, c, tm, tres)
                    else:
                        P.add("dve", lambda e, c=c, tm=tm, tsl=tsl: e.tensor_scalar(
                            out=hT[:, c, tsl], in0=tm, scalar1=B_ap[:, c:c + 1], scalar2=None, op0=ALU.add),
                            reads=[tres, B_res], writes=[("hT", c, t) for t in range(blk * 4, blk * 4 + 4)])

        def dump_dbg(src_ap_list):
            off = 0
            for ap_, n in src_ap_list:
                P.add("sp", lambda e, ap_=ap_, off=off, n=n: [e.dma_start(out=dbg_d[:, off:off + n], in_=ap_)],
                      reads=[], writes=[("dbgout", off)], dma=True)
                off += n

        if stop_after == "x0":
            P.barrier()
            off = 0
            for b_ in [0, 3]:
                P.add("sp", lambda e, b_=b_, off=off: [e.dma_start(
                    out=dbg_d[:, off:off + 4096].rearrange("p (a b) -> p a b", a=8),
                    in_=xT[:, :, b_ * 512:(b_ + 1) * 512])], writes=[("dbgout", off)], dma=True)
                off += 4096
            P.barrier()
            P.finalize(nc, es)
            return nc
        P.barrier()
        cv.reset()
        modnorm(0, modv[:, 0, 0:8], "modv0")

        if stop_after == "h0":
            P.barrier()
            P.finalize(nc, es)
            return nc

        def dump_xT(blocks):
            P.barrier()
            off = 0
            for b_ in blocks:
                P.add("sp", lambda e, b_=b_, off=off: [e.dma_start(
                    out=dbg_d[:, off:off + 4096].rearrange("p (a b) -> p a b", a=8),
                    in_=xT[:, :, b_ * 512:(b_ + 1) * 512])], writes=[("dbgout", off)], dma=True)
                off += 4096
            P.barrier()

        cs = cv.alloc([128, 3, 256], F32)
        t1 = cv.alloc([128, 2, 512], F32)
        t2 = cv.alloc([128, 2, 512], F32)
        qkr = cv.alloc([128, 2, 512], BF16)
        kd = cv.alloc([128, 3, 256], BF16)
        v_sb = cv.alloc([128, 3, 512], BF16)
        sg = cv.alloc([128, 3, 512], F32)
        eg = cv.alloc([128, 2, 512], F32)
        qkT = cv.alloc([128, 2, 512], BF16)
        innerm = cv.alloc([128, 2, 128], BF16)
        ssq = cv.alloc([128, 2], F32)
        rst = cv.alloc([128, 2], F32)
        rst2 = cv.alloc([128, 2], F32)
        junk = cv.alloc([128, 512], BF16)
        og = cv.alloc([128, 2, 512], BF16)
        ogT = cv.alloc([128, 2, 2048], BF16)
        T32 = cv.alloc([128, 2, 512], F32)
        state_bf = cv.alloc([128, 2, 512], BF16)
        qkT_ps = ps[6][:, 0:256].bitcast(BF16).rearrange("p (a b) -> p a b", a=4)
        ogT_ps = ps[5][:, 0:256].bitcast(BF16).rearrange("p (a b) -> p a b", a=4)
        CS = P.ring("cs", 3, "sp")
        g1_0 = modv[:, 0, 16:24]

        def wview(slot, a):
            return wring[:, slot, :].rearrange("p (a b) -> p a b", a=a)

        def wdma_cols(w_d, col_ranges, width):
            def fn(e, s):
                r = []
                wv = wview(s, 8)
                o = 0
                for (c0, n_) in col_ranges:
                    r.append(e.dma_start(out=wv[:, :, o:o + n_],
                                         in_=w_d[:, c0:c0 + n_].rearrange("(kc p) f -> p kc f", p=128)))
                    o += n_
                return r
            return fn, len(col_ranges)

        def wdma_rows(w_d, r0, nchunks):
            def fn(e, s):
                wv = wring[:, s, 0:nchunks * 1024].rearrange("p (a b) -> p a b", a=nchunks)
                return [e.dma_start(out=wv, in_=w_d[r0:r0 + nchunks * 128, :].rearrange("(c p) f -> p c f", p=128))]
            return fn, 1

        def proj_tok(bank, bres, wk, wslot_, t, col0=0, ncol=512):
            def fn(e):
                r = None
                wv = wview(wslot_, 8)
                for kc in range(8):
                    r = e.matmul(bank[:, 0:ncol], lhsT=hT[:, kc, t * 128:(t + 1) * 128], rhs=wv[:, kc, col0:col0 + ncol],
                                 start=(kc == 0), stop=(kc == 7))
                return r
            P.add("pe", fn, reads=[("hT", kc, t) for kc in range(8)] + [W.res(wk)], writes=[bres])

        tiles = {}
        for h in range(4):
            f, nd = wdma_cols(ret_w_in_d, [(h * 256, 256), (1024 + h * 256, 256)], 512)
            tiles[("qk", h)] = W.get(f, nd)
            f, nd = wdma_cols(ret_w_in_d, [(2048 + h * 512, 512)], 512)
            tiles[("v", h)] = W.get(f, nd)
            f, nd = wdma_cols(ret_w_in_d, [(4096 + h * 512, 512)], 512)
            tiles[("g", h)] = W.get(f, nd)
            f, nd = wdma_rows(ret_w_out_d, h * 512, 4)
            tiles[("o", h)] = W.get(f, nd)

        NIT = 64

        def s1_pe_qk(i):
            h, n = divmod(i, 16)
            kcs, cslot = CS.get(lambda e, s, n=n: [e.dma_start(out=cs[:, s, :], in_=rope_ret_d[n * 128:(n + 1) * 128, :])])
            wk, ws = tiles[("qk", h)]
            proj_tok(ps[0], PSR[0], wk, ws, n)
            if n == 15:
                W.release(wk)
            return kcs, cslot

        def s1_rope(i, kcs, cslot):
            h, n = divmod(i, 16)
            par = i % 2
            qk3 = ps[0][:].rearrange("p (a b) -> p a b", a=4)
            qk4 = ps[0][:].rearrange("p (j h f) -> p j h f", j=2, h=2)
            cosb = cs[:, cslot, 0:128].unsqueeze(1).broadcast_to([128, 4, 128])
            sinb = cs[:, cslot, 128:256].unsqueeze(1).broadcast_to([128, 2, 128])
            t1v3 = t1[:, par, :].rearrange("p (a b) -> p a b", a=4)
            t1v4 = t1[:, par, :].rearrange("p (j h f) -> p j h f", j=2, h=2)
            t2a = t2[:, par, 0:256].rearrange("p (a b) -> p a b", a=2)
            t2b = t2[:, par, 256:512].rearrange("p (a b) -> p a b", a=2)
            qkr4 = qkr[:, par, :].rearrange("p (j h f) -> p j h f", j=2, h=2)
            csr = CS.res(kcs)
            P.add("dve", lambda e: e.tensor_tensor(out=t1v3, in0=qk3, in1=cosb, op=ALU.mult),
                  reads=[PSR[0], csr], writes=[("t1", par)])
            P.add("dve", lambda e: e.tensor_tensor(out=t2a, in0=qk4[:, :, 1, :], in1=sinb, op=ALU.mult),
                  reads=[PSR[0], csr], writes=[("t2a", par)])
            P.add("dve", lambda e: e.tensor_tensor(out=t2b, in0=qk4[:, :, 0, :], in1=sinb, op=ALU.mult),
                  reads=[PSR[0], csr], writes=[("t2b", par)])
            CS.release(kcs)
            P.add("dve", lambda e: e.tensor_tensor(out=qkr4[:, :, 0, :], in0=t1v4[:, :, 0, :], in1=t2a, op=ALU.subtract),
                  reads=[("t1", par), ("t2a", par)], writes=[("qkr0", par)])
            P.add("dve", lambda e: e.tensor_tensor(out=qkr4[:, :, 1, :], in0=t1v4[:, :, 1, :], in1=t2b, op=ALU.add),
                  reads=[("t1", par), ("t2b", par)], writes=[("qkr1", par)])

        def s1_kd(i):
            h, n = divmod(i, 16)
            par, p3 = i % 2, i % 3
            P.add("act", lambda e: e.activation(out=kd[:, p3, :], in_=qkr[:, par, 256:512], func=AF.Copy,
                                                scale=kdec[:, h:h + 1]),
                  reads=[("qkr0", par), ("qkr1", par), "kdec"], writes=[("kd", p3)])

        def s1_pe_v(i):
            h, n = divmod(i, 16)
            wk, ws = tiles[("v", h)]
            proj_tok(ps[1], PSR[1], wk, ws, n)
            if n == 15:
                W.release(wk)

        def s1_act_v(i):
            p3 = i % 3
            P.add("act", lambda e: e.activation(out=v_sb[:, p3, :], in_=ps[1][:], func=AF.Copy),
                  reads=[PSR[1]], writes=[("v_sb", p3)])

        def s1_pe_g(i):
            h, n = divmod(i, 16)
            wk, ws = tiles[("g", h)]
            proj_tok(ps[2], PSR[2], wk, ws, n)
            if n == 15:
                W.release(wk)

        def s1_act_g(i):
            par = i % 2
            P.add("act", lambda e: e.activation(out=eg[:, par, :], in_=ps[2][:], func=AF.Exp, scale=-1.0),
                  reads=[PSR[2]], writes=[("eg", par)])
            P.add("act", lambda e: e.activation(out=eg[:, par, :], in_=eg[:, par, :], func=AF.Ln, bias=epsc[:, 1:2]),
                  reads=[("eg", par), "epsc1"], writes=[("eg", par)])
            P.add("act", lambda e: e.activation(out=eg[:, par, :], in_=eg[:, par, :], func=AF.Exp, scale=-1.0),
                  reads=[("eg", par)], writes=[("eg", par)])

        def s1_dve_g(i):
            par, p3 = i % 2, i % 3
            P.add("dve", lambda e: e.tensor_tensor(out=sg[:, p3, :], in0=ps[2][:], in1=eg[:, par, :], op=ALU.mult),
                  reads=[PSR[2], ("eg", par)], writes=[("sg", p3)])

        def a_pe_tr(i):
            par = i % 2
            qkr4 = qkr[:, par, :].rearrange("p (j h f) -> p j h f", j=2, h=2)

            def trq(e):
                r = None
                for idx, (j, hf) in enumerate([(0, 0), (0, 1), (1, 0), (1, 1)]):
                    r = e.transpose(out=qkT_ps[:, idx, :], in_=qkr4[:, j, hf, :], identity=identb[:])
                return r
            P.add("pe", trq, reads=[("qkr0", par), ("qkr1", par), "identb"], writes=[PSR[6]])

        def a_act_cp(i):
            par = i % 2
            P.add("act", lambda e: e.activation(out=qkT[:, par, :].rearrange("p (a b) -> p a b", a=4),
                                                in_=qkT_ps, func=AF.Copy),
                  reads=[PSR[6]], writes=[("qkT", par)])

        def a_pe_inner(i):
            par = i % 2

            def inner(e):
                r = None
                for hf in range(2):
                    r = e.matmul(ps[6][:, 256:384], lhsT=qkT[:, par, (2 + hf) * 128:(3 + hf) * 128],
                                 rhs=qkT[:, par, hf * 128:(hf + 1) * 128], start=(hf == 0), stop=(hf == 1))
                return r
            P.add("pe", inner, reads=[("qkT", par)], writes=[PSR[6]])

        def a_dve_mask(i):
            h, n = divmod(i, 16)
            par = i % 2
            P.add("dve", lambda e: e.tensor_tensor(out=innerm[:, par, :], in0=ps[6][:, 256:384], in1=maskp[:, h, :],
                                                   op=ALU.mult),
                  reads=[PSR[6], "maskp"], writes=[("innerm", par)])

        def b_pe_p(i):
            h, n = divmod(i, 16)
            par, p3 = i % 2, i % 3

            def pmm(e):
                r = e.matmul(ps[3][:], lhsT=innerm[:, par, :], rhs=v_sb[:, p3, :], start=True, stop=(n == 0))
                if n > 0:
                    for hf in range(2):
                        r = e.matmul(ps[3][:], lhsT=qkT[:, par, hf * 128:(hf + 1) * 128], rhs=state_bf[:, hf, :],
                                     start=False, stop=(hf == 1))
                return r
            P.add("pe", pmm, reads=[("innerm", par), ("v_sb", p3), ("qkT", par)] +
                  ([("state_bf", 0), ("state_bf", 1)] if n > 0 else []), writes=[PSR[3]])

        def b_pe_st(i, hf):
            h, n = divmod(i, 16)
            p3 = i % 3
            if n == 15:
                return
            bank, bres = (ps[4], PSR[4]) if hf == 0 else (ps[7], PSR[7])
            P.add("pe", lambda e: e.matmul(bank[:], lhsT=kd[:, p3, hf * 128:(hf + 1) * 128], rhs=v_sb[:, p3, :],
                                           start=True, stop=True),
                  reads=[("kd", p3), ("v_sb", p3)], writes=[bres])

        def b_dve_T(i, hf):
            h, n = divmod(i, 16)
            if n == 15:
                return
            cdec = RET_GAMMA[h] ** 128
            bank, bres = (ps[4], PSR[4]) if hf == 0 else (ps[7], PSR[7])
            if n == 0:
                P.add("dve", lambda e: e.tensor_copy(out=T32[:, hf, :], in_=bank[:]),
                      reads=[bres], writes=[("T32", hf)])
            else:
                P.add("dve", lambda e: e.scalar_tensor_tensor(out=T32[:, hf, :], in0=T32[:, hf, :], scalar=cdec,
                                                              in1=bank[:], op0=ALU.mult, op1=ALU.add),
                      reads=[bres, ("T32", hf)], writes=[("T32", hf)])

        def b_act_state(i, hf):
            h, n = divmod(i, 16)
            if n == 15:
                return
            P.add("act", lambda e: e.activation(out=state_bf[:, hf, :], in_=T32[:, hf, :], func=AF.Copy),
                  reads=[("T32", hf)], writes=[("state_bf", hf)])

        def b_dve_ms(i):
            par = i % 2
            P.add("dve", lambda e: e.memset(ssq[:, par:par + 1], 0.0), writes=[("ssq", par)])

        def b_act_norm(i):
            h, n = divmod(i, 16)
            par = i % 2
            P.add("act", lambda e: e.activation(out=junk, in_=ps[3][:], func=AF.Square, accum_out=ssq[:, par:par + 1]),
                  reads=[PSR[3], ("ssq", par)], writes=[("ssq", par), "junk"])
            P.add("act", lambda e: e.activation(out=rst[:, par:par + 1], in_=ssq[:, par:par + 1], func=AF.Ln,
                                                bias=epsq[:, h:h + 1], scale=1.0 / 512.0),
                  reads=[("ssq", par), "epsq"], writes=[("rst", par)])
            P.add("act", lambda e: e.activation(out=rst2[:, par:par + 1], in_=rst[:, par:par + 1], func=AF.Exp,
                                                scale=-0.5),
                  reads=[("rst", par)], writes=[("rst2", par)])

        def b_dve_og(i):
            par, p3 = i % 2, i % 3
            P.add("dve", lambda e: e.scalar_tensor_tensor(out=og[:, par, :], in0=ps[3][:], scalar=rst2[:, par:par + 1],
                                                          in1=sg[:, p3, :], op0=ALU.mult, op1=ALU.mult),
                  reads=[PSR[3], ("rst2", par), ("sg", p3)], writes=[("og", par)])

        def c_pe_tr(i):
            par = i % 2

            def trog(e):
                r = None
                for dvc in range(4):
                    r = e.transpose(out=ogT_ps[:, dvc, :], in_=og[:, par, dvc * 128:(dvc + 1) * 128], identity=identb[:])
                return r
            P.add("pe", trog, reads=[("og", par), "identb"], writes=[PSR[5]])

        def c_act_cp(i):
            h, n = divmod(i, 16)
            hb = (i // 4) % 2
            ogTv = ogT[:, hb, :].rearrange("p (a b) -> p a b", a=4)
            P.add("act", lambda e: e.activation(out=ogTv[:, :, (n % 4) * 128:(n % 4 + 1) * 128], in_=ogT_ps, func=AF.Copy),
                  reads=[PSR[5]], writes=[("ogT", hb, n % 4)])

        opq = []

        def c_outproj(i):
            h, n = divmod(i, 16)
            if n % 4 != 3:
                return
            for dc in range(8):
                opq.append((h, n // 4, (i // 4) % 2, dc))

        def outproj_one():
            if not opq:
                return
            h, blk, hb, dc = opq.pop(0)
            ogTv = ogT[:, hb, :].rearrange("p (a b) -> p a b", a=4)
            wk, ws = tiles[("o", h)]
            wo = wring[:, ws, :].rearrange("p (a b) -> p a b", a=4)
            bank, bres = ps[1], PSR[1]

            def opj(e):
                r = None
                for dvc in range(4):
                    r = e.matmul(bank[:], lhsT=wo[:, dvc, dc * 128:(dc + 1) * 128], rhs=ogTv[:, dvc, :],
                                 start=(dvc == 0), stop=(dvc == 3))
                return r
            P.add("pe", opj, reads=[W.res(wk)] + [("ogT", hb, q) for q in range(4)], writes=[bres])
            xs = xT[:, dc, blk * 512:(blk + 1) * 512]
            cells = [("xT", dc, t) for t in range(blk * 4, blk * 4 + 4)]
            P.add("dve", lambda e: e.scalar_tensor_tensor(
                out=xs, in0=bank[:], scalar=g1_0[:, dc:dc + 1], in1=xs, op0=ALU.mult, op1=ALU.add),
                reads=[bres, "modv0"] + cells, writes=cells)
            if blk == 3 and dc == 7:
                W.release(wk)

        def S1_all(i):
            kcs, cslot = s1_pe_qk(i)
            s1_pe_v(i); s1_pe_g(i)
            s1_act_v(i); s1_act_g(i)
            s1_rope(i, kcs, cslot)
            s1_dve_g(i); s1_kd(i)
        S1_all(0)
        S1_all(1)
        a_pe_tr(0); a_act_cp(0); a_pe_inner(0); a_dve_mask(0)
        for j in range(NIT):
            if DEBUG_BARRIER:
                P.barrier()
            has_a = j + 1 < NIT
            has_s = j + 2 < NIT
            has_c = j >= 1
            i2 = j + 2
            if has_s:
                kcs, cslot = s1_pe_qk(i2)
                s1_pe_v(i2)
                s1_pe_g(i2)
            b_dve_ms(j)
            if has_s:
                s1_rope(i2, kcs, cslot)
                s1_act_v(i2)
            if has_a:
                a_pe_tr(j + 1); a_act_cp(j + 1)
            b_pe_p(j)
            if has_s:
                s1_act_g(i2)
            b_act_norm(j)
            b_pe_st(j, 0); b_dve_T(j, 0)
            b_pe_st(j, 1); b_dve_T(j, 1)
            outproj_one()
            b_act_state(j, 0); b_act_state(j, 1)
            if has_c:
                c_pe_tr(j - 1); c_act_cp(j - 1)
            if has_a:
                a_pe_inner(j + 1); a_dve_mask(j + 1)
            outproj_one()
            if has_s:
                s1_dve_g(i2)
            b_dve_og(j)
            if has_c:
                c_outproj(j - 1)
            if has_s:
                s1_kd(i2)
        c_pe_tr(NIT - 1); c_act_cp(NIT - 1); c_outproj(NIT - 1)
        while opq:
            outproj_one()

        if stop_after == "mix0":
            dump_xT([0, 3])
            P.finalize(nc, es)
            return nc

        def ffn(l):
            P.barrier()
            cv.reset()
            modnorm(1 + 2 * l, modv[:, l, 24:32], "modv%d" % l)
            m_buf = cv.alloc([128, NFC, 1024], BF16)
            a_full = cv.alloc([128, 2, 1026], F32)
            u = cv.alloc([128, 2, 512], F32)
            u2 = cv.alloc([128, 2, 512], F32)
            halo = cv.alloc([128, NFC, 2], F32)
            g2 = modv[:, l, 40:48]
            P.add("dve", lambda e: e.memset(halo, 0.0), writes=["halo"])
            wt = {}
            for half in range(2):
                for un in range(11):
                    f, nd = wdma_cols(ffn_w_in_d[l], [(un * 256, 256), (DFF + un * 256, 256)], 512)
                    wt[("in", half, un)] = W.get(f, nd)
                for dc in range(8):
                    def fn(e, s_, dc=dc):
                        wv = wring[:, s_, 0:NFC * 128].rearrange("p (a b) -> p a b", a=NFC)
                        src = ffn_w_down_d[l][:, dc * 128:(dc + 1) * 128].rearrange("(fc p) d -> p fc d", p=128)
                        return [e.dma_start(out=wv[:, 0:11, :], in_=src[:, 0:11, :]),
                                e.dma_start(out=wv[:, 11:22, :], in_=src[:, 11:22, :])]
                    wt[("dn", half, dc)] = W.get(fn, 2)
            it = 0
            for half in range(2):
                for un in range(11):
                    wk, ws = wt[("in", half, un)]
                    wv = wview(ws, 8)
                    for fcl in range(2):
                        fc = un * 2 + fcl
                        sl = fc % 2
                        P.add("act", lambda e, sl=sl, fc=fc: e.activation(out=a_full[:, sl, 0:2], in_=halo[:, fc, :],
                                                                          func=AF.Copy),
                              reads=["halo", ("halo", fc)], writes=[("a_full_h", sl)])
                        for tb in range(2):
                            gb = half * 2 + tb
                            tsl = slice(gb * 512, (gb + 1) * 512)
                            pa, pg = ps[(it % 2) * 2], ps[(it % 2) * 2 + 1]
                            ra, rg = PSR[(it % 2) * 2], PSR[(it % 2) * 2 + 1]
                            ub = it % 2
                            it += 1

                            def mma(e, bank=pa, c0=fcl * 128, tsl=tsl, wv=wv):
                                r = None
                                for kc in range(8):
                                    r = e.matmul(bank[:], lhsT=wv[:, kc, c0:c0 + 128], rhs=hT[:, kc, tsl],
                                                 start=(kc == 0), stop=(kc == 7))
                                return r
                            hreads = [("hT", kc, t) for kc in range(8) for t in range(gb * 4, gb * 4 + 4)]
                            P.add("pe", mma, reads=hreads + [W.res(wk)], writes=[ra])
                            P.add("pe", lambda e, bank=pg, c0=256 + fcl * 128, tsl=tsl, wv=wv: mma(e, bank, c0, tsl, wv),
                                  reads=hreads + [W.res(wk)], writes=[rg])
                            off = tb * 512
                            P.add("act", lambda e, sl=sl, off=off, pa=pa: e.activation(
                                out=a_full[:, sl, 2 + off:2 + off + 512], in_=pa[:], func=AF.Copy),
                                reads=[ra], writes=[("a_full", sl, tb)])
                            P.add("act", lambda e, pa=pa, ub=ub, fc=fc: e.activation(
                                out=u[:, ub, :], in_=pa[:], func=AF.Identity, bias=convb[:, l, fc:fc + 1],
                                scale=convw[:, l, 2, fc:fc + 1]),
                                reads=[ra, "convw", "convb"], writes=[("u", ub)])
                            prev = [("a_full", sl, tb - 1)] if tb > 0 else [("a_full_h", sl)]
                            P.add("dve", lambda e, sl=sl, off=off, ub=ub, fc=fc: e.scalar_tensor_tensor(
                                out=u[:, ub, :], in0=a_full[:, sl, 1 + off:1 + off + 512],
                                scalar=convw[:, l, 1, fc:fc + 1], in1=u[:, ub, :], op0=ALU.mult, op1=ALU.add),
                                reads=[("a_full", sl, tb), ("u", ub), "convw"] + prev, writes=[("u", ub)])
                            P.add("dve", lambda e, sl=sl, off=off, ub=ub, fc=fc: e.scalar_tensor_tensor(
                                out=u[:, ub, :], in0=a_full[:, sl, off:off + 512],
                                scalar=convw[:, l, 0, fc:fc + 1], in1=u[:, ub, :], op0=ALU.mult, op1=ALU.add),
                                reads=[("a_full", sl, tb), ("u", ub), "convw"] + prev, writes=[("u", ub)])
                            P.add("act", lambda e, ub=ub: e.activation(out=u2[:, ub, :], in_=u[:, ub, :], func=AF.Gelu),
                                  reads=[("u", ub)], writes=[("u2", ub)])
                            P.add("dve", lambda e, ub=ub, fc=fc, off=off, pg=pg: e.tensor_tensor(
                                out=m_buf[:, fc, off:off + 512], in0=u2[:, ub, :], in1=pg[:], op=ALU.mult),
                                reads=[("u2", ub), rg], writes=[("m", fc, tb)])
                        if half == 0:
                            P.add("act", lambda e, sl=sl, fc=fc: e.activation(out=halo[:, fc, :],
                                                                              in_=a_full[:, sl, 1024:1026], func=AF.Copy),
                                  reads=[("a_full", sl, 1)], writes=[("halo", fc)])
                    W.release(wk)
                for dc in range(8):
                    wk, ws = wt[("dn", half, dc)]
                    wd = wring[:, ws, 0:NFC * 128].rearrange("p (a b) -> p a b", a=NFC)
                    for tb in range(2):
                        gb = half * 2 + tb
                        bank, br = ps[4 + (dc * 2 + tb) % 2], PSR[4 + (dc * 2 + tb) % 2]

                        def dmm(e, bank=bank, wd=wd, tb=tb):
                            r = None
                            for fc in range(NFC):
                                r = e.matmul(bank[:], lhsT=wd[:, fc, :], rhs=m_buf[:, fc, tb * 512:(tb + 1) * 512],
                                             start=(fc == 0), stop=(fc == NFC - 1))
                            return r
                        P.add("pe", dmm, reads=[W.res(wk)] + [("m", fc, tb) for fc in range(NFC)], writes=[br])
                        xs = xT[:, dc, gb * 512:(gb + 1) * 512]
                        cells = [("xT", dc, t) for t in range(gb * 4, gb * 4 + 4)]
                        P.add("dve", lambda e, xs=xs, bank=bank, dc=dc: e.scalar_tensor_tensor(
                            out=xs, in0=bank[:], scalar=g2[:, dc:dc + 1], in1=xs, op0=ALU.mult, op1=ALU.add),
                            reads=[br, "modv%d" % l] + cells, writes=cells)
                    W.release(wk)

        ffn(0)
        if stop_after == "ffn0":
            dump_xT([0, 3])
            P.finalize(nc, es)
            return nc

        P.barrier()
        cv.reset()
        KT = cv.alloc([128, 8, S], BF16)
        Vaug = cv.alloc([128, NT, 4 * 258], BF16)
        mark_kv = cv.off
        modnorm(4, modkv[:, 0:8], "modkv", nt=1)
        P.barrier()
        cv.off = mark_kv
        csd = cv.alloc([128, 2, 128], F32)
        d1 = cv.alloc([128, 1, 512], F32)
        d2 = cv.alloc([128, 1, 512], F32)
        rr = cv.alloc([128, 2, 512], BF16)
        Vv = Vaug.rearrange("p t (h c) -> p t h c", h=4)
        P.add("dve", lambda e: e.memset(Vaug, 1.0), writes=["Vaug_init"])
        CD = P.ring("csd", 2, "sp")
        trT_ps = ps[6][:, 0:256].bitcast(BF16).rearrange("p (a b) -> p a b", a=4)

        def rope_tile(bank, bres, par, cslot, csr):
            b3 = bank[:].rearrange("p (a b) -> p a b", a=8)
            b4 = bank[:].rearrange("p (u h f) -> p u h f", u=4, h=2)
            cosb = csd[:, cslot, 0:64].unsqueeze(1).broadcast_to([128, 8, 64])
            sinb = csd[:, cslot, 64:128].unsqueeze(1).broadcast_to([128, 4, 64])
            d1v3 = d1[:, 0, :].rearrange("p (a b) -> p a b", a=8)
            d1v4 = d1[:, 0, :].rearrange("p (u h f) -> p u h f", u=4, h=2)
            d2a = d2[:, 0, 0:256].rearrange("p (a b) -> p a b", a=4)
            d2b = d2[:, 0, 256:512].rearrange("p (a b) -> p a b", a=4)
            rr4 = rr[:, par, :].rearrange("p (u h f) -> p u h f", u=4, h=2)
            P.add("dve", lambda e: e.tensor_tensor(out=d1v3, in0=b3, in1=cosb, op=ALU.mult),
                  reads=[bres, csr], writes=[("d1", 0)])
            P.add("dve", lambda e: e.tensor_tensor(out=d2a, in0=b4[:, :, 1, :], in1=sinb, op=ALU.mult),
                  reads=[bres, csr], writes=[("d2a", 0)])
            P.add("dve", lambda e: e.tensor_tensor(out=d2b, in0=b4[:, :, 0, :], in1=sinb, op=ALU.mult),
                  reads=[bres, csr], writes=[("d2b", 0)])
            P.add("dve", lambda e: e.tensor_tensor(out=rr4[:, :, 0, :], in0=d1v4[:, :, 0, :], in1=d2a, op=ALU.subtract),
                  reads=[("d1", 0), ("d2a", 0)], writes=[("rr0", par)])
            P.add("dve", lambda e: e.tensor_tensor(out=rr4[:, :, 1, :], in0=d1v4[:, :, 1, :], in1=d2b, op=ALU.add),
                  reads=[("d1", 0), ("d2b", 0)], writes=[("rr1", par)])

        def proj_rope_T(wk, ws, t, it, dstT, dres, u0):
            par = it % 2
            bank, bres = ps[par], PSR[par]
            kcs, cslot = CD.get(lambda e, s_, t=t: [e.dma_start(out=csd[:, s_, :], in_=rope_dif_d[t * 128:(t + 1) * 128, :])])
            proj_tok(bank, bres, wk, ws, t)
            rope_tile(bank, bres, par, cslot, CD.res(kcs))
            CD.release(kcs)

            def trr(e, par=par):
                r = None
                for uu in range(4):
                    r = e.transpose(out=trT_ps[:, uu, :], in_=rr[:, par, uu * 128:(uu + 1) * 128], identity=identb[:])
                return r
            P.add("pe", trr, reads=[("rr0", par), ("rr1", par), "identb"], writes=[PSR[6]])
            P.add("act", lambda e: e.activation(out=dstT[:, u0:u0 + 4, t * 128:(t + 1) * 128], in_=trT_ps, func=AF.Copy),
                  reads=[PSR[6]], writes=[(dres, u, t) for u in range(u0, u0 + 4)])

        kvt = []
        for j in range(4):
            f, nd = wdma_cols(w_kv_d, [(j * 512, 512)], 512)
            kvt.append(W.get(f, nd))
        kitems = [(j, t) for j in range(2) for t in range(NT)]

        def k_proj(i):
            j, t = kitems[i]
            wk, ws = kvt[j]
            proj_tok(ps[i % 2], PSR[i % 2], wk, ws, t)
            if t == NT - 1:
                W.release(wk)

        def k_rest(i):
            j, t = kitems[i]
            par = i % 2
            kcs, cslot = CD.get(lambda e, s_, t=t: [e.dma_start(out=csd[:, s_, :], in_=rope_dif_d[t * 128:(t + 1) * 128, :])])
            rope_tile(ps[par], PSR[par], par, cslot, CD.res(kcs))
            CD.release(kcs)

            def trr(e):
                r = None
                for uu in range(4):
                    r = e.transpose(out=trT_ps[:, uu, :], in_=rr[:, par, uu * 128:(uu + 1) * 128], identity=identb[:])
                return r
            P.add("pe", trr, reads=[("rr0", par), ("rr1", par), "identb"], writes=[PSR[6]])
            P.add("act", lambda e: e.activation(out=KT[:, j * 4:j * 4 + 4, t * 128:(t + 1) * 128], in_=trT_ps, func=AF.Copy),
                  reads=[PSR[6]], writes=[("KT", u, t) for u in range(j * 4, j * 4 + 4)])
        k_proj(0)
        for i in range(len(kitems)):
            if i + 1 < len(kitems):
                k_proj(i + 1)
            k_rest(i)
        for j in range(2):
            wk, ws = kvt[2 + j]
            for t in range(NT):
                bank, bres = ps[2 + t % 2], PSR[2 + t % 2]
                proj_tok(bank, bres, wk, ws, t)
                P.add("act", lambda e, bank=bank, t=t, j=j: e.activation(
                    out=Vv[:, t, 2 * j:2 * j + 2, 0:256], in_=bank[:].rearrange("p (a b) -> p a b", a=2), func=AF.Copy),
                    reads=[bres, "Vaug_init"], writes=[("V", t, j)])
            W.release(wk)

        P.barrier()
        cv.off = mark_kv
        modnorm(2, modv[:, 1, 0:8], "modv1", nt=1)
        P.barrier()
        cv.off = mark_kv
        csd = cv.alloc([128, 2, 128], F32)
        d1 = cv.alloc([128, 1, 512], F32)
        d2 = cv.alloc([128, 1, 512], F32)
        rr = cv.alloc([128, 2, 512], BF16)
        CD = P.ring("csd2", 2, "sp")
        qt_ = []
        for j in range(2):
            f, nd = wdma_cols(diff_w_q_d, [(j * 512, 512)], 512)
            qt_.append(W.get(f, nd))
        def q_proj(t):
            for j in range(2):
                b = (t % 2) * 2 + j
                proj_tok(ps[b], PSR[b], qt_[j][0], qt_[j][1], t)

        def q_rest(t):
            kcs, cslot = CD.get(lambda e, s_, t=t: [e.dma_start(out=csd[:, s_, :], in_=rope_dif_d[t * 128:(t + 1) * 128, :])])
            for j in range(2):
                b = (t % 2) * 2 + j
                par = j
                rope_tile(ps[b], PSR[b], par, cslot, CD.res(kcs))

                def trr(e, par=par):
                    r = None
                    for uu in range(4):
                        r = e.transpose(out=trT_ps[:, uu, :], in_=rr[:, par, uu * 128:(uu + 1) * 128], identity=identb[:])
                    return r
                P.add("pe", trr, reads=[("rr0", par), ("rr1", par), "identb"], writes=[PSR[6]])
                P.add("act", lambda e, j=j, t=t: e.activation(out=hT[:, j * 4:j * 4 + 4, t * 128:(t + 1) * 128],
                                                              in_=trT_ps, func=AF.Copy),
                      reads=[PSR[6]], writes=[("hT", u, t) for u in range(j * 4, j * 4 + 4)])
            CD.release(kcs)
        q_proj(0)
        for t in range(NT):
            if t + 1 < NT:
                q_proj(t + 1)
            q_rest(t)
        for j in range(2):
            W.release(qt_[j][0])
        QT = hT

        P.barrier()
        cv.off = mark_kv
        eT = cv.alloc([128, 2, 512], BF16)
        rec2 = cv.alloc([128, 2, 2], F32)
        r1n2 = cv.alloc([128, 2], F32)
        facc0 = cv.alloc([128, 2, 258], F32)
        ssa2 = cv.alloc([128, 2], F32)
        rsa_2 = cv.alloc([128, 2], F32)
        rsa2_2 = cv.alloc([128, 2], F32)
        on2 = cv.alloc([128, 2, 256], BF16)
        oTb = cv.alloc([128, 2 * 512], BF16)
        g1_1 = modv[:, 1, 16:24]
        oT_ps = ps[6][:, 0:128].bitcast(BF16).rearrange("p (a b) -> p a b", a=2)
        wot = []
        for h in range(4):
            def fn(e, s_, h=h):
                wv = wring[:, s_, 0:2048].rearrange("p (a b) -> p a b", a=2)
                return [e.dma_start(out=wv, in_=diff_w_out_d[h * 256:(h + 1) * 256, :].rearrange("(c p) f -> p c f", p=128))]
            wot.append(W.get(fn, 1))
        SCALE = 128.0 ** -0.5
        items = [(h, qb, kt) for h in range(4) for qb in range(8) for kt in range(2 * qb + 2)]
        oTv = oTb.rearrange("p (a b) -> p a b", a=2)
        SB = [0, 1]
        deferred = []
        opq2 = []

        def subs_of(qb, kt):
            return [0, 1] if kt <= 2 * qb else [1]

        def att_score(k):
            h, qb, kt = items[k]
            sl = k % 2
            sbank, sres = ps[SB[sl]], PSR[SB[sl]]
            subs = subs_of(qb, kt)
            c0 = subs[0] * 128
            q0 = qb * 256 + c0
            nq = 128 * len(subs)

            def smm(e):
                r = None
                for half in range(2):
                    r = e.matmul(sbank[:, half * 256 + c0:half * 256 + c0 + nq],
                                 lhsT=KT[:, h * 2 + half, kt * 128:(kt + 1) * 128],
                                 rhs=QT[:, h * 2 + half, q0:q0 + nq], start=True, stop=True)
                return r
            qtiles = [2 * qb + s_ for s_ in subs]
            P.add("pe", smm, reads=[("KT", h * 2, kt), ("KT", h * 2 + 1, kt)] +
                  [("hT", h * 2 + half, qt) for half in range(2) for qt in qtiles], writes=[sres])
            s3 = sbank[:].rearrange("p (a b) -> p a b", a=2)
            e3 = eT[:, sl, :].rearrange("p (a b) -> p a b", a=2)
            P.add("act", lambda e: e.activation(out=e3[:, :, c0:c0 + nq], in_=s3[:, :, c0:c0 + nq], func=AF.Exp,
                                                scale=SCALE),
                  reads=[sres], writes=[("eT", sl)])

        fcount = [0]
        pend = [0]

        def finalize(h, qt, sub):
            a0, a1 = ps[2 + sub], ps[4 + sub]
            r0, r1 = PSR[2 + sub], PSR[4 + sub]
            fp = fcount[0] % 2
            fcount[0] += 1
            if fp == 0:
                facc, fres = facc0, []
            else:
                k3, s3_ = wot[3]
                facc = wring[:, s3_, 2048:4096].bitcast(F32)[:, 0:516].rearrange("p (a b) -> p a b", a=2)
                fres = [W.res(k3)]
            rec = rec2[:, fp, :]
            r1n = r1n2[:, fp:fp + 1]
            ssa = ssa2[:, fp:fp + 1]
            rsa = rsa_2[:, fp:fp + 1]
            rsa2 = rsa2_2[:, fp:fp + 1]
            on = on2[:, fp, :]
            F0, F1 = ("facc", fp, 0), ("facc", fp, 1)

            def st1():
                P.add("act", lambda e: e.activation(out=facc[:, 0, 0:257], in_=a0[:, 0:257], func=AF.Copy),
                      reads=[r0] + fres, writes=[F0])
                P.add("dve", lambda e: e.tensor_copy(out=facc[:, 1, 0:257], in_=a1[:, 0:257]),
                      reads=[r1] + fres, writes=[F1])
                P.add("dve", lambda e: e.reciprocal(out=rec[:, 1:2], in_=facc[:, 1, 256:257]),
                      reads=[F1], writes=[("rec1", fp)])
                P.add("dve", lambda e: e.tensor_tensor(out=r1n, in0=rec[:, 1:2], in1=neglam[:], op=ALU.mult),
                      reads=[("rec1", fp), "neglam"], writes=[("r1n", fp)])
                P.add("dve", lambda e: e.reciprocal(out=rec[:, 0:1], in_=facc[:, 0, 256:257]),
                      reads=[F0], writes=[("rec0", fp)])
                P.add("dve", lambda e: e.tensor_scalar(out=facc[:, 1, 0:256], in0=facc[:, 1, 0:256], scalar1=r1n,
                                                       scalar2=None, op0=ALU.mult),
                      reads=[F1, ("r1n", fp)] + fres, writes=[F1])
                P.add("dve", lambda e: e.scalar_tensor_tensor(out=facc[:, 0, 0:256], in0=facc[:, 0, 0:256],
                                                              scalar=rec[:, 0:1], in1=facc[:, 1, 0:256],
                                                              op0=ALU.mult, op1=ALU.add),
                      reads=[F0, F1, ("rec0", fp)] + fres, writes=[F0])
                P.add("dve", lambda e: e.memset(ssa, 0.0), writes=[("ssa", fp)])

            def st2():
                P.add("act", lambda e: e.activation(out=on, in_=facc[:, 0, 0:256], func=AF.Square, accum_out=ssa),
                      reads=[F0, ("ssa", fp)] + fres, writes=[("ssa", fp), ("on", fp)])
                P.add("act", lambda e: e.activation(out=rsa, in_=ssa, func=AF.Ln, bias=epsc[:, 0:1],
                                                    scale=1.0 / 256.0), reads=[("ssa", fp), "epsc"], writes=[("rsa", fp)])
                P.add("act", lambda e: e.activation(out=rsa2, in_=rsa, func=AF.Exp, scale=-0.5),
                      reads=[("rsa", fp)], writes=[("rsa2", fp)])
                P.add("dve", lambda e: e.scalar_tensor_tensor(out=on, in0=facc[:, 0, 0:256], scalar=rsa2,
                                                              in1=gsub[:], op0=ALU.mult, op1=ALU.mult),
                      reads=[F0, ("rsa2", fp), "gsub"] + fres, writes=[("on", fp)])

            def st3():
                def tro(e):
                    r = None
                    for j in range(2):
                        r = e.transpose(out=oT_ps[:, j, :], in_=on[:, j * 128:(j + 1) * 128], identity=identb[:])
                    return r
                if qt % 4 == 0:
                    flush_opq2()
                P.add("pe", tro, reads=[("on", fp), "identb"], writes=[PSR[6]])
                pend[0] += 1

            def st4():
                P.add("act", lambda e: e.activation(out=oTv[:, :, (qt % 4) * 128:(qt % 4 + 1) * 128], in_=oT_ps,
                                                    func=AF.Copy),
                      reads=[PSR[6]], writes=[("oTb", qt % 4)])
                pend[0] -= 1
                if qt % 4 == 3:
                    for dc in range(8):
                        opq2.append((h, qt // 4, dc))
            for d in [d for d in deferred if d[2] == fp]:
                if d in deferred:
                    deferred.remove(d)
                    d[1]()
            st1()
            deferred.append([1, st2, fp, "a"])
            deferred.append([2, st3, fp, "b"])
            deferred.append([3, st4, fp, "c"])

        def flush_opq2():
            while opq2:
                if pend[0] > 0 and opq2[0][2] % 2 == 1:
                    for d in [d for d in deferred if d[3] == "c"]:
                        if not any(x[2] == d[2] and x[3] == "b" for x in deferred):
                            deferred.remove(d)
                            d[1]()
                    assert pend[0] == 0
                outproj2()

        def run_deferred(flush=False):
            while True:
                ready = [d for d in deferred if d[0] <= 0 or flush]
                if not ready:
                    break
                d = ready[0]
                deferred.remove(d)
                d[1]()
            for d in deferred:
                d[0] -= 1

        def outproj2():
            if not opq2:
                return
            if pend[0] > 0 and opq2[0][2] % 2 == 1:
                return
            h, blk, dc = opq2.pop(0)
            wk, ws = wot[h]
            wo = wring[:, ws, 0:2048].rearrange("p (a b) -> p a b", a=2)
            bank, bres = (ps[7], PSR[7]) if dc % 2 == 0 else (ps[6], PSR[6])

            def opj(e):
                r = None
                for j in range(2):
                    r = e.matmul(bank[:], lhsT=wo[:, j, dc * 128:(dc + 1) * 128], rhs=oTv[:, j, :],
                                 start=(j == 0), stop=(j == 1))
                return r
            P.add("pe", opj, reads=[W.res(wk)] + [("oTb", q) for q in range(4)], writes=[bres])
            xs = xT[:, dc, blk * 512:(blk + 1) * 512]
            cells = [("xT", dc, t) for t in range(blk * 4, blk * 4 + 4)]
            P.add("dve", lambda e: e.scalar_tensor_tensor(
                out=xs, in0=bank[:], scalar=g1_1[:, dc:dc + 1], in1=xs, op0=ALU.mult, op1=ALU.add),
                reads=[bres, "modv1"] + cells, writes=cells)
            if blk == 3 and dc == 7:
                W.release(wk)

        def att_av(k):
            h, qb, kt = items[k]
            sl = k % 2
            e3 = eT[:, sl, :].rearrange("p (a b) -> p a b", a=2)
            subs = subs_of(qb, kt)
            if kt >= 2 * qb:
                dsub = kt - 2 * qb
                P.add("dve", lambda e: e.tensor_tensor(
                    out=e3[:, :, dsub * 128:(dsub + 1) * 128], in0=e3[:, :, dsub * 128:(dsub + 1) * 128],
                    in1=tri[:].unsqueeze(1).broadcast_to([128, 2, 128]), op=ALU.mult),
                    reads=[("eT", sl), "tri"], writes=[("eT", sl)])
            fins = []
            for sub in subs:
                last = (kt == 2 * qb + sub)

                def avm(e, sub=sub, last=last):
                    r = None
                    for half in range(2):
                        acc = ps[2 + half * 2 + sub]
                        r = e.matmul(acc[:, 0:257], lhsT=e3[:, half, sub * 128:(sub + 1) * 128],
                                     rhs=Vv[:, kt, h, 0:257], start=(kt == 0), stop=last)
                    return r
                P.add("pe", avm, reads=[("eT", sl), ("V", kt, h // 2), "Vaug_init"],
                      writes=[PSR[2 + sub], PSR[4 + sub]])
                if last:
                    fins.append((h, 2 * qb + sub, sub))
            outproj2()
            run_deferred()
            for f_ in fins:
                finalize(*f_)
                outproj2()
                outproj2()

        att_score(0)
        for k in range(len(items)):
            if k + 1 < len(items):
                att_score(k + 1)
            att_av(k)
        for _ in range(6):
            run_deferred()
        run_deferred(flush=True)
        flush_opq2()

        if stop_after == "mix1":
            dump_xT([0, 3])
            P.finalize(nc, es)
            return nc

        ffn(1)
        if stop_after == "ffn1":
            dump_xT([0, 3])
            P.finalize(nc, es)
            return nc

        P.barrier()
        cv.reset()
        yT = cv.alloc([128, 2, 8, 512], F32)
        y_sb = cv.alloc([128, 4, 1024], F32)

        def fin_out(blk, c, tm, tres):
            if tm is None:
                return yT[:, blk % 2, c, :], ("yT", blk % 2, c)
            if c == 7:
                for tt in range(4):
                    t = blk * 4 + tt
                    ys = (blk * 4 + tt) % 4
                    for half in range(2):
                        bank, bres = ps[half], PSR[half]

                        def trf(e, bank=bank, half=half, tt=tt, blk=blk):
                            r = None
                            for j in range(4):
                                cc = half * 4 + j
                                r = e.transpose(out=bank[:, j * 128:(j + 1) * 128], in_=yT[:, blk % 2, cc, tt * 128:(tt + 1) * 128],
                                                identity=identf[:])
                            return r
                        P.add("pe", trf, reads=[("yT", blk % 2, cc) for cc in range(half * 4, half * 4 + 4)] + ["identf"],
                              writes=[bres])
                        if half == 0:
                            P.add("act", lambda e, bank=bank, ys=ys: e.activation(out=y_sb[:, ys, 0:512], in_=bank[:],
                                                                                 func=AF.Copy),
                                  reads=[bres], writes=[("y_sb", ys, 0)])
                        else:
                            P.add("dve", lambda e, bank=bank, ys=ys: e.tensor_copy(out=y_sb[:, ys, 512:1024], in_=bank[:]),
                                  reads=[bres], writes=[("y_sb", ys, 1)])
                    P.add("sp", lambda e, t=t, ys=ys: [e.dma_start(out=out_d[t * 128:(t + 1) * 128, :], in_=y_sb[:, ys, :])],
                          reads=[("y_sb", ys, 0), ("y_sb", ys, 1)], writes=[("out", t)], dma=True)
        modnorm(5, None, ("A", 5), out_fn=fin_out)
        P.barrier()
        P.finalize(nc, es)
        return nc

        raise NotImplementedError
    return nc


def fm(v):
    v = np.asarray(v, np.float32)
    n = v.shape[-1] // 128
    r = v.reshape(v.shape[:-1] + (n, 128))
    return np.ascontiguousarray(np.moveaxis(r, -1, 0))


def const_tables():
    pos = np.arange(S, dtype=np.float32)
    f_ret = (1.0 / (np.float32(10000.0) ** np.linspace(0.0, 1.0, 128, dtype=np.float32))).astype(np.float32)
    ang = (pos[:, None] * f_ret[None, :]).astype(np.float32)
    rope_ret = np.concatenate([np.cos(ang), np.sin(ang)], axis=1).astype(np.float32)
    f_dif = (1.0 / (np.float32(10000.0) ** (np.arange(0, 128, 2, dtype=np.float32) / np.float32(128)))).astype(np.float32)
    ang = (pos[:, None] * f_dif[None, :]).astype(np.float32)
    rope_dif = np.concatenate([np.cos(ang), np.sin(ang)], axis=1).astype(np.float32)
    i = np.arange(128, dtype=np.float64)
    scale = 256.0 ** -0.5
    maskp = np.zeros((128, 4, 128), np.float32)
    kdec = np.zeros((128, 4), np.float32)
    epsq = np.zeros((128, 4), np.float32)
    causal = (i[:, None] <= i[None, :])
    for h in range(4):
        lg = math.log(RET_GAMMA[h])
        maskp[:, h, :] = (scale * np.exp(-lg * (i[:, None] + 1.0)) * causal).astype(np.float32)
        kdec[:, h] = scale * np.exp(lg * (127.0 - i))
        epsq[:, h] = EPS * np.exp(-2.0 * lg * (i + 1.0))
    tri01 = causal.astype(np.float32)
    identf = np.eye(128, dtype=np.float32)
    return dict(rope_ret=rope_ret, rope_dif=rope_dif, maskp=maskp, kdec=kdec, epsq=epsq, tri01=tri01, identf=identf)


def make_in_maps(inputs, cores):
    g = {k: np.asarray(v, np.float32) for k, v in inputs.items()}
    shared = dict(
        w_ada=g["w_ada"], b_ada_fm=fm(g["b_ada"]), gain_fm=fm(g["norm_gain"]),
        kvgain_fm=fm(g["kv_norm_gain"]), fgain_fm=fm(g["final_norm_gain"]),
        kv_w_ada=g["kv_w_ada"], kv_b_ada_fm=fm(g["kv_b_ada"]),
        ret_w_in=g["ret_w_in"][0], ret_w_out=g["ret_w_out"][0],
        ffn_w_in=g["ffn_w_in"], ffn_w_down=g["ffn_w_down"],
        convw_fm=fm(g["ffn_w_conv"]), convb_fm=fm(g["ffn_b_conv"]),
        w_kv=g["w_kv"], diff_w_q=g["diff_w_q"][0], diff_w_out=g["diff_w_out"][0],
        diff_lambda=g["diff_lambda"][0], diff_subln_gain=g["diff_subln_gain"],
    )
    shared.update(const_tables())
    maps = []
    for b in cores:
        m = dict(shared)
        m["x"] = np.ascontiguousarray(g["x"][b])
        m["cT"] = fm(g["c"][b])
        maps.append(m)
    return maps


_NC_CACHE = {}


def kernel(**inputs):
    if "nc" not in _NC_CACHE:
        _NC_CACHE["nc"] = build()
    nc = _NC_CACHE["nc"]
    maps = make_in_maps(inputs, list(range(NCORES)))
    res = run_bass_kernel_spmd(nc, maps, core_ids=list(range(NCORES)))
    return np.stack([np.asarray(r["out"], np.float32) for r in res.results], axis=0)
```
